# Optimizing a Trainium2 kernel written in Bass

```python
import math
import jax, jax.numpy as jnp
from jax import lax
import numpy as np

D_MODEL = 1024
BATCH = 8
SEQ = 4096
DEPTH = 4

D_MIX = D_MODEL
N_HEADS = 8
N_KV_HEADS = 2
HEAD_DIM = 64
D_ATTN = N_HEADS * HEAD_DIM
D_KV = N_KV_HEADS * HEAD_DIM
D_SSM = D_MIX - D_ATTN
SSM_GROUP = 16
N_SSM_GROUPS = D_SSM // SSM_GROUP
SSM_STATE = 64
N_DIR = 2
WINDOW = 128
BLOCK = 128
N_BUCKETS = 32
MAX_DISTANCE = 128
D_FF = 2816
D_IN = D_ATTN + 2 * D_KV + D_SSM
EPS = 1e-6
NEG_INF = -1e30

kernel_name = "hymba_swa_s5_macaron_encoder"


def rms_norm(x, g):
    xf = x.astype(jnp.float32)
    y = xf * lax.rsqrt(jnp.mean(xf * xf, axis=-1, keepdims=True) + EPS)
    return (y * g.astype(jnp.float32)).astype(x.dtype)


def swiglu(x, w_gate, w_up, w_down):
    return (jax.nn.silu(x @ w_gate) * (x @ w_up)) @ w_down


def t5_bucket(rel):
    half = N_BUCKETS // 2
    max_exact = half // 2
    ret = jnp.where(rel > 0, half, 0)
    n = jnp.abs(rel)
    nf = jnp.maximum(n, 1).astype(jnp.float32)
    large = max_exact + (jnp.log(nf / max_exact) / math.log(MAX_DISTANCE / max_exact)
                         * (half - max_exact)).astype(jnp.int32)
    large = jnp.minimum(large, half - 1)
    return ret + jnp.where(n < max_exact, n, large)


def banded_attention(q, k, v, sink, bias):
    b, l = q.shape[0], q.shape[1]
    nb = l // BLOCK
    grp = N_HEADS // N_KV_HEADS
    qb = q.reshape(b, nb, BLOCK, N_KV_HEADS, grp, HEAD_DIM)

    def windows(t):
        tb = t.reshape(b, nb, BLOCK, N_KV_HEADS, HEAD_DIM)
        pad = jnp.pad(tb, ((0, 0), (1, 1), (0, 0), (0, 0), (0, 0)))
        return jnp.concatenate([pad[:, :-2], pad[:, 1:-1], pad[:, 2:]], axis=2)

    kw, vw = windows(k), windows(v)
    s = jnp.einsum('bnqkgd,bnskd->bnkgqs', qb, kw).astype(jnp.float32) * (HEAD_DIM ** -0.5)
    s = s + bias.astype(jnp.float32).reshape(N_KV_HEADS, grp, BLOCK, 3 * BLOCK)
    qpos = jnp.arange(BLOCK)[:, None]
    kpos = jnp.arange(3 * BLOCK)[None, :] - BLOCK
    in_band = jnp.abs(kpos - qpos) <= WINDOW
    kabs = jnp.arange(nb)[:, None] * BLOCK + kpos
    k_valid = (kabs >= 0) & (kabs < l)
    mask = in_band[None] & k_valid[:, None, :]
    s = jnp.where(mask[None, :, None, None], s, NEG_INF)
    sink_b = sink.astype(jnp.float32).reshape(N_KV_HEADS, grp, 1, 1)
    m = jnp.maximum(jnp.max(s, axis=-1, keepdims=True), sink_b)
    p = jnp.exp(s - m)
    denom = jnp.sum(p, axis=-1, keepdims=True) + jnp.exp(sink_b - m)
    p = (p / denom).astype(v.dtype)
    o = jnp.einsum('bnkgqs,bnskd->bnqkgd', p, vw)
    return o.reshape(b, l, D_ATTN)


def _scan_combine(e1, e2):
    a1r, a1i, b1r, b1i = e1
    a2r, a2i, b2r, b2i = e2
    return (a2r * a1r - a2i * a1i,
            a2r * a1i + a2i * a1r,
            a2r * b1r - a2i * b1i + b2r,
            a2r * b1i + a2i * b1r + b2i)


def s5_direction(u, a_re, a_im, log_dt, b_re, b_im, c_re, c_im, reverse):
    dt = jnp.exp(log_dt)[:, None]
    mag = jnp.exp(dt * a_re)
    ab_re = mag * jnp.cos(dt * a_im)
    ab_im = mag * jnp.sin(dt * a_im)
    den = a_re * a_re + a_im * a_im
    nr, ni = ab_re - 1.0, ab_im
    k_re = (nr * a_re + ni * a_im) / den
    k_im = (ni * a_re - nr * a_im) / den
    bb_re = k_re[..., None] * b_re - k_im[..., None] * b_im
    bb_im = k_re[..., None] * b_im + k_im[..., None] * b_re
    x_re = jnp.einsum('blgh,gph->blgp', u, bb_re)
    x_im = jnp.einsum('blgh,gph->blgp', u, bb_im)
    shape = x_re.shape
    elems = (jnp.broadcast_to(ab_re, shape), jnp.broadcast_to(ab_im, shape), x_re, x_im)
    _, _, s_re, s_im = lax.associative_scan(_scan_combine, elems, reverse=reverse, axis=1)
    return jnp.einsum('blgp,ghp->blgh', s_re, c_re) - jnp.einsum('blgp,ghp->blgh', s_im, c_im)


def s5_mixer(u, a_re, a_im, log_dt, b_re, b_im, c_re, c_im, d, w_glu, b_glu):
    bsz, l = u.shape[0], u.shape[1]
    f32 = jnp.float32
    uf = u.astype(f32)
    ug = uf.reshape(bsz, l, N_SSM_GROUPS, SSM_GROUP)
    y = jnp.zeros_like(ug)
    for di, rev in ((0, False), (1, True)):
        y = y + s5_direction(ug, a_re[di].astype(f32), a_im[di].astype(f32), log_dt[di].astype(f32),
                             b_re[di].astype(f32), b_im[di].astype(f32),
                             c_re[di].astype(f32), c_im[di].astype(f32), rev)
    y = y.reshape(bsz, l, D_SSM) + d.astype(f32) * uf
    z = jax.nn.gelu(y)
    out = z * jax.nn.sigmoid(z @ w_glu.astype(f32) + b_glu.astype(f32))
    return out.astype(u.dtype)


def setup_inputs(seed: int = 0) -> dict:
    key = jax.random.key(seed)
    ks = iter(jax.random.split(key, 40))
    f32 = jnp.float32

    def nrm(shape, scale):
        return jax.random.normal(next(ks), shape, f32) * scale

    def gain(shape):
        return 1.0 + 0.02 * jax.random.normal(next(ks), shape, f32)

    L, D, G, P, H = DEPTH, D_MODEL, N_SSM_GROUPS, SSM_STATE, SSM_GROUP
    a_re_init = -0.5 * jnp.ones((L, N_DIR, G, P), f32)
    a_im_init = jnp.broadcast_to(jnp.pi * jnp.arange(P, dtype=f32), (L, N_DIR, G, P))
    return {
        "x": nrm((BATCH, SEQ, D), 1.0),
        "rel_bias_table": nrm((N_BUCKETS, N_HEADS), 0.5),
        "ffn1_norm": gain((L, D)),
        "ffn1_w_gate": nrm((L, D, D_FF), D ** -0.5),
        "ffn1_w_up": nrm((L, D, D_FF), D ** -0.5),
        "ffn1_w_down": nrm((L, D_FF, D), D_FF ** -0.5),
        "mix_norm": gain((L, D)),
        "w_in": nrm((L, D, D_IN), D ** -0.5),
        "attn_sink": nrm((L, N_HEADS), 0.5),
        "ssm_a_re": a_re_init - 0.02 * jnp.abs(jax.random.normal(next(ks), (L, N_DIR, G, P), f32)),
        "ssm_a_im": a_im_init + 0.01 * jax.random.normal(next(ks), (L, N_DIR, G, P), f32),
        "ssm_log_dt": jax.random.uniform(next(ks), (L, N_DIR, G), f32, math.log(1e-3), math.log(1e-1)),
        "ssm_b_re": nrm((L, N_DIR, G, P, H), (2.0 * H) ** -0.5),
        "ssm_b_im": nrm((L, N_DIR, G, P, H), (2.0 * H) ** -0.5),
        "ssm_c_re": nrm((L, N_DIR, G, H, P), (2.0 * P) ** -0.5),
        "ssm_c_im": nrm((L, N_DIR, G, H, P), (2.0 * P) ** -0.5),
        "ssm_d": nrm((L, D_SSM), 1.0),
        "ssm_w_glu": nrm((L, D_SSM, D_SSM), D_SSM ** -0.5),
        "ssm_b_glu": nrm((L, D_SSM), 0.01),
        "attn_out_norm": gain((L, D_ATTN)),
        "ssm_out_norm": gain((L, D_SSM)),
        "w_out": nrm((L, D_MIX, D), D_MIX ** -0.5),
        "ffn2_norm": gain((L, D)),
        "ffn2_w_gate": nrm((L, D, D_FF), D ** -0.5),
        "ffn2_w_up": nrm((L, D, D_FF), D ** -0.5),
        "ffn2_w_down": nrm((L, D_FF, D), D_FF ** -0.5),
        "final_norm": gain((D,)),
    }


def reference(x, rel_bias_table, ffn1_norm, ffn1_w_gate, ffn1_w_up, ffn1_w_down, mix_norm, w_in,
              attn_sink, ssm_a_re, ssm_a_im, ssm_log_dt, ssm_b_re, ssm_b_im, ssm_c_re, ssm_c_im,
              ssm_d, ssm_w_glu, ssm_b_glu, attn_out_norm, ssm_out_norm, w_out, ffn2_norm,
              ffn2_w_gate, ffn2_w_up, ffn2_w_down, final_norm):
    bsz, l = x.shape[0], x.shape[1]
    rel = (jnp.arange(3 * BLOCK)[None, :] - BLOCK) - jnp.arange(BLOCK)[:, None]
    bias = jnp.transpose(rel_bias_table[t5_bucket(rel)], (2, 0, 1))
    split_pts = [D_ATTN, D_ATTN + D_KV, D_ATTN + 2 * D_KV]
    for i in range(DEPTH):
        x = x + 0.5 * swiglu(rms_norm(x, ffn1_norm[i]), ffn1_w_gate[i], ffn1_w_up[i], ffn1_w_down[i])
        h = rms_norm(x, mix_norm[i])
        proj = h @ w_in[i]
        q, k, v, u = jnp.split(proj, split_pts, axis=-1)
        attn = banded_attention(q.reshape(bsz, l, N_HEADS, HEAD_DIM),
                                k.reshape(bsz, l, N_KV_HEADS, HEAD_DIM),
                                v.reshape(bsz, l, N_KV_HEADS, HEAD_DIM),
                                attn_sink[i], bias)
        ssm = s5_mixer(u, ssm_a_re[i], ssm_a_im[i], ssm_log_dt[i], ssm_b_re[i], ssm_b_im[i],
                       ssm_c_re[i], ssm_c_im[i], ssm_d[i], ssm_w_glu[i], ssm_b_glu[i])
        mixed = jnp.concatenate([rms_norm(attn, attn_out_norm[i]), rms_norm(ssm, ssm_out_norm[i])], axis=-1)
        x = x + mixed @ w_out[i]
        x = x + 0.5 * swiglu(rms_norm(x, ffn2_norm[i]), ffn2_w_gate[i], ffn2_w_up[i], ffn2_w_down[i])
    return rms_norm(x, final_norm)
```

```python
import math
import numpy as np
import concourse.bass as bass
import concourse.mybir as mybir
from concourse.bass_utils import run_bass_kernel_spmd

F32 = mybir.dt.float32
BF16 = mybir.dt.bfloat16
I32 = mybir.dt.int32
AF = mybir.ActivationFunctionType
ALU = mybir.AluOpType
AX = mybir.AxisListType

D = 1024
SEQ = 4096
DEPTH = 4
DFF = 2816
NFT = DFF // 128
DIN = 1280
NH = 8
HD = 64
G = 32
P = 64
HS = 16
T1 = 32
NCH = SEQ // T1
EPS = 1e-6
NEG = -1e30
N_BUCKETS = 32
MAX_DISTANCE = 128
TT = 2048
SSM_STOP = 99


class StopSSM(Exception):
    pass
NPASS = SEQ // TT


class Buf:
    __slots__ = ("name", "w", "r")

    def __init__(self, name):
        self.name = name
        self.w = None
        self.r = {}


class Stream:
    __slots__ = ("sem", "total", "name")

    def __init__(self, sem, name):
        self.sem = sem
        self.total = 0
        self.name = name


class K:
    def __init__(self, nc, sems, dsems, active=None, handle=None):
        self.nc = nc
        self.active = active
        self.eng = {"pe": nc.tensor, "act": nc.scalar, "dve": nc.vector, "pool": nc.gpsimd, "sp": nc.sync}
        if handle is not None:
            self.eng[active] = handle
        self.sem = sems
        self.cnt = {e: 0 for e in self.eng}
        self.waited = {e: {} for e in self.eng}
        self.dsems = list(dsems)
        self.streams = []
        self.ninstr = 0
        self.prog = {e: [] for e in self.eng}

    def stream(self, name):
        s = Stream(self.dsems.pop(), name)
        self.streams.append(s)
        return s

    def _wait(self, e, dep):
        if dep is None:
            return
        w = self.waited[e]
        if dep[0] == "e":
            _, e2, c = dep
            if e2 == e and e == "pe":
                return
            if w.get(e2, 0) >= c:
                return
            sem = self.sem[e2]
            if e == self.active:
                self.eng[e].wait_ge(sem, c)
            w[e2] = c
        else:
            s = dep[1]
            key = ("d", id(s))
            if w.get(key, 0) >= s.total:
                return
            tot = s.total
            if e == self.active:
                self.eng[e].wait_ge(s.sem, tot)
            w[key] = s.total

    def _deps(self, e, reads, writes):
        for b in reads:
            self._wait(e, b.w)
        for b in writes:
            self._wait(e, b.w)
            for dep in b.r.values():
                self._wait(e, dep)

    def op(self, e, fn, reads=(), writes=()):
        self._deps(e, reads, writes)
        sem = self.sem[e]
        if e == self.active:
            fn(self.eng[e]).then_inc(sem, 1)
        self.cnt[e] += 1
        dep = ("e", e, self.cnt[e])
        for b in reads:
            b.r[e] = dep
        for b in writes:
            b.w = dep
            b.r = {}
        self.ninstr += 1

    def dma(self, q, stream, out, in_, reads=(), writes=(), **kw):
        self._deps(q, reads, writes)
        if q == self.active:
            self.eng[q].dma_start(out=out, in_=in_, **kw).then_inc(stream.sem, 16)
        stream.total += 16
        dep = ("d", stream)
        for b in reads:
            b.r[("d", id(stream))] = dep
        for b in writes:
            b.w = dep
            b.r = {}
        self.ninstr += 1

    def barrier(self):
        for e in self.eng:
            for e2 in self.eng:
                if e2 != e and self.cnt[e2] > 0:
                    self._wait(e, ("e", e2, self.cnt[e2]))
            for s in self.streams:
                if s.total > 0:
                    self._wait(e, ("d", s))

    def wait_stream(self, e, s):
        if e == self.active:
            self.eng[e].wait_ge(s.sem, s.total)


def t5_bucket_np(rel):
    half = N_BUCKETS // 2
    max_exact = half // 2
    ret = np.where(rel > 0, half, 0)
    n = np.abs(rel)
    nf = np.maximum(n, 1).astype(np.float32)
    large = max_exact + (np.log(nf / max_exact) / math.log(MAX_DISTANCE / max_exact)
                         * (half - max_exact)).astype(np.int32)
    large = np.minimum(large, half - 1)
    return ret + np.where(n < max_exact, n, large)


def act(out, in_, func, **kw):
    return lambda e: e.activation(out=out, in_=in_, func=func, **kw)


def build(depth=DEPTH, do_mixer=True, do_ssm=True, dbg=None, do_ffn=True, do_attn=True):
    nc = bass.Bass("TRN2", target_bir_lowering=False)
    dt = nc.dram_tensor
    x_in = dt("x", [SEQ, D], F32, kind="ExternalInput").ap()
    biasm_d = dt("biasm", [128, NH, 384], F32, kind="ExternalInput").ap()
    w = {}
    for nm, shp in [("ffn1_norm", [DEPTH, D]), ("ffn1_w_gate", [DEPTH, D, DFF]), ("ffn1_w_up", [DEPTH, D, DFF]),
                    ("ffn1_w_down", [DEPTH, DFF, D]), ("mix_norm", [DEPTH, D]), ("w_in", [DEPTH, D, DIN]),
                    ("attn_sink", [DEPTH, NH]), ("ssm_a_re", [DEPTH, 2, G, P]), ("ssm_a_im", [DEPTH, 2, G, P]),
                    ("ssm_log_dt", [DEPTH, 2, G]), ("ssm_b_re", [DEPTH, 2, G, P, HS]),
                    ("ssm_b_im", [DEPTH, 2, G, P, HS]), ("ssm_c_re", [DEPTH, 2, G, HS, P]),
                    ("ssm_c_im", [DEPTH, 2, G, HS, P]), ("ssm_d", [DEPTH, 512]), ("ssm_w_glu", [DEPTH, 512, 512]),
                    ("ssm_b_glu", [DEPTH, 512]), ("attn_out_norm", [DEPTH, 512]), ("ssm_out_norm", [DEPTH, 512]),
                    ("w_out", [DEPTH, D, D]), ("ffn2_norm", [DEPTH, D]), ("ffn2_w_gate", [DEPTH, D, DFF]),
                    ("ffn2_w_up", [DEPTH, D, DFF]), ("ffn2_w_down", [DEPTH, DFF, D]), ("final_norm", [D])]:
        w[nm] = dt(nm, shp, F32, kind="ExternalInput").ap()
    y_out = dt("y", [SEQ, D], F32, kind="ExternalOutput").ap()
    xres = dt("xres", [8, 128, SEQ], F32, kind="Internal").ap()
    dbg_out = None
    if dbg is not None:
        dbg_out = dt("dbg", list(dbg), F32, kind="ExternalOutput").ap()

    from contextlib import ExitStack
    es = ExitStack()
    with es:
        def sb(name, shape, dtype):
            return es.enter_context(nc.sbuf_tensor(name, shape, dtype))

        def ps(name, shape, dtype):
            return es.enter_context(nc.psum_tensor(name, shape, dtype))

        sems = {e: es.enter_context(nc.semaphore("s_" + e)) for e in ["pe", "act", "dve", "pool", "sp"]}
        dsems = [es.enter_context(nc.semaphore("d%d" % i)) for i in range(40)]

        bigA = sb("bigA", [128, NFT * TT], BF16)
        bigB = sb("bigB", [128, 8 * TT], BF16)
        xn = [sb("xn%d" % i, [128, 8, 512], F32) for i in range(2)]
        sq = [sb("sq%d" % i, [128, 512], BF16) for i in range(2)]
        rs = [sb("rs%d" % i, [128, 512], F32) for i in range(2)]
        scr = sb("scr", [128, 7680], BF16)
        sg = [scr[:, 5632 + i * 1024:5632 + (i + 1) * 1024].bitcast(F32) for i in range(2)]
        wgu = [sb("wgu%d" % i, [128, 8, 128], BF16) for i in range(6)]
        wd = [scr[:, i * 2816:(i + 1) * 2816].rearrange("p (ft n) -> p ft n", n=128) for i in range(2)]
        xr = [sb("xr%d" % i, [128, 512], F32) for i in range(4)]
        gains = sb("gains", [128, DEPTH * 3 + 1, 8], F32)
        ones_bf = sb("ones_bf", [128, 128], BF16)
        ident_f = sb("ident_f", [128, 128], F32)
        ident_b = sb("ident_b", [128, 128], BF16)
        xtok = [bigB[:, i * 2048:(i + 1) * 2048].bitcast(F32) for i in range(2)]
        eps_t = sb("eps_t", [128, 1], F32)
        biasm = xn[1][:, :, 0:384]
        sinkb = sb("sinkb", [128, DEPTH * NH], F32)
        gains2 = sb("gains2", [128, DEPTH, 2, 4], F32)
        sc = [scr[:, i * 768:(i + 1) * 768].bitcast(F32) for i in range(2)]
        pexp = [scr[:, 1536 + i * 384:1536 + (i + 1) * 384] for i in range(2)]
        pT = [scr[:, 2304 + i * 384:2304 + (i + 1) * 384] for i in range(2)]
        atok = [scr[:, 3072 + i * 1024:3072 + (i + 1) * 1024].bitcast(F32) for i in range(2)]
        stat = [sb("stat%d" % i, [128, 4], F32) for i in range(2)]
        rowsum = [sb("rowsum%d" % i, [128, 2, NH], F32) for i in range(2)]
        nstat = [sb("nstat%d" % i, [128, 4], F32) for i in range(2)]
        kall = sb("kall", [128, 99], F32)
        kall1 = sb("kall1", [128, 99], F32)
        kalli = sb("kalli", [128, 99], I32)
        cst = sb("cst", [128, 2, 128], F32)
        rmt = sb("rmt", [128, 24], F32)
        rmi = sb("rmi", [128, 24], I32)
        bglu = sb("bglu", [128, DEPTH, 4], F32)
        ssm_scr = sb("ssm_scr", [128, 2048], BF16)
        ptmp = sb("ptmp", [128, 2, 528], F32)
        sctmp = sb("sctmp", [128, 4, 64], F32)
        hm = sb("hm", [128, 2], F32)

        pb = [ps("pb%d" % i, [128, 512], F32) for i in range(8)]

        def gen(k):
            B = {}

            def buf(name):
                if name not in B:
                    B[name] = Buf(name)
                return B[name]

            st_x = [k.stream("xn%d" % i) for i in range(2)]
            st_w = [k.stream("wgu%d" % i) for i in range(6)]
            st_wd = [k.stream("wd%d" % i) for i in range(2)]
            st_xr_in = [k.stream("xri%d" % i) for i in range(4)]
            st_xr_out = [k.stream("xro%d" % i) for i in range(4)]
            st_misc = k.stream("misc")
            st_tok = [k.stream("tok%d" % i) for i in range(2)]
            st_out = [k.stream("out%d" % i) for i in range(2)]

            if True:
                k.op("pool", lambda e: e.memset(ones_bf[:], 1.0), writes=[buf("ones")])
                k.op("pool", lambda e: e.memset(eps_t[:], EPS), writes=[buf("eps")])
                k.op("pool", lambda e: e.memset(ident_f[:], 0.0), writes=[buf("identf")])
                k.op("pool", lambda e: e.affine_select(out=ident_f[:], in_=ident_f[:], pattern=[[1, 128]], base=0,
                                                       channel_multiplier=-1, compare_op=ALU.not_equal, fill=1.0),
                     reads=[buf("identf")], writes=[buf("identf")])
                k.op("dve", lambda e: e.tensor_copy(out=ident_b[:], in_=ident_f[:]), reads=[buf("identf")],
                     writes=[buf("identb")])
                gidx = {}
                gi = 0
                with nc.allow_non_contiguous_dma(reason="small param loads"):
                    for l in range(DEPTH):
                        for nm in ["ffn1_norm", "mix_norm", "ffn2_norm"]:
                            gidx[(nm, l)] = gi
                            k.dma("sp", st_misc, gains[:, gi, :], w[nm][l].rearrange("(kt p) -> p kt", p=128),
                                  writes=[buf("gains")], allow_slow_non_contiguous=True)
                            gi += 1
                    gidx["final"] = gi
                    k.dma("sp", st_misc, gains[:, gi, :], w["final_norm"].rearrange("(kt p) -> p kt", p=128),
                          writes=[buf("gains")], allow_slow_non_contiguous=True)

                k.dma("sp", st_misc, sinkb[:], w["attn_sink"].rearrange("l h -> (l h)").partition_broadcast(128), writes=[buf("sinkb")],
                      allow_slow_non_contiguous=True)
                for l in range(DEPTH):
                    for i2, nm in enumerate(["attn_out_norm", "ssm_out_norm"]):
                        k.dma("sp", st_misc, gains2[:, l, i2, :], w[nm][l].rearrange("(kt p) -> p kt", p=128), writes=[buf("gains2")],
                              allow_slow_non_contiguous=True)
                XB = [[buf("xres_%d_%d" % (kt, nb)) for nb in range(8)] for kt in range(8)]

                for g4 in range(SEQ // 512):
                    stg = xn[g4 % 2]
                    stgb = buf("xn%d" % (g4 % 2))
                    for t4 in range(4):
                        tt = g4 * 4 + t4
                        xt = xtok[tt % 2]
                        xtb = buf("xtok%d" % (tt % 2))
                        k.dma("sp", st_tok[tt % 2], xt[:], x_in[tt * 128:(tt + 1) * 128, :], writes=[xtb])
                        for kt in range(8):
                            bank = pb[kt % 4]
                            bb = buf("pb%d" % (kt % 4))
                            k.op("pe", lambda e, bank=bank, xt=xt, kt=kt: e.transpose(bank[:, 0:128], xt[:, kt * 128:(kt + 1) * 128], ident_f[:]),
                                 reads=[xtb, buf("identf")], writes=[bb])
                            eng = "act" if kt % 2 == 0 else "dve"
                            if eng == "act":
                                k.op("act", act(stg[:, kt, t4 * 128:(t4 + 1) * 128], bank[:, 0:128], AF.Copy),
                                     reads=[bb], writes=[stgb])
                            else:
                                k.op("dve", lambda e, bank=bank, stg=stg, kt=kt, t4=t4: e.tensor_copy(out=stg[:, kt, t4 * 128:(t4 + 1) * 128], in_=bank[:, 0:128]),
                                     reads=[bb], writes=[stgb])
                    k.dma("sp", st_x[g4 % 2], xres[:, :, g4 * 512:(g4 + 1) * 512].rearrange("kt p n -> p kt n"), stg[:],
                          reads=[stgb], writes=[XB[kt][g4] for kt in range(8)])

                hT = bigB[:].rearrange("p (kt n) -> p kt n", kt=8)

                def norm_block(gi_, blk, hcol, hb):
                    s = blk % 2
                    xb = buf("xn%d" % s)
                    k.dma("sp", st_x[s], xn[s][:], xres[:, :, blk * 512:(blk + 1) * 512].rearrange("kt p n -> p kt n"),
                          reads=[XB[kt][blk] for kt in range(8)], writes=[xb])
                    pbn = pb[6]
                    pbb = buf("pb6")
                    for kt in range(8):
                        q2 = kt % 2
                        sqb = buf("sq%d" % q2)
                        k.op("act", act(sq[q2][:], xn[s][:, kt, :], AF.Square), reads=[xb], writes=[sqb])
                        k.op("pe", lambda e, kt=kt, q2=q2: e.matmul(pbn[:], ones_bf[:], sq[q2][:], start=(kt == 0), stop=(kt == 7)),
                             reads=[sqb, buf("ones")], writes=[pbb])
                    rb = buf("rs%d" % s)
                    k.op("act", act(rs[s][:], pbn[:], AF.Sqrt, scale=1.0 / D, bias=eps_t[:, 0:1]), reads=[pbb, buf("eps")], writes=[rb])
                    k.op("dve", lambda e: e.reciprocal(out=rs[s][:], in_=rs[s][:]), reads=[rb], writes=[rb])
                    for kt in range(8):
                        k.op("dve", lambda e, kt=kt: e.scalar_tensor_tensor(out=hT[:, kt, hcol:hcol + 512], in0=xn[s][:, kt, :],
                                                                           scalar=gains[:, gi_, kt:kt + 1], in1=rs[s][:],
                                                                           op0=ALU.mult, op1=ALU.mult),
                             reads=[xb, rb, buf("gains")], writes=[hb])

                hid = bigA[:].rearrange("p (ft n) -> p ft n", ft=NFT)
                wctr = [0]
                wdctr = [0]
                xrctr = [0]

                def load_w(src_ap):
                    s = wctr[0] % 6
                    wctr[0] += 1
                    wb = buf("wgu%d" % s)
                    with nc.allow_non_contiguous_dma(reason="512B weight rows"):
                        k.dma("pool", st_w[s], wgu[s][:], src_ap.rearrange("(kt p) f -> p kt f", p=128), writes=[wb])
                    return wgu[s], wb

                def resid_update(nt, blk, psum_bank, pbb, scale):
                    s = xrctr[0] % 4
                    xrctr[0] += 1
                    xb = buf("xr%d" % s)
                    k.dma("sp", st_xr_in[s], xr[s][:], xres[nt, :, blk * 512:(blk + 1) * 512], reads=[XB[nt][blk]], writes=[xb])
                    k.op("dve", lambda e: e.scalar_tensor_tensor(out=xr[s][:], in0=psum_bank[:], scalar=float(scale), in1=xr[s][:],
                                                                 op0=ALU.mult, op1=ALU.add),
                         reads=[pbb, xb], writes=[xb])
                    k.dma("act", st_xr_out[s], xres[nt, :, blk * 512:(blk + 1) * 512], xr[s][:], reads=[xb], writes=[XB[nt][blk]])

                pre_done = set()

                def do_norm(gi_, blk, hcol, hb):
                    if (gi_, blk) in pre_done:
                        pre_done.discard((gi_, blk))
                        return
                    norm_block(gi_, blk, hcol, hb)

                def ffn(l, pre, next_gi=None):
                    wg_d, wu_d, wd_d = w[pre + "_w_gate"][l], w[pre + "_w_up"][l], w[pre + "_w_down"][l]
                    gi_ = gidx[(pre + "_norm", l)]
                    for tp in range(NPASS):
                        hbs = [buf("hT%d" % nb) for nb in range(4)]
                        for nb in range(4):
                            do_norm(gi_, tp * 4 + nb, nb * 512, hbs[nb])
                        for ft in range(NFT):
                            wg_t, wg_b = load_w(wg_d[:, ft * 128:(ft + 1) * 128])
                            wu_t, wu_b = load_w(wu_d[:, ft * 128:(ft + 1) * 128])
                            hidb = buf("hid%d" % ft)
                            for nb in range(4):
                                pg, pgb = pb[nb % 2], buf("pb%d" % (nb % 2))
                                pu, pub = pb[2 + nb % 2], buf("pb%d" % (2 + nb % 2))
                                for kt in range(8):
                                    k.op("pe", lambda e, kt=kt, pg=pg, wg_t=wg_t, nb=nb: e.matmul(pg[:], wg_t[:, kt, :], hT[:, kt, nb * 512:(nb + 1) * 512],
                                                                                                   start=(kt == 0), stop=(kt == 7)),
                                         reads=[wg_b, hbs[nb]], writes=[pgb])
                                for kt in range(8):
                                    k.op("pe", lambda e, kt=kt, pu=pu, wu_t=wu_t, nb=nb: e.matmul(pu[:], wu_t[:, kt, :], hT[:, kt, nb * 512:(nb + 1) * 512],
                                                                                                   start=(kt == 0), stop=(kt == 7)),
                                         reads=[wu_b, hbs[nb]], writes=[pub])
                                sgs = sg[nb % 2]
                                sgb = buf("sg%d" % (nb % 2))
                                k.op("act", act(sgs[:], pg[:], AF.Silu), reads=[pgb], writes=[sgb])
                                k.op("dve", lambda e, sgs=sgs, pu=pu, ft=ft, nb=nb: e.tensor_tensor(out=hid[:, ft, nb * 512:(nb + 1) * 512], in0=sgs[:], in1=pu[:], op=ALU.mult),
                                     reads=[sgb, pub], writes=[hidb])
                        hall = [buf("hid%d" % ft) for ft in range(NFT)]
                        for nt in range(8):
                            s = wdctr[0] % 2
                            wdctr[0] += 1
                            wdb = buf("wd%d" % s)
                            with nc.allow_non_contiguous_dma(reason="512B weight rows"):
                                k.dma("pool", st_wd[s], wd[s][:], wd_d[:, nt * 128:(nt + 1) * 128].rearrange("(ft p) n -> p ft n", p=128),
                                      writes=[wdb])
                            for nb in range(4):
                                pd, pdb = pb[4 + nb % 2], buf("pb%d" % (4 + nb % 2))
                                for ft in range(NFT):
                                    k.op("pe", lambda e, ft=ft, pd=pd, s=s, nb=nb: e.matmul(pd[:], wd[s][:, ft, :], hid[:, ft, nb * 512:(nb + 1) * 512],
                                                                                             start=(ft == 0), stop=(ft == NFT - 1)),
                                         reads=[wdb] + (hall if ft == 0 else []), writes=[pdb])
                                resid_update(nt, tp * 4 + nb, pd, pdb, 0.5)
                            if nt >= 4:
                                nbp = nt - 4
                                if tp + 1 < NPASS:
                                    norm_block(gi_, (tp + 1) * 4 + nbp, nbp * 512, hbs[nbp])
                                    pre_done.add((gi_, (tp + 1) * 4 + nbp))
                                elif next_gi is not None:
                                    norm_block(next_gi, nbp, nbp * 512, hbs[nbp])
                                    pre_done.add((next_gi, nbp))


                qT = bigA[:, 0:4 * SEQ].rearrange("p (j n) -> p j n", j=4)
                KK = bigA[:, 4 * SEQ:6 * SEQ].rearrange("p (j n) -> p j n", j=2)
                vtok = bigA[:, 6 * SEQ:7 * SEQ].rearrange("p (t c) -> p t c", c=128)
                uT = bigA[:, 7 * SEQ:11 * SEQ].rearrange("p (j n) -> p j n", j=4)
                mixedT = bigB[:].rearrange("p (j n) -> p j n", j=4)

                def mixer(l):
                    k.barrier()
                    gi_ = gidx[("mix_norm", l)]
                    win = w["w_in"][l]
                    qb, kb_, vb, ub = buf("qT"), buf("KK"), buf("vtok"), buf("uT")
                    for tp in range(NPASS):
                        hbs = [buf("hT%d" % nb) for nb in range(4)]
                        for nb in range(4):
                            do_norm(gi_, tp * 4 + nb, nb * 512, hbs[nb])
                        c0 = tp * TT
                        evi = 0
                        for kind, j, col in ([("q", j, 128 * j) for j in range(4)] + [("u", j, 768 + 128 * j) for j in range(4)]):
                            wt, wb = load_w(win[:, col:col + 128])
                            dst, db = (qT, qb) if kind == "q" else (uT, ub)
                            for nb in range(4):
                                bank, bb = pb[nb % 2], buf("pb%d" % (nb % 2))
                                for kt in range(8):
                                    k.op("pe", lambda e: e.matmul(bank[:], wt[:, kt, :], hT[:, kt, nb * 512:(nb + 1) * 512], start=(kt == 0), stop=(kt == 7)),
                                         reads=[wb, hbs[nb]], writes=[bb])
                                o_ap = dst[:, j, c0 + nb * 512:c0 + (nb + 1) * 512]
                                if evi % 2 == 0:
                                    k.op("act", act(o_ap, bank[:], AF.Copy), reads=[bb], writes=[db])
                                else:
                                    k.op("dve", lambda e: e.tensor_copy(out=o_ap, in_=bank[:]), reads=[bb], writes=[db])
                                evi += 1
                        for kv in range(2):
                            s_ = wctr[0] % 6
                            wctr[0] += 1
                            wb = buf("wgu%d" % s_)
                            wt = wgu[s_]
                            for half in range(2):
                                k.dma("pool", st_w[s_], wt[:, :, half * 64:(half + 1) * 64],
                                      win[:, 512 + kv * 64:512 + (kv + 1) * 64].rearrange("(kt p) f -> p kt f", p=128), writes=[wb])
                            for nb in range(4):
                                bank, bb = pb[nb % 2], buf("pb%d" % (nb % 2))
                                for kt in range(8):
                                    k.op("pe", lambda e: e.matmul(bank[:], wt[:, kt, :], hT[:, kt, nb * 512:(nb + 1) * 512], start=(kt == 0), stop=(kt == 7)),
                                         reads=[wb, hbs[nb]], writes=[bb])
                                o_ap = KK[:, kv, c0 + nb * 512:c0 + (nb + 1) * 512]
                                k.op("act", act(o_ap, bank[:], AF.Copy), reads=[bb], writes=[kb_])
                        wt, wb = load_w(win[:, 640:768])
                        for t4 in range(TT // 512):
                            bank, bb = pb[2 + t4 % 2], buf("pb%d" % (2 + t4 % 2))
                            for ti in range(4):
                                tcol = t4 * 512 + ti * 128
                                for kt in range(8):
                                    k.op("pe", lambda e: e.matmul(bank[:, ti * 128:(ti + 1) * 128], hT[:, kt, tcol:tcol + 128], wt[:, kt, :], start=(kt == 0), stop=(kt == 7)),
                                         reads=[wb, hbs[t4]], writes=[bb])
                            tt0 = (c0 + t4 * 512) // 128
                            k.op("dve", lambda e: e.tensor_copy(out=vtok[:, tt0:tt0 + 4, :], in_=bank[:].rearrange("p (t c) -> p t c", c=128)),
                                 reads=[bb], writes=[vb])

                    k.barrier()
                    if not do_attn:
                        return
                    k.dma("sp", st_misc, biasm, biasm_d, writes=[buf("biasm")])
                    mb = buf("mixedT")
                    NBLK = SEQ // 128

                    def geom(n):
                        kb0 = max(n - 1, 0)
                        kb1 = min(n + 1, NBLK - 1)
                        nk = (kb1 - kb0 + 1) * 128
                        bc0 = (kb0 - (n - 1)) * 128
                        return kb0, nk, bc0

                    def st_scores(i):
                        n, h = divmod(i, NH)
                        kb0, nk, bc0 = geom(n)
                        hp = (h % 2) * 64
                        rsum, rsb = rowsum[n % 2], buf("rowsum%d" % (n % 2))
                        sbank, sbb = pb[h % 2], buf("pb%d" % (h % 2))
                        k.op("pe", lambda e: e.matmul(sbank[:, 0:nk], qT[hp:hp + 64, h // 2, n * 128:(n + 1) * 128],
                                                      KK[hp:hp + 64, h // 4, kb0 * 128:kb0 * 128 + nk], start=True, stop=True),
                             reads=[qb, kb_], writes=[sbb])
                        scs, scb = sc[h % 2], buf("sc%d" % (h % 2))
                        k.op("dve", lambda e: e.scalar_tensor_tensor(out=scs[:, 0:nk], in0=sbank[:, 0:nk], scalar=0.125,
                                                                     in1=biasm[:, h, bc0:bc0 + nk], op0=ALU.mult, op1=ALU.add),
                             reads=[sbb, buf("biasm")], writes=[scb])
                        sts, stb = stat[h % 2], buf("stat%d" % (h % 2))
                        k.op("dve", lambda e: e.tensor_reduce(out=sts[:, 0:1], in_=scs[:, 0:nk], op=ALU.max, axis=AX.X),
                             reads=[scb], writes=[stb])
                        k.op("dve", lambda e: e.tensor_scalar(out=sts[:, 1:2], in0=sts[:, 0:1], scalar1=sinkb[:, l * NH + h:l * NH + h + 1],
                                                              scalar2=-1.0, op0=ALU.max, op1=ALU.mult),
                             reads=[stb, buf("sinkb")], writes=[stb])
                        pes, peb = pexp[h % 2], buf("pexp%d" % (h % 2))
                        k.op("act", lambda e: e.activation(out=pes[:, 0:nk], in_=scs[:, 0:nk], func=AF.Exp, bias=sts[:, 1:2], scale=1.0,
                                                           accum_out=rsum[:, 0, h:h + 1]),
                             reads=[scb, stb], writes=[peb, rsb])
                        k.op("act", lambda e: e.activation(out=rsum[:, 1, h:h + 1], in_=sinkb[:, l * NH + h:l * NH + h + 1], func=AF.Exp,
                                                           bias=sts[:, 1:2], scale=1.0),
                             reads=[stb, buf("sinkb")], writes=[rsb])

                    def st_transpose(i):
                        n, h = divmod(i, NH)
                        kb0, nk, bc0 = geom(n)
                        pes, peb = pexp[h % 2], buf("pexp%d" % (h % 2))
                        tbank, tbb = pb[2 + h % 2], buf("pb%d" % (2 + h % 2))
                        tview = tbank[:].bitcast(BF16)
                        for kb in range(nk // 128):
                            k.op("pe", lambda e: e.transpose(tview[:, kb * 128:(kb + 1) * 128], pes[:, kb * 128:(kb + 1) * 128], ident_b[:]),
                                 reads=[peb, buf("identb")], writes=[tbb])
                        pts, ptb = pT[h % 2], buf("pT%d" % (h % 2))
                        if h % 2 == 0:
                            k.op("act", act(pts[:, 0:nk], tview[:, 0:nk], AF.Copy), reads=[tbb], writes=[ptb])
                        else:
                            k.op("dve", lambda e: e.tensor_copy(out=pts[:, 0:nk], in_=tview[:, 0:nk]), reads=[tbb], writes=[ptb])

                    def st_pv(i):
                        n, h = divmod(i, NH)
                        kb0, nk, bc0 = geom(n)
                        obank, obb = pb[4 + n % 2], buf("pb%d" % (4 + n % 2))
                        rsum, rsb = rowsum[n % 2], buf("rowsum%d" % (n % 2))
                        pts, ptb = pT[h % 2], buf("pT%d" % (h % 2))
                        kvh = h // 4
                        for kb in range(nk // 128):
                            k.op("pe", lambda e: e.matmul(obank[:, h * 64:(h + 1) * 64], pts[:, kb * 128:(kb + 1) * 128],
                                                          vtok[:, kb0 + kb, kvh * 64:(kvh + 1) * 64], start=(kb == 0), stop=(kb == nk // 128 - 1)),
                                 reads=[ptb, vb], writes=[obb])
                        if h != NH - 1:
                            return
                        k.op("dve", lambda e: e.tensor_tensor(out=rsum[:, 0, :], in0=rsum[:, 0, :], in1=rsum[:, 1, :], op=ALU.add), reads=[rsb], writes=[rsb])
                        k.op("dve", lambda e: e.reciprocal(out=rsum[:, 0, :], in_=rsum[:, 0, :]), reads=[rsb], writes=[rsb])
                        at, atb = atok[n % 2], buf("atok%d" % (n % 2))
                        k.op("dve", lambda e: e.tensor_tensor(out=at[:].rearrange("p (h d) -> p h d", d=64), in0=obank[:].rearrange("p (h d) -> p h d", d=64),
                                                              in1=rsum[:, 0, :].unsqueeze(2).broadcast_to([128, NH, 64]), op=ALU.mult),
                             reads=[obb, rsb], writes=[atb])
                        ns, nsb = nstat[n % 2], buf("nstat%d" % (n % 2))
                        junk, jb = sq[0], buf("sq0")
                        k.op("act", lambda e: e.activation(out=junk[:, 0:512], in_=at[:, 0:512], func=AF.Square, accum_out=ns[:, 2:3]),
                             reads=[atb], writes=[jb, nsb])
                        k.op("act", act(ns[:, 3:4], ns[:, 2:3], AF.Sqrt, scale=1.0 / 512, bias=eps_t[:, 0:1]), reads=[nsb, buf("eps")], writes=[nsb])
                        k.op("dve", lambda e: e.reciprocal(out=ns[:, 3:4], in_=ns[:, 3:4]), reads=[nsb], writes=[nsb])
                        k.op("dve", lambda e: e.tensor_scalar(out=at[:], in0=at[:], scalar1=ns[:, 3:4], scalar2=None, op0=ALU.mult), reads=[atb, nsb], writes=[atb])
                        trb, trbb = pb[6 + n % 2], buf("pb%d" % (6 + n % 2))
                        for c in range(4):
                            k.op("pe", lambda e: e.transpose(trb[:, c * 128:(c + 1) * 128], at[:, c * 128:(c + 1) * 128], ident_f[:]),
                                 reads=[atb, buf("identf")], writes=[trbb])
                        for c in range(4):
                            k.op("dve", lambda e: e.tensor_scalar(out=mixedT[:, c, n * 128:(n + 1) * 128], in0=trb[:, c * 128:(c + 1) * 128],
                                                                  scalar1=gains2[:, l, 0, c:c + 1], scalar2=None, op0=ALU.mult),
                                 reads=[trbb, buf("gains2")], writes=[mb])

                    NI = NBLK * NH
                    for i in range(NI + 2):
                        if i < NI:
                            st_scores(i)
                        if 0 <= i - 1 < NI:
                            st_transpose(i - 1)
                        if 0 <= i - 2 < NI:
                            st_pv(i - 2)
                    out_proj(l, 0, mb)
                    k.barrier()

                def out_proj(l, half, mb):
                    wo = w["w_out"][l]
                    for nt in range(8):
                        s_ = wctr[0] % 6
                        wctr[0] += 1
                        wb = buf("wgu%d" % s_)
                        wt = wgu[s_]
                        k.dma("pool", st_w[s_], wt[:, 0:4, :], wo[half * 512:(half + 1) * 512, nt * 128:(nt + 1) * 128].rearrange("(kt p) f -> p kt f", p=128),
                              writes=[wb])
                        for blk in range(8):
                            bank, bb = pb[blk % 2], buf("pb%d" % (blk % 2))
                            for kt in range(4):
                                k.op("pe", lambda e: e.matmul(bank[:], wt[:, kt, :], mixedT[:, kt, blk * 512:(blk + 1) * 512], start=(kt == 0), stop=(kt == 3)),
                                     reads=[wb, mb], writes=[bb])
                            resid_update(nt, blk, bank, bb, 1.0)

                PA = xn[0][:].rearrange("p a b -> p (a b)")
                PBt = xn[1][:].rearrange("p a b -> p (a b)")

                def pa(off, n):
                    return PA[:, off:off + n]
                are, aim, dtv, rho, tht, den, arn, ain = [pa(32 * i, 32) for i in range(8)]
                L32r, L32i, L31r, L31i, L31in = [pa(256 + 32 * i, 32) for i in range(5)]
                Bre = pa(448, 512).rearrange("p (g h) -> p g h", h=16)
                Bim = pa(960, 512).rearrange("p (g h) -> p g h", h=16)
                Cre = pa(1472, 512).rearrange("p (g h) -> p g h", h=16)
                Cim = pa(1984, 512).rearrange("p (g h) -> p g h", h=16)
                Dv = pa(2496, 32)
                tmpA = pa(2528, 256)
                ldt = pa(2784, 32)
                XS = bigA[:, 0:4 * SEQ].bitcast(F32).rearrange("p (c g t) -> p c g t", g=G, t=2)
                SelM = bigA[:, 4 * SEQ:6 * SEQ].rearrange("p (a b m) -> p a b m", a=8, b=8)
                Zt = bigA[:, 6 * SEQ:7 * SEQ].rearrange("p (g m c) -> p g m c", g=8, m=4)
                WT = [[bigB[:, sl * 4096 + d_ * 2048:sl * 4096 + (d_ + 1) * 2048].rearrange("p (k j) -> p k j", k=4) for d_ in range(2)] for sl in range(2)]
                WYF = [bigB[:, 8192 + sl * 2112:8192 + sl * 2112 + 1056].rearrange("p (t j) -> p t j", t=2) for sl in range(2)]
                WYB = [bigB[:, 8192 + sl * 2112 + 1056:8192 + (sl + 1) * 2112].rearrange("p (t j) -> p t j", t=2) for sl in range(2)]
                WY = [bigB[:, 12416 + sl * 1056:12416 + (sl + 1) * 1056].rearrange("p (t j) -> p t j", t=2) for sl in range(2)]
                WX = [scr[:, sl * 1024:(sl + 1) * 1024].rearrange("p (k t m) -> p k t m", k=4, t=2) for sl in range(2)]
                BX = [[scr[:, 2048 + sl * 1024 + pt * 512:2048 + sl * 1024 + (pt + 1) * 512] for pt in range(2)] for sl in range(2)]
                Ug = [scr[:, 4096 + i * 512:4096 + (i + 1) * 512].rearrange("p (k c) -> p k c", k=4) for i in range(2)]
                SG = [scr[:, 5120 + sl * 256:5120 + (sl + 1) * 256].rearrange("p (c t) -> p c t", t=2) for sl in range(2)]
                DDt = [scr[:, 5632 + sl * 128:5632 + (sl + 1) * 128] for sl in range(2)]
                BB = [[ssm_scr[:, sl * 1024 + pt * 512:sl * 1024 + (pt + 1) * 512] for pt in range(2)] for sl in range(2)]
                POL = [xr[2 * sl][:, 0:396].rearrange("p (t g k) -> p t g k", t=3, g=4) for sl in range(2)]
                POK = [xr[2 * sl + 1][:, 0:256].rearrange("p (t g k) -> p t g k", t=2, g=4) for sl in range(2)]
                P1, P2 = [ptmp[:, i, :] for i in range(2)]
                P3, P4 = pa(2816, 528), pa(3344, 528)
                MASKF, MASKB = cst[:, 0, :], cst[:, 1, :]
                RM = rmt[:, 0:8]
                TWO_PI = 2.0 * math.pi
                NB4 = 4
                def pbt(i, n=NB4 * 99):
                    return PBt[:, i * 396:i * 396 + n]
                T0, T1f, T2, T3, T4, TG = [pbt(i).rearrange("p (g k) -> p g k", g=NB4) for i in range(6)]
                T1i = PBt[:, 6 * 396:7 * 396].bitcast(I32).rearrange("p (g k) -> p g k", g=NB4)
                KAP = [PBt[:, 2772 + i * 128:2772 + (i + 1) * 128].rearrange("p (g k) -> p g k", g=NB4) for i in range(6)]
                D15 = PBt[:, 0:1920].rearrange("p (d m) -> p d m", d=15)

                def ssm_consts():
                    k.barrier()
                    hb_ = buf("hm")
                    k.op("pool", lambda e: e.memset(hm[:], 0.0), writes=[hb_])
                    k.op("pool", lambda e: e.memset(hm[0:64, 0:1], 1.0), reads=[hb_], writes=[hb_])
                    k.op("pool", lambda e: e.memset(hm[64:128, 1:2], 1.0), reads=[hb_], writes=[hb_])
                    pbf = buf("kall")
                    def io(ap, pat, base):
                        k.op("pool", lambda e: e.iota(out=ap, pattern=pat, base=base, channel_multiplier=0), writes=[pbf])
                    io(kalli[0:64, 0:32], [[-1, 32]], 0)
                    io(kalli[64:128, 0:32], [[1, 32]], -31)
                    io(kalli[0:64, 32:64], [[-1, 32]], 1)
                    io(kalli[64:128, 32:64], [[1, 32]], -30)
                    io(kalli[0:64, 64:97], [[1, 33]], 0)
                    io(kalli[64:128, 64:97], [[-1, 33]], 32)
                    io(kalli[:, 97:99], [[1, 2]], 31)
                    k.op("dve", lambda e: e.tensor_copy(out=kall[:], in_=kalli[:]), reads=[pbf], writes=[pbf])
                    k.op("dve", lambda e: e.tensor_copy(out=kall1[:], in_=kalli[:]), reads=[pbf], writes=[pbf])
                    k.op("dve", lambda e: e.tensor_scalar(out=kall1[:, 0:64], in0=kall1[:, 0:64], scalar1=31.0, scalar2=None, op0=ALU.add), reads=[pbf], writes=[pbf])
                    rb_ = buf("rmt")
                    k.op("pool", lambda e: e.iota(out=rmi[:, 8:9], pattern=[[0, 1]], base=0, channel_multiplier=1), writes=[rb_])
                    k.op("dve", lambda e: e.tensor_single_scalar(out=rmi[:, 8:9], in_=rmi[:, 8:9], scalar=4, op=ALU.arith_shift_right), reads=[rb_], writes=[rb_])
                    k.op("pool", lambda e: e.iota(out=rmi[:, 16:24], pattern=[[1, 8]], base=0, channel_multiplier=0), reads=[rb_], writes=[rb_])
                    k.op("dve", lambda e: e.tensor_copy(out=rmt[:, 8:9], in_=rmi[:, 8:9]), reads=[rb_], writes=[rb_])
                    k.op("dve", lambda e: e.tensor_copy(out=rmt[:, 16:24], in_=rmi[:, 16:24]), reads=[rb_], writes=[rb_])
                    k.op("dve", lambda e: e.tensor_scalar(out=rmt[:, 0:8], in0=rmt[:, 16:24], scalar1=rmt[:, 8:9], scalar2=None, op0=ALU.is_equal), reads=[rb_], writes=[rb_])
                    cb = buf("cst")
                    ci = PBt[:, 0:128].bitcast(I32)
                    k.op("pool", lambda e: e.iota(out=ci, pattern=[[1, 8], [0, 16]], base=0, channel_multiplier=0), writes=[buf("PBt")])
                    k.op("dve", lambda e: e.tensor_copy(out=cst[:, 1, :], in_=ci), reads=[buf("PBt")], writes=[cb])
                    k.op("dve", lambda e: e.tensor_scalar(out=cst[:, 0, :], in0=cst[:, 1, :], scalar1=rmt[:, 8:9], scalar2=None, op0=ALU.is_ge), reads=[cb, rb_], writes=[cb])
                    k.op("dve", lambda e: e.tensor_scalar(out=cst[:, 1, :], in0=cst[:, 1, :], scalar1=rmt[:, 8:9], scalar2=None, op0=ALU.is_le), reads=[cb, rb_], writes=[cb])
                    for l_ in range(DEPTH):
                        k.dma("sp", st_misc, bglu[:, l_, :], w["ssm_b_glu"][l_].rearrange("(kt p) -> p kt", p=128), writes=[buf("bglu")],
                              allow_slow_non_contiguous=True)

                def power_batch(bi, pab, phase1):
                    g0 = bi * NB4
                    sl = bi % 2
                    tb = buf("PBt")
                    pob = buf("PO%d" % sl)
                    kb3 = (kall1 if phase1 else kall)[:, :].unsqueeze(1).broadcast_to([128, NB4, 99])
                    def bc(pg):
                        return pg[:, g0:g0 + NB4].unsqueeze(2).broadcast_to([128, NB4, 99])
                    V = lambda fn: k.op("dve", fn, reads=[tb, pab, buf("kall")], writes=[tb])
                    A_ = lambda fn: k.op("act", fn, reads=[tb, pab], writes=[tb])
                    V(lambda e: e.scalar_tensor_tensor(out=T0, in0=bc(tht), scalar=1.0 / TWO_PI, in1=kb3, op0=ALU.mult, op1=ALU.mult))
                    V(lambda e: e.tensor_copy(out=T1i, in_=T0))
                    V(lambda e: e.tensor_copy(out=T1f, in_=T1i))
                    V(lambda e: e.tensor_tensor(out=T0, in0=T0, in1=T1f, op=ALU.subtract))
                    V(lambda e: e.tensor_scalar(out=T0, in0=T0, scalar1=0.49999, scalar2=-0.49999, op0=ALU.min, op1=ALU.max))
                    A_(lambda e: e.activation(out=T3, in_=T0, func=AF.Sin, scale=TWO_PI))
                    V(lambda e: e.tensor_scalar(out=TG, in0=T0, scalar1=0.25, scalar2=None, op0=ALU.is_gt))
                    V(lambda e: e.scalar_tensor_tensor(out=T0, in0=T0, scalar=0.25, in1=TG, op0=ALU.add, op1=ALU.subtract))
                    V(lambda e: e.tensor_scalar(out=T0, in0=T0, scalar1=0.49999, scalar2=-0.49999, op0=ALU.min, op1=ALU.max))
                    A_(lambda e: e.activation(out=T4, in_=T0, func=AF.Sin, scale=TWO_PI))
                    V(lambda e: e.tensor_tensor(out=T2, in0=bc(rho), in1=kb3, op=ALU.mult))
                    A_(lambda e: e.activation(out=T2, in_=T2, func=AF.Exp))
                    V(lambda e: e.tensor_tensor(out=T3, in0=T3, in1=T2, op=ALU.mult))
                    V(lambda e: e.tensor_tensor(out=T4, in0=T4, in1=T2, op=ALU.mult))
                    Nr, Ni, kr_, ki_, t1, t2 = KAP
                    V(lambda e: e.tensor_tensor(out=Nr, in0=T4[:, :, 32:64], in1=T4[:, :, 0:32], op=ALU.subtract))
                    V(lambda e: e.tensor_tensor(out=Ni, in0=T3[:, :, 32:64], in1=T3[:, :, 0:32], op=ALU.subtract))
                    def bc32(pg):
                        return pg[:, g0:g0 + NB4].unsqueeze(2).broadcast_to([128, NB4, 32])
                    VO = lambda fn: k.op("dve", fn, reads=[tb, pab], writes=[pob])
                    V(lambda e: e.tensor_tensor(out=t1, in0=Nr, in1=bc32(arn), op=ALU.mult))
                    V(lambda e: e.tensor_tensor(out=t2, in0=Ni, in1=bc32(ain), op=ALU.mult))
                    VO(lambda e: e.tensor_tensor(out=POK[sl][:, 0], in0=t1, in1=t2, op=ALU.add))
                    V(lambda e: e.tensor_tensor(out=t1, in0=Ni, in1=bc32(arn), op=ALU.mult))
                    V(lambda e: e.tensor_tensor(out=t2, in0=Nr, in1=bc32(ain), op=ALU.mult))
                    VO(lambda e: e.tensor_tensor(out=POK[sl][:, 1], in0=t1, in1=t2, op=ALU.subtract))
                    if phase1:
                        plb = buf("PAL")
                        k.op("dve", lambda e: e.tensor_copy(out=L32r[:, g0:g0 + NB4], in_=T4[:, :, 98]), reads=[tb], writes=[plb])
                        k.op("dve", lambda e: e.tensor_copy(out=L32i[:, g0:g0 + NB4], in_=T3[:, :, 98]), reads=[tb], writes=[plb])
                    else:
                        VO(lambda e: e.tensor_copy(out=POL[sl][:, 0], in_=T4[:, :, 64:97]))
                        VO(lambda e: e.tensor_copy(out=POL[sl][:, 1], in_=T3[:, :, 64:97]))
                        VO(lambda e: e.tensor_scalar(out=POL[sl][:, 2], in0=T3[:, :, 64:97], scalar1=-1.0, scalar2=None, op0=ALU.mult))

                def build_BB(g, pab, dst=None, dname="BB"):
                    sl, bi, gi = g % 2, g // NB4, g % NB4
                    if dst is None:
                        dst = BB
                    pob, bbb, p12 = buf("PO%d" % (bi % 2)), buf("%s%d" % (dname, sl)), buf("P12")
                    kr_, ki_ = POK[bi % 2][:, 0], POK[bi % 2][:, 1]
                    def kb_(t):
                        return t[:, gi, :].unsqueeze(2).broadcast_to([128, 32, 16])
                    def bb_(t):
                        return t[:, g, :].unsqueeze(1).broadcast_to([128, 32, 16])
                    p1v = P1[:, 0:512].rearrange("p (i h) -> p i h", h=16)
                    p2v = P2[:, 0:512].rearrange("p (i h) -> p i h", h=16)
                    V = lambda fn, w_: k.op("dve", fn, reads=[pob, pab, p12], writes=w_)
                    V(lambda e: e.tensor_tensor(out=p1v, in0=kb_(kr_), in1=bb_(Bre), op=ALU.mult), [p12])
                    V(lambda e: e.tensor_tensor(out=p2v, in0=kb_(ki_), in1=bb_(Bim), op=ALU.mult), [p12])
                    V(lambda e: e.tensor_tensor(out=dst[sl][0], in0=P1[:, 0:512], in1=P2[:, 0:512], op=ALU.subtract), [bbb])
                    V(lambda e: e.tensor_tensor(out=p1v, in0=kb_(kr_), in1=bb_(Bim), op=ALU.mult), [p12])
                    V(lambda e: e.tensor_tensor(out=p2v, in0=kb_(ki_), in1=bb_(Bre), op=ALU.mult), [p12])
                    V(lambda e: e.tensor_tensor(out=dst[sl][1], in0=P1[:, 0:512], in1=P2[:, 0:512], op=ALU.add), [bbb])
                    return bbb

                def build_U(g, slot):
                    kc, gl = g // 8, g % 8
                    ub_, ubank, ubb = buf("U%d" % slot), pb[slot], buf("pb%d" % slot)
                    for kti in range(4):
                        for i8 in range(8):
                            off = 8 * kti + i8
                            rhs = uT[:, kc, off:SEQ:T1]
                            k.op("pe", lambda e: e.matmul(ubank[:, kti * 128:(kti + 1) * 128], SelM[:, gl, i8, :], rhs, start=(i8 == 0), stop=(i8 == 7)),
                                 reads=[buf("uT"), buf("SelM")], writes=[ubb])
                    k.op("act", act(Ug[slot], ubank[:].rearrange("p (k c) -> p k c", k=4), AF.Copy), reads=[ubb], writes=[ub_])
                    return ub_

                def ssm(l):
                    k.barrier()
                    pab = buf("PA")
                    for nm, dst_off in (("ssm_a_re", 0), ("ssm_a_im", 1)):
                        k.dma("sp", st_misc, tmpA[0:32, dst_off * 128:(dst_off + 1) * 128].rearrange("g (d p) -> g d p", d=2),
                              w[nm][l].rearrange("d g p -> g d p"), writes=[pab])
                    for d_ in range(2):
                        k.dma("sp", st_misc, ldt[64 * d_:64 * d_ + 64, :], w["ssm_log_dt"][l, d_].partition_broadcast(64), writes=[pab],
                              allow_slow_non_contiguous=True)
                        k.dma("sp", st_misc, Bre[64 * d_:64 * d_ + 64, :, :], w["ssm_b_re"][l, d_].rearrange("g p h -> p g h"), writes=[pab])
                        k.dma("sp", st_misc, Bim[64 * d_:64 * d_ + 64, :, :], w["ssm_b_im"][l, d_].rearrange("g p h -> p g h"), writes=[pab])
                    for i8 in range(8):
                        k.dma("sp", st_misc, Dv[16 * i8:16 * i8 + 16, :], w["ssm_d"][l].rearrange("(g h) -> h g", h=16), writes=[pab],
                              allow_slow_non_contiguous=True)
                    if SSM_STOP <= 0:
                        raise StopSSM()
                    tb = buf("PBt")
                    Cin = PBt[:, 0:512].rearrange("p (b m) -> p b m", b=4)
                    p6, p6b = pb[6], buf("pb6")
                    for nm, dstC in (("ssm_c_re", Cre), ("ssm_c_im", Cim)):
                        for d_ in range(2):
                            k.dma("sp", st_misc, Cin[:, :, 64 * d_:64 * d_ + 64], w[nm][l, d_].rearrange("(gb g8) h p -> (g8 h) gb p", g8=8), writes=[tb])
                        for gb in range(4):
                            k.op("pe", lambda e: e.transpose(p6[:, gb * 128:(gb + 1) * 128], Cin[:, gb, :], ident_f[:]), reads=[tb, buf("identf")], writes=[p6b])
                        k.op("dve", lambda e: e.tensor_copy(out=dstC.rearrange("p g h -> p (g h)"), in_=p6[:]), reads=[p6b], writes=[pab])
                    if SSM_STOP <= 1:
                        raise StopSSM()
                    for i_, dstA in ((0, are), (1, aim)):
                        k.op("pe", lambda e: e.transpose(p6[:, i_ * 32:(i_ + 1) * 32], tmpA[0:32, i_ * 128:(i_ + 1) * 128], ident_f[0:32, 0:32]),
                             reads=[pab, buf("identf")], writes=[p6b])
                    k.op("dve", lambda e: e.tensor_copy(out=are, in_=p6[:, 0:32]), reads=[p6b], writes=[pab])
                    k.op("dve", lambda e: e.tensor_copy(out=aim, in_=p6[:, 32:64]), reads=[p6b], writes=[pab])
                    if SSM_STOP <= 2:
                        raise StopSSM()
                    V = lambda fn: k.op("dve", fn, reads=[pab], writes=[pab])
                    k.op("act", act(dtv, ldt, AF.Exp), reads=[pab], writes=[pab])
                    V(lambda e: e.tensor_tensor(out=rho, in0=dtv, in1=are, op=ALU.mult))
                    V(lambda e: e.tensor_tensor(out=tht, in0=dtv, in1=aim, op=ALU.mult))
                    V(lambda e: e.tensor_tensor(out=den, in0=are, in1=are, op=ALU.mult))
                    V(lambda e: e.tensor_tensor(out=arn, in0=aim, in1=aim, op=ALU.mult))
                    V(lambda e: e.tensor_tensor(out=den, in0=den, in1=arn, op=ALU.add))
                    V(lambda e: e.reciprocal(out=den, in_=den))
                    V(lambda e: e.tensor_tensor(out=arn, in0=are, in1=den, op=ALU.mult))
                    V(lambda e: e.tensor_tensor(out=ain, in0=aim, in1=den, op=ALU.mult))
                    if SSM_STOP <= 3:
                        raise StopSSM()
                    selb = buf("SelM")
                    k.op("pool", lambda e: e.memset(D15, 0.0), reads=[tb], writes=[tb])
                    k.op("pool", lambda e: e.affine_select(out=D15, in_=D15, pattern=[[-16, 15], [1, 128]], base=112, channel_multiplier=-1,
                                                           compare_op=ALU.not_equal, fill=1.0), reads=[tb], writes=[tb])
                    for a_ in range(8):
                        for b_ in range(8):
                            k.op("dve", lambda e: e.tensor_scalar(out=SelM[:, a_, b_, :], in0=D15[:, b_ - a_ + 7, :], scalar1=RM[:, a_:a_ + 1], scalar2=None, op0=ALU.mult),
                                 reads=[tb, buf("rmt")], writes=[selb])
                    if SSM_STOP <= 4:
                        raise StopSSM()
                    xsb = buf("XS")
                    plb = buf("PAL")
                    p6b = None

                    def p1_A(g):
                        build_BB(g, pab, BX, "BX")

                    def p1_B(g):
                        sl = g % 2
                        bxb, wxb = buf("BX%d" % sl), buf("WX%d" % sl)
                        tb_, tbb_ = pb[6 + sl], buf("pb%d" % (6 + sl))
                        tv = tb_[:].bitcast(BF16)
                        for kt in range(4):
                            for part in range(2):
                                col = (kt * 2 + part) * 128
                                k.op("pe", lambda e: e.transpose(tv[:, col:col + 128], BX[sl][part][:, kt * 128:(kt + 1) * 128], ident_b[:]), reads=[bxb, buf("identb")], writes=[tbb_])
                        k.op("act", act(WX[sl].rearrange("p k t m -> p (k t m)"), tv, AF.Copy), reads=[tbb_], writes=[wxb])
                        ub_ = build_U(g, sl)
                        xbank, xbb = pb[2 + sl], buf("pb%d" % (2 + sl))
                        for part in range(2):
                            for kt in range(4):
                                k.op("pe", lambda e: e.matmul(xbank[:, part * 128:(part + 1) * 128], WX[sl][:, kt, part, :], Ug[sl][:, kt, :], start=(kt == 0), stop=(kt == 3)),
                                     reads=[wxb, ub_], writes=[xbb])
                        k.op("act", act(XS[:, :, g, :].rearrange("p c t -> p t c"), xbank[:, 0:256].rearrange("p (t c) -> p t c", t=2), AF.Copy),
                             reads=[xbb], writes=[xsb])

                    for step in range(G + 1):
                        if 0 <= step - 1 < G:
                            p1_B(step - 1)
                        if step < G:
                            if step % NB4 == 0:
                                power_batch(step // NB4, pab, True)
                            p1_A(step)
                    if SSM_STOP <= 6:
                        raise StopSSM()
                    LrB = sctmp[:, 2, :].rearrange("p (g t) -> p g t", t=2)
                    LiS = sctmp[:, 3, :].rearrange("p (g t) -> p g t", t=2)
                    scb = buf("sctmp")
                    k.op("dve", lambda e: e.tensor_copy(out=LrB, in_=L32r.unsqueeze(2).broadcast_to([128, G, 2])), reads=[plb], writes=[scb])
                    k.op("dve", lambda e: e.tensor_scalar(out=LiS[:, :, 0], in0=L32i, scalar1=-1.0, scalar2=None, op0=ALU.mult), reads=[plb], writes=[scb])
                    k.op("dve", lambda e: e.tensor_copy(out=LiS[:, :, 1], in_=L32i), reads=[plb], writes=[scb])
                    for s_ in range(1, NCH):
                        for d_ in range(2):
                            en_ = "dve" if d_ == 0 else "pool"
                            lo, hi = 64 * d_, 64 * d_ + 64
                            c = s_ if d_ == 0 else NCH - 1 - s_
                            cp = c - 1 if d_ == 0 else c + 1
                            prev = XS[lo:hi, cp, :, :]
                            cur = XS[lo:hi, c, :, :]
                            t1_ = sctmp[lo:hi, 0, :].rearrange("p (g t) -> p g t", t=2)
                            t2_ = sctmp[lo:hi, 1, :].rearrange("p (g t) -> p g t", t=2)
                            tb1, tb2 = buf("sct1_%d" % d_), buf("sct2_%d" % d_)
                            xh = buf("XS%d" % d_)
                            k.op(en_, lambda e: e.tensor_tensor(out=t1_, in0=prev, in1=LrB[lo:hi], op=ALU.mult), reads=[xh, xsb, scb], writes=[tb1])
                            k.op(en_, lambda e: e.tensor_tensor(out=t2_[:, :, 0], in0=prev[:, :, 1], in1=LiS[lo:hi, :, 0], op=ALU.mult), reads=[xh, xsb, scb], writes=[tb2])
                            k.op(en_, lambda e: e.tensor_tensor(out=t2_[:, :, 1], in0=prev[:, :, 0], in1=LiS[lo:hi, :, 1], op=ALU.mult), reads=[xh, xsb, scb], writes=[tb2])
                            k.op(en_, lambda e: e.tensor_tensor(out=t1_, in0=t1_, in1=t2_, op=ALU.add), reads=[tb1, tb2], writes=[tb1])
                            k.op(en_, lambda e: e.tensor_tensor(out=cur, in0=cur, in1=t1_, op=ALU.add), reads=[tb1, xh, xsb], writes=[xh])
                    xs_done = [buf("XS0"), buf("XS1"), xsb]
                    if SSM_STOP <= 7:
                        raise StopSSM()
                    zb = buf("Zt")

                    def p2_A(g):
                        sl, bi, gi = g % 2, g // NB4, g % NB4
                        pob = buf("PO%d" % (bi % 2))
                        build_BB(g, pab)
                        ddb = buf("DD%d" % sl)
                        k.op("dve", lambda e: e.tensor_scalar(out=DDt[sl], in0=ident_f[:], scalar1=Dv[:, g:g + 1], scalar2=None, op0=ALU.mult), reads=[pab, buf("identf")], writes=[ddb])
                        wyb, wyfb, p34 = buf("WY%d" % sl), buf("WYFB%d" % sl), buf("P34")
                        LPr, LPi = POL[bi % 2][:, 0], POL[bi % 2][:, 1]
                        def lp(t):
                            return t[:, gi, :].unsqueeze(2).broadcast_to([128, 33, 16])
                        def cb_(t):
                            return t[:, g, :].unsqueeze(1).broadcast_to([128, 33, 16])
                        p3v = P3.rearrange("p (t h) -> p t h", h=16)
                        p4v = P4.rearrange("p (t h) -> p t h", h=16)
                        Pl = lambda fn, w_: k.op("pool", fn, reads=[pob, pab, p34], writes=w_)
                        LPin = POL[bi % 2][:, 2]
                        Pl(lambda e: e.tensor_tensor(out=p3v, in0=cb_(Cre), in1=lp(LPr), op=ALU.mult), [p34])
                        Pl(lambda e: e.tensor_tensor(out=p4v, in0=cb_(Cim), in1=lp(LPi), op=ALU.mult), [p34])
                        Pl(lambda e: e.tensor_tensor(out=WY[sl][:, 0, :], in0=P3, in1=P4, op=ALU.subtract), [wyb])
                        Pl(lambda e: e.tensor_tensor(out=p3v, in0=cb_(Cre), in1=lp(LPin), op=ALU.mult), [p34])
                        Pl(lambda e: e.tensor_tensor(out=p4v, in0=cb_(Cim), in1=lp(LPr), op=ALU.mult), [p34])
                        Pl(lambda e: e.tensor_tensor(out=WY[sl][:, 1, :], in0=P3, in1=P4, op=ALU.subtract), [wyb])
                        k.op("act", lambda e: e.activation(out=WYF[sl], in_=WY[sl], func=AF.Copy, scale=hm[:, 0:1]), reads=[wyb, buf("hm")], writes=[wyfb])
                        k.op("act", lambda e: e.activation(out=WYB[sl], in_=WY[sl], func=AF.Copy, scale=hm[:, 1:2]), reads=[wyb, buf("hm")], writes=[wyfb])

                    def p2_B(g):
                        sl = g % 2
                        bbb, wyfb = buf("BB%d" % sl), buf("WYFB%d" % sl)
                        wtb = [buf("WT%d_%d" % (sl, d_)) for d_ in range(2)]
                        for d_ in range(2):
                            toff = 0 if d_ == 0 else 16
                            wyp = WYF[sl] if d_ == 0 else WYB[sl]
                            for kt in range(4):
                                jlo, jhi = (kt * 128, 512) if d_ == 0 else (0, (kt + 1) * 128)
                                bi_ = 4 + 2 * d_ + kt % 2
                                wbank, wbb = pb[bi_], buf("pb%d" % bi_)
                                k.op("pe", lambda e: e.matmul(wbank[:, jlo:jhi], BB[sl][0][:, kt * 128:(kt + 1) * 128], wyp[:, 0, toff + jlo:toff + jhi], start=True, stop=False),
                                     reads=[bbb, wyfb], writes=[wbb])
                                k.op("pe", lambda e: e.matmul(wbank[:, jlo:jhi], BB[sl][1][:, kt * 128:(kt + 1) * 128], wyp[:, 1, toff + jlo:toff + jhi], start=False, stop=True),
                                     reads=[bbb, wyfb], writes=[wbb])
                                dlo = kt * 128
                                olo, ohi = (dlo + 128, 512) if d_ == 0 else (0, dlo)
                                if ohi > olo:
                                    k.op("act", act(WT[sl][d_][:, kt, olo:ohi], wbank[:, olo:ohi], AF.Copy), reads=[wbb], writes=[wtb[d_]])
                                msk = MASKF if d_ == 0 else MASKB
                                k.op("dve", lambda e: e.tensor_tensor(out=WT[sl][d_][:, kt, dlo:dlo + 128], in0=wbank[:, dlo:dlo + 128], in1=msk, op=ALU.mult), reads=[wbb, buf("cst")], writes=[wtb[d_]])
                        sgb = buf("SG%d" % sl)
                        k.op("act", act(SG[sl], XS[:, :, g, :], AF.Copy), reads=xs_done, writes=[sgb])
                        build_U(g, sl)

                    def p2_C(g):
                        sl = g % 2
                        kc, gl = g // 8, g % 8
                        wyfb, sgb, ddb, ub_ = buf("WYFB%d" % sl), buf("SG%d" % sl), buf("DD%d" % sl), buf("U%d" % sl)
                        wtb = [buf("WT%d_%d" % (sl, d_)) for d_ in range(2)]
                        ybank, ybb = pb[2 + sl], buf("pb%d" % (2 + sl))
                        for mt in range(4):
                            mlo = mt * 128
                            first = True
                            for kt in range(0, mt + 1):
                                k.op("pe", lambda e: e.matmul(ybank[:, mlo:mlo + 128], WT[sl][0][:, kt, mlo:mlo + 128], Ug[sl][:, kt, :], start=first, stop=False),
                                     reads=[wtb[0], ub_], writes=[ybb])
                                first = False
                            for kt in range(mt, 4):
                                k.op("pe", lambda e: e.matmul(ybank[:, mlo:mlo + 128], WT[sl][1][:, kt, mlo:mlo + 128], Ug[sl][:, kt, :], start=False, stop=False),
                                     reads=[wtb[1], ub_], writes=[ybb])
                            k.op("pe", lambda e: e.matmul(ybank[:, mlo:mlo + 128], DDt[sl], Ug[sl][:, mt, :], start=False, stop=False), reads=[ddb, ub_], writes=[ybb])
                            for part in range(2):
                                k.op("pe", lambda e: e.matmul(ybank[:, mlo + 1:mlo + 128], WYF[sl][:, part, 16 + mlo:16 + mlo + 128], SG[sl][:, 0:NCH - 1, part], start=False, stop=False),
                                     reads=[wyfb, sgb], writes=[ybb])
                            for part in range(2):
                                k.op("pe", lambda e: e.matmul(ybank[:, mlo:mlo + 127], WYB[sl][:, part, mlo:mlo + 128], SG[sl][:, 1:NCH, part], start=False, stop=(part == 1)),
                                     reads=[wyfb, sgb], writes=[ybb])
                        k.op("act", act(Zt[:, gl, :, :].rearrange("p m c -> p (m c)"), ybank[:], AF.Gelu_apprx_tanh), reads=[ybb], writes=[zb])
                        if gl == 7:
                            ubuf = buf("uT")
                            for j4 in range(8):
                                ibank, ibb = pb[j4 % 2], buf("pb%d" % (j4 % 2))
                                for jj in range(4):
                                    j = j4 * 4 + jj
                                    for gl2 in range(8):
                                        k.op("pe", lambda e: e.matmul(ibank[:, jj * 128:(jj + 1) * 128], SelM[:, j % 8, gl2, :], Zt[:, gl2, j // 8, :], start=(gl2 == 0), stop=(gl2 == 7)),
                                             reads=[zb, buf("SelM")], writes=[ibb])
                                dst = uT[:, kc, :].rearrange("p (c j) -> p j c", j=T1)[:, j4 * 4:(j4 + 1) * 4, :]
                                if j4 % 2 == 0:
                                    k.op("act", act(dst, ibank[:].rearrange("p (j c) -> p j c", j=4), AF.Copy), reads=[ibb], writes=[ubuf])
                                else:
                                    k.op("dve", lambda e: e.tensor_copy(out=dst, in_=ibank[:].rearrange("p (j c) -> p j c", j=4)), reads=[ibb], writes=[ubuf])

                    for step in range(G + 2):
                        if 0 <= step - 1 < G:
                            p2_B(step - 1)
                        if 0 <= step - 2 < G:
                            p2_C(step - 2)
                        if step < G:
                            if step % NB4 == 0:
                                power_batch(step // NB4, pab, False)
                            p2_A(step)
                    if SSM_STOP <= 8:
                        raise StopSSM()
                    k.barrier()
                    mb = buf("mixedT")
                    zTb = buf("uT")
                    for nt in range(4):
                        s_ = wctr[0] % 6
                        wctr[0] += 1
                        wb = buf("wgu%d" % s_)
                        wt = wgu[s_]
                        k.dma("pool", st_w[s_], wt[:, 0:4, :], w["ssm_w_glu"][l][:, nt * 128:(nt + 1) * 128].rearrange("(kt p) f -> p kt f", p=128), writes=[wb])
                        for blk in range(8):
                            bank, bb = pb[blk % 2], buf("pb%d" % (blk % 2))
                            for kt in range(4):
                                k.op("pe", lambda e: e.matmul(bank[:], wt[:, kt, :], uT[:, kt, blk * 512:(blk + 1) * 512], start=(kt == 0), stop=(kt == 3)),
                                     reads=[wb, zTb], writes=[bb])
                            sgs, sgbf = sg[blk % 2], buf("sg%d" % (blk % 2))
                            k.op("act", lambda e: e.activation(out=sgs, in_=bank[:], func=AF.Sigmoid, bias=bglu[:, l, nt:nt + 1], scale=1.0), reads=[bb, buf("bglu")], writes=[sgbf])
                            k.op("dve", lambda e: e.tensor_tensor(out=mixedT[:, nt, blk * 512:(blk + 1) * 512], in0=uT[:, nt, blk * 512:(blk + 1) * 512], in1=sgs, op=ALU.mult),
                                 reads=[sgbf, zTb], writes=[mb])
                    for blk in range(8):
                        pbn, pbb = pb[6], buf("pb6")
                        for kt in range(4):
                            q2 = kt % 2
                            sqb = buf("sq%d" % q2)
                            k.op("act", act(sq[q2][:], mixedT[:, kt, blk * 512:(blk + 1) * 512], AF.Square), reads=[mb], writes=[sqb])
                            k.op("pe", lambda e: e.matmul(pbn[:], ones_bf[:], sq[q2][:], start=(kt == 0), stop=(kt == 3)), reads=[sqb, buf("ones")], writes=[pbb])
                        s2 = blk % 2
                        rb = buf("rs%d" % s2)
                        k.op("act", act(rs[s2][:], pbn[:], AF.Sqrt, scale=1.0 / 512, bias=eps_t[:, 0:1]), reads=[pbb, buf("eps")], writes=[rb])
                        k.op("dve", lambda e: e.reciprocal(out=rs[s2][:], in_=rs[s2][:]), reads=[rb], writes=[rb])
                        for kt in range(4):
                            k.op("dve", lambda e: e.scalar_tensor_tensor(out=mixedT[:, kt, blk * 512:(blk + 1) * 512], in0=mixedT[:, kt, blk * 512:(blk + 1) * 512],
                                                                         scalar=gains2[:, l, 1, kt:kt + 1], in1=rs[s2][:], op0=ALU.mult, op1=ALU.mult),
                                 reads=[mb, rb, buf("gains2")], writes=[mb])
                    out_proj(l, 1, mb)
                    k.barrier()

                if do_ssm:
                    ssm_consts()
                    k.barrier()
                for l in range(depth):
                    if do_ffn:
                        ffn(l, "ffn1", gidx[("mix_norm", l)] if do_mixer else None)
                    if do_mixer:
                        mixer(l)
                        if do_ssm:
                            try:
                                ssm(l)
                            except StopSSM:
                                k.barrier()
                    if do_ffn:
                        ffn(l, "ffn2", gidx[("ffn1_norm", l + 1)] if l + 1 < depth else None)

                gfin = gidx["final"]
                for blk in range(8):
                    s = blk % 2
                    xb = buf("xn%d" % s)
                    k.dma("sp", st_x[s], xn[s][:], xres[:, :, blk * 512:(blk + 1) * 512].rearrange("kt p n -> p kt n"),
                          reads=[XB[kt][blk] for kt in range(8)], writes=[xb])
                    pbn, pbb = pb[6], buf("pb6")
                    for kt in range(8):
                        q2 = kt % 2
                        sqb = buf("sq%d" % q2)
                        k.op("act", act(sq[q2][:], xn[s][:, kt, :], AF.Square), reads=[xb], writes=[sqb])
                        k.op("pe", lambda e, kt=kt, q2=q2: e.matmul(pbn[:], ones_bf[:], sq[q2][:], start=(kt == 0), stop=(kt == 7)),
                             reads=[sqb, buf("ones")], writes=[pbb])
                    rb = buf("rs%d" % s)
                    k.op("act", act(rs[s][:], pbn[:], AF.Sqrt, scale=1.0 / D, bias=eps_t[:, 0:1]), reads=[pbb, buf("eps")], writes=[rb])
                    k.op("dve", lambda e: e.reciprocal(out=rs[s][:], in_=rs[s][:]), reads=[rb], writes=[rb])
                    for kt in range(8):
                        k.op("dve", lambda e, kt=kt: e.scalar_tensor_tensor(out=xn[s][:, kt, :], in0=xn[s][:, kt, :],
                                                                           scalar=gains[:, gfin, kt:kt + 1], in1=rs[s][:],
                                                                           op0=ALU.mult, op1=ALU.mult),
                             reads=[xb, rb, buf("gains")], writes=[xb])
                    for t4 in range(4):
                        tt = blk * 4 + t4
                        xt, xtb = xtok[tt % 2], buf("xtok%d" % (tt % 2))
                        for kt in range(8):
                            bank, bb = pb[kt % 4], buf("pb%d" % (kt % 4))
                            k.op("pe", lambda e, bank=bank, kt=kt, t4=t4: e.transpose(bank[:, 0:128], xn[s][:, kt, t4 * 128:(t4 + 1) * 128], ident_f[:]),
                                 reads=[xb, buf("identf")], writes=[bb])
                            if kt % 2 == 0:
                                k.op("act", act(xt[:, kt * 128:(kt + 1) * 128], bank[:, 0:128], AF.Copy), reads=[bb], writes=[xtb])
                            else:
                                k.op("dve", lambda e, bank=bank, xt=xt, kt=kt: e.tensor_copy(out=xt[:, kt * 128:(kt + 1) * 128], in_=bank[:, 0:128]),
                                     reads=[bb], writes=[xtb])
                        k.dma("sp", st_out[tt % 2], y_out[tt * 128:(tt + 1) * 128, :], xt[:], reads=[xtb], writes=[buf("yout")])
                for s_ in st_out:
                    k.wait_stream("sp", s_)
                if k.active == "pe":
                    print("instructions:", k.ninstr, k.cnt)

        with nc.Block() as block:
            @block.tensor
            def _(en):
                gen(K(nc, sems, dsems, "pe", en))

            @block.scalar
            def _(en):
                gen(K(nc, sems, dsems, "act", en))

            @block.vector
            def _(en):
                gen(K(nc, sems, dsems, "dve", en))

            @block.gpsimd
            def _(en):
                gen(K(nc, sems, dsems, "pool", en))

            @block.sync
            def _(en):
                gen(K(nc, sems, dsems, "sp", en))
    return nc


_BUCKET = None


def _bias_host(rel_bias_table):
    rel = (np.arange(384)[None, :] - 128) - np.arange(128)[:, None]
    bucket = t5_bucket_np(rel)
    b = np.asarray(rel_bias_table)[bucket]
    b = np.transpose(b, (0, 2, 1)).copy()
    band = np.abs(rel) <= 128
    b = np.where(band[:, None, :], b, np.float32(NEG)).astype(np.float32)
    return np.ascontiguousarray(b)


def kernel(**inputs):
    x = np.asarray(inputs["x"], dtype=np.float32)
    nb = x.shape[0]
    nc = build()
    shared = {nm: np.ascontiguousarray(np.asarray(v, dtype=np.float32)) for nm, v in inputs.items()
              if nm not in ("x", "rel_bias_table")}
    shared["biasm"] = _bias_host(inputs["rel_bias_table"])
    in_maps = []
    for b in range(nb):
        m = dict(shared)
        m["x"] = np.ascontiguousarray(x[b])
        in_maps.append(m)
    res = run_bass_kernel_spmd(nc, in_maps, core_ids=list(range(nb)))
    return np.stack([r["y"] for r in res.results], axis=0).astype(np.float32)
```

```python
import math
import numpy as np
import concourse.bass as bass
import concourse.mybir as mybir
from concourse.bass_utils import run_bass_kernel_spmd

F32 = mybir.dt.float32
BF16 = mybir.dt.bfloat16
I32 = mybir.dt.int32
AF = mybir.ActivationFunctionType
ALU = mybir.AluOpType
AX = mybir.AxisListType

D = 1024
SEQ = 4096
DEPTH = 4
DFF = 2816
NFT = DFF // 128
DIN = 1280
NH = 8
HD = 64
G = 32
P = 64
HS = 16
T1 = 32
NCH = SEQ // T1
EPS = 1e-6
NEG = -1e30
N_BUCKETS = 32
MAX_DISTANCE = 128
TT = 2048
SSM_STOP = 99


class StopSSM(Exception):
    pass
NPASS = SEQ // TT


class Buf:
    __slots__ = ("name", "w", "r")

    def __init__(self, name):
        self.name = name
        self.w = None
        self.r = {}


class Stream:
    __slots__ = ("sem", "total", "name")

    def __init__(self, sem, name):
        self.sem = sem
        self.total = 0
        self.name = name


class K:
    def __init__(self, nc, sems, dsems, active=None, handle=None):
        self.nc = nc
        self.active = active
        self.eng = {"pe": nc.tensor, "act": nc.scalar, "dve": nc.vector, "pool": nc.gpsimd, "sp": nc.sync}
        if handle is not None:
            self.eng[active] = handle
        self.sem = sems
        self.cnt = {e: 0 for e in self.eng}
        self.waited = {e: {} for e in self.eng}
        self.dsems = list(dsems)
        self.streams = []
        self.ninstr = 0
        self.prog = {e: [] for e in self.eng}

    def stream(self, name):
        s = Stream(self.dsems.pop(), name)
        self.streams.append(s)
        return s

    def _wait(self, e, dep):
        if dep is None:
            return
        w = self.waited[e]
        if dep[0] == "e":
            _, e2, c = dep
            if e2 == e and e == "pe":
                return
            if w.get(e2, 0) >= c:
                return
            sem = self.sem[e2]
            if e == self.active:
                self.eng[e].wait_ge(sem, c)
            w[e2] = c
        else:
            s = dep[1]
            key = ("d", id(s))
            if w.get(key, 0) >= s.total:
                return
            tot = s.total
            if e == self.active:
                self.eng[e].wait_ge(s.sem, tot)
            w[key] = s.total

    def _deps(self, e, reads, writes):
        for b in reads:
            self._wait(e, b.w)
        for b in writes:
            self._wait(e, b.w)
            for dep in b.r.values():
                self._wait(e, dep)

    def op(self, e, fn, reads=(), writes=()):
        self._deps(e, reads, writes)
        sem = self.sem[e]
        if e == self.active:
            fn(self.eng[e]).then_inc(sem, 1)
        self.cnt[e] += 1
        dep = ("e", e, self.cnt[e])
        for b in reads:
            b.r[e] = dep
        for b in writes:
            b.w = dep
            b.r = {}
        self.ninstr += 1

    def dma(self, q, stream, out, in_, reads=(), writes=(), **kw):
        self._deps(q, reads, writes)
        if q == self.active:
            self.eng[q].dma_start(out=out, in_=in_, **kw).then_inc(stream.sem, 16)
        stream.total += 16
        dep = ("d", stream)
        for b in reads:
            b.r[("d", id(stream))] = dep
        for b in writes:
            b.w = dep
            b.r = {}
        self.ninstr += 1

    def barrier(self):
        for e in self.eng:
            for e2 in self.eng:
                if e2 != e and self.cnt[e2] > 0:
                    self._wait(e, ("e", e2, self.cnt[e2]))
            for s in self.streams:
                if s.total > 0:
                    self._wait(e, ("d", s))

    def wait_stream(self, e, s):
        if e == self.active:
            self.eng[e].wait_ge(s.sem, s.total)


def t5_bucket_np(rel):
    half = N_BUCKETS // 2
    max_exact = half // 2
    ret = np.where(rel > 0, half, 0)
    n = np.abs(rel)
    nf = np.maximum(n, 1).astype(np.float32)
    large = max_exact + (np.log(nf / max_exact) / math.log(MAX_DISTANCE / max_exact)
                         * (half - max_exact)).astype(np.int32)
    large = np.minimum(large, half - 1)
    return ret + np.where(n < max_exact, n, large)


def act(out, in_, func, **kw):
    return lambda e: e.activation(out=out, in_=in_, func=func, **kw)


def build(depth=DEPTH, do_mixer=True, do_ssm=True, dbg=None, do_ffn=True, do_attn=True):
    nc = bass.Bass("TRN2", target_bir_lowering=False)
    dt = nc.dram_tensor
    x_in = dt("x", [SEQ, D], F32, kind="ExternalInput").ap()
    biasm_d = dt("biasm", [128, NH, 384], F32, kind="ExternalInput").ap()
    w = {}
    for nm, shp in [("ffn1_norm", [DEPTH, D]), ("ffn1_w_gate", [DEPTH, D, DFF]), ("ffn1_w_up", [DEPTH, D, DFF]),
                    ("ffn1_w_down", [DEPTH, DFF, D]), ("mix_norm", [DEPTH, D]), ("w_in", [DEPTH, D, DIN]),
                    ("attn_sink", [DEPTH, NH]), ("ssm_a_re", [DEPTH, 2, G, P]), ("ssm_a_im", [DEPTH, 2, G, P]),
                    ("ssm_log_dt", [DEPTH, 2, G]), ("ssm_b_re", [DEPTH, 2, G, P, HS]),
                    ("ssm_b_im", [DEPTH, 2, G, P, HS]), ("ssm_c_re", [DEPTH, 2, G, HS, P]),
                    ("ssm_c_im", [DEPTH, 2, G, HS, P]), ("ssm_d", [DEPTH, 512]), ("ssm_w_glu", [DEPTH, 512, 512]),
                    ("ssm_b_glu", [DEPTH, 512]), ("attn_out_norm", [DEPTH, 512]), ("ssm_out_norm", [DEPTH, 512]),
                    ("w_out", [DEPTH, D, D]), ("ffn2_norm", [DEPTH, D]), ("ffn2_w_gate", [DEPTH, D, DFF]),
                    ("ffn2_w_up", [DEPTH, D, DFF]), ("ffn2_w_down", [DEPTH, DFF, D]), ("final_norm", [D])]:
        w[nm] = dt(nm, shp, F32, kind="ExternalInput").ap()
    y_out = dt("y", [SEQ, D], F32, kind="ExternalOutput").ap()
    xres = dt("xres", [8, 128, SEQ], F32, kind="Internal").ap()
    dbg_out = None
    if dbg is not None:
        dbg_out = dt("dbg", list(dbg), F32, kind="ExternalOutput").ap()

    from contextlib import ExitStack
    es = ExitStack()
    with es:
        def sb(name, shape, dtype):
            return es.enter_context(nc.sbuf_tensor(name, shape, dtype))

        def ps(name, shape, dtype):
            return es.enter_context(nc.psum_tensor(name, shape, dtype))

        sems = {e: es.enter_context(nc.semaphore("s_" + e)) for e in ["pe", "act", "dve", "pool", "sp"]}
        dsems = [es.enter_context(nc.semaphore("d%d" % i)) for i in range(40)]

        bigA = sb("bigA", [128, NFT * TT], BF16)
        bigB = sb("bigB", [128, 8 * TT], BF16)
        xn = [sb("xn%d" % i, [128, 8, 512], F32) for i in range(2)]
        sq = [sb("sq%d" % i, [128, 512], BF16) for i in range(2)]
        rs = [sb("rs%d" % i, [128, 512], F32) for i in range(2)]
        scr = sb("scr", [128, 7680], BF16)
        sg = [scr[:, 5632 + i * 1024:5632 + (i + 1) * 1024].bitcast(F32) for i in range(2)]
        wgu = [sb("wgu%d" % i, [128, 8, 128], BF16) for i in range(6)]
        wd = [scr[:, i * 2816:(i + 1) * 2816].rearrange("p (ft n) -> p ft n", n=128) for i in range(2)]
        xr = [sb("xr%d" % i, [128, 512], F32) for i in range(4)]
        gains = sb("gains", [128, DEPTH * 3 + 1, 8], F32)
        ones_bf = sb("ones_bf", [128, 128], BF16)
        ident_f = sb("ident_f", [128, 128], F32)
        ident_b = sb("ident_b", [128, 128], BF16)
        xtok = [bigB[:, i * 2048:(i + 1) * 2048].bitcast(F32) for i in range(4)]
        eps_t = sb("eps_t", [128, 1], F32)
        biasm = xn[1][:, :, 0:384]
        sinkb = sb("sinkb", [128, DEPTH * NH], F32)
        gains2 = sb("gains2", [128, DEPTH, 2, 4], F32)
        sc = [scr[:, i * 768:(i + 1) * 768].bitcast(F32) for i in range(2)]
        pexp = [scr[:, 1536 + i * 384:1536 + (i + 1) * 384] for i in range(2)]
        pT = [scr[:, 2304 + i * 384:2304 + (i + 1) * 384] for i in range(2)]
        atok = [scr[:, 3072 + i * 1024:3072 + (i + 1) * 1024].bitcast(F32) for i in range(2)]
        stat = [sb("stat%d" % i, [128, 4], F32) for i in range(2)]
        rowsum = [sb("rowsum%d" % i, [128, 2, NH], F32) for i in range(2)]
        nstat = [sb("nstat%d" % i, [128, 4], F32) for i in range(2)]
        kall = sb("kall", [128, 99], F32)
        kall1 = sb("kall1", [128, 99], F32)
        kalli = sb("kalli", [128, 99], I32)
        cst = sb("cst", [128, 2, 128], F32)
        rmt = sb("rmt", [128, 24], F32)
        rmi = sb("rmi", [128, 24], I32)
        bglu = sb("bglu", [128, DEPTH, 4], F32)
        ssm_scr = sb("ssm_scr", [128, 2048], BF16)
        ptmp = sb("ptmp", [128, 2, 528], F32)
        sctmp = sb("sctmp", [128, 4, 64], F32)
        hm = sb("hm", [128, 2], F32)

        pb = [ps("pb%d" % i, [128, 512], F32) for i in range(8)]

        def gen(k):
            B = {}

            def buf(name):
                if name not in B:
                    B[name] = Buf(name)
                return B[name]

            st_x = [k.stream("xn%d" % i) for i in range(2)]
            st_w = [k.stream("wgu%d" % i) for i in range(6)]
            st_wd = [k.stream("wd%d" % i) for i in range(2)]
            st_xr_in = [k.stream("xri%d" % i) for i in range(4)]
            st_xr_out = [k.stream("xro%d" % i) for i in range(4)]
            st_misc = k.stream("misc")
            st_tok = [k.stream("tok%d" % i) for i in range(4)]
            st_out = [k.stream("out%d" % i) for i in range(4)]

            if True:
                k.op("pool", lambda e: e.memset(ones_bf[:], 1.0), writes=[buf("ones")])
                k.op("pool", lambda e: e.memset(eps_t[:], EPS), writes=[buf("eps")])
                k.op("pool", lambda e: e.memset(ident_f[:], 0.0), writes=[buf("identf")])
                k.op("pool", lambda e: e.affine_select(out=ident_f[:], in_=ident_f[:], pattern=[[1, 128]], base=0,
                                                       channel_multiplier=-1, compare_op=ALU.not_equal, fill=1.0),
                     reads=[buf("identf")], writes=[buf("identf")])
                k.op("dve", lambda e: e.tensor_copy(out=ident_b[:], in_=ident_f[:]), reads=[buf("identf")],
                     writes=[buf("identb")])
                gidx = {}
                gi = 0
                with nc.allow_non_contiguous_dma(reason="small param loads"):
                    for l in range(DEPTH):
                        for nm in ["ffn1_norm", "mix_norm", "ffn2_norm"]:
                            gidx[(nm, l)] = gi
                            k.dma("sp", st_misc, gains[:, gi, :], w[nm][l].rearrange("(kt p) -> p kt", p=128),
                                  writes=[buf("gains")], allow_slow_non_contiguous=True)
                            gi += 1
                    gidx["final"] = gi
                    k.dma("sp", st_misc, gains[:, gi, :], w["final_norm"].rearrange("(kt p) -> p kt", p=128),
                          writes=[buf("gains")], allow_slow_non_contiguous=True)

                k.dma("sp", st_misc, sinkb[:], w["attn_sink"].rearrange("l h -> (l h)").partition_broadcast(128), writes=[buf("sinkb")],
                      allow_slow_non_contiguous=True)
                for l in range(DEPTH):
                    for i2, nm in enumerate(["attn_out_norm", "ssm_out_norm"]):
                        k.dma("sp", st_misc, gains2[:, l, i2, :], w[nm][l].rearrange("(kt p) -> p kt", p=128), writes=[buf("gains2")],
                              allow_slow_non_contiguous=True)
                XB = [[buf("xres_%d_%d" % (kt, nb)) for nb in range(8)] for kt in range(8)]

                for g4 in range(SEQ // 512):
                    stg = xn[g4 % 2]
                    stgb = buf("xn%d" % (g4 % 2))
                    for t4 in range(4):
                        tt = g4 * 4 + t4
                        xt = xtok[tt % 4]
                        xtb = buf("xtok%d" % (tt % 4))
                        k.dma("pool", st_tok[tt % 4], xt[:], x_in[tt * 128:(tt + 1) * 128, :], writes=[xtb])
                        for kt in range(8):
                            bank = pb[kt]
                            bb = buf("pb%d" % kt)
                            k.op("pe", lambda e, bank=bank, xt=xt, kt=kt: e.transpose(bank[:, 0:128], xt[:, kt * 128:(kt + 1) * 128], ident_f[:]),
                                 reads=[xtb, buf("identf")], writes=[bb])
                            eng = "act" if kt % 2 == 0 else "dve"
                            if eng == "act":
                                k.op("act", act(stg[:, kt, t4 * 128:(t4 + 1) * 128], bank[:, 0:128], AF.Copy),
                                     reads=[bb], writes=[stgb])
                            else:
                                k.op("dve", lambda e, bank=bank, stg=stg, kt=kt, t4=t4: e.tensor_copy(out=stg[:, kt, t4 * 128:(t4 + 1) * 128], in_=bank[:, 0:128]),
                                     reads=[bb], writes=[stgb])
                    k.dma("sp", st_x[g4 % 2], xres[:, :, g4 * 512:(g4 + 1) * 512].rearrange("kt p n -> p kt n"), stg[:],
                          reads=[stgb], writes=[XB[kt][g4] for kt in range(8)])

                hT = bigB[:].rearrange("p (kt n) -> p kt n", kt=8)

                def norm_block(gi_, blk, hcol, hb):
                    s = blk % 2
                    xb = buf("xn%d" % s)
                    k.dma("sp", st_x[s], xn[s][:], xres[:, :, blk * 512:(blk + 1) * 512].rearrange("kt p n -> p kt n"),
                          reads=[XB[kt][blk] for kt in range(8)], writes=[xb])
                    pbn = pb[6]
                    pbb = buf("pb6")
                    for kt in range(8):
                        q2 = kt % 2
                        sqb = buf("sq%d" % q2)
                        k.op("act", act(sq[q2][:], xn[s][:, kt, :], AF.Square), reads=[xb], writes=[sqb])
                        k.op("pe", lambda e, kt=kt, q2=q2: e.matmul(pbn[:], ones_bf[:], sq[q2][:], start=(kt == 0), stop=(kt == 7)),
                             reads=[sqb, buf("ones")], writes=[pbb])
                    rb = buf("rs%d" % s)
                    k.op("act", act(rs[s][:], pbn[:], AF.Sqrt, scale=1.0 / D, bias=eps_t[:, 0:1]), reads=[pbb, buf("eps")], writes=[rb])
                    k.op("dve", lambda e: e.reciprocal(out=rs[s][:], in_=rs[s][:]), reads=[rb], writes=[rb])
                    for kt in range(8):
                        k.op("dve", lambda e, kt=kt: e.scalar_tensor_tensor(out=hT[:, kt, hcol:hcol + 512], in0=xn[s][:, kt, :],
                                                                           scalar=gains[:, gi_, kt:kt + 1], in1=rs[s][:],
                                                                           op0=ALU.mult, op1=ALU.mult),
                             reads=[xb, rb, buf("gains")], writes=[hb])

                hid = bigA[:].rearrange("p (ft n) -> p ft n", ft=NFT)
                wctr = [0]
                wdctr = [0]
                xrctr = [0]

                def load_w(src_ap):
                    s = wctr[0] % 6
                    wctr[0] += 1
                    wb = buf("wgu%d" % s)
                    with nc.allow_non_contiguous_dma(reason="512B weight rows"):
                        k.dma("pool", st_w[s], wgu[s][:], src_ap.rearrange("(kt p) f -> p kt f", p=128), writes=[wb])
                    return wgu[s], wb

                def resid_update(nt, blk, psum_bank, pbb, scale):
                    s = xrctr[0] % 4
                    xrctr[0] += 1
                    xb = buf("xr%d" % s)
                    k.dma("sp", st_xr_in[s], xr[s][:], xres[nt, :, blk * 512:(blk + 1) * 512], reads=[XB[nt][blk]], writes=[xb])
                    k.op("dve", lambda e: e.scalar_tensor_tensor(out=xr[s][:], in0=psum_bank[:], scalar=float(scale), in1=xr[s][:],
                                                                 op0=ALU.mult, op1=ALU.add),
                         reads=[pbb, xb], writes=[xb])
                    k.dma("act", st_xr_out[s], xres[nt, :, blk * 512:(blk + 1) * 512], xr[s][:], reads=[xb], writes=[XB[nt][blk]])

                pre_done = set()

                def do_norm(gi_, blk, hcol, hb):
                    if (gi_, blk) in pre_done:
                        pre_done.discard((gi_, blk))
                        return
                    norm_block(gi_, blk, hcol, hb)

                def ffn(l, pre, next_gi=None):
                    wg_d, wu_d, wd_d = w[pre + "_w_gate"][l], w[pre + "_w_up"][l], w[pre + "_w_down"][l]
                    gi_ = gidx[(pre + "_norm", l)]
                    for tp in range(NPASS):
                        hbs = [buf("hT%d" % nb) for nb in range(4)]
                        for nb in range(4):
                            do_norm(gi_, tp * 4 + nb, nb * 512, hbs[nb])
                        for ft in range(NFT):
                            wg_t, wg_b = load_w(wg_d[:, ft * 128:(ft + 1) * 128])
                            wu_t, wu_b = load_w(wu_d[:, ft * 128:(ft + 1) * 128])
                            hidb = buf("hid%d" % ft)
                            for nb in range(4):
                                pg, pgb = pb[nb % 2], buf("pb%d" % (nb % 2))
                                pu, pub = pb[2 + nb % 2], buf("pb%d" % (2 + nb % 2))
                                for kt in range(8):
                                    k.op("pe", lambda e, kt=kt, pg=pg, wg_t=wg_t, nb=nb: e.matmul(pg[:], wg_t[:, kt, :], hT[:, kt, nb * 512:(nb + 1) * 512],
                                                                                                   start=(kt == 0), stop=(kt == 7)),
                                         reads=[wg_b, hbs[nb]], writes=[pgb])
                                for kt in range(8):
                                    k.op("pe", lambda e, kt=kt, pu=pu, wu_t=wu_t, nb=nb: e.matmul(pu[:], wu_t[:, kt, :], hT[:, kt, nb * 512:(nb + 1) * 512],
                                                                                                   start=(kt == 0), stop=(kt == 7)),
                                         reads=[wu_b, hbs[nb]], writes=[pub])
                                sgs = sg[nb % 2]
                                sgb = buf("sg%d" % (nb % 2))
                                k.op("act", act(sgs[:], pg[:], AF.Silu), reads=[pgb], writes=[sgb])
                                k.op("dve", lambda e, sgs=sgs, pu=pu, ft=ft, nb=nb: e.tensor_tensor(out=hid[:, ft, nb * 512:(nb + 1) * 512], in0=sgs[:], in1=pu[:], op=ALU.mult),
                                     reads=[sgb, pub], writes=[hidb])
                        hall = [buf("hid%d" % ft) for ft in range(NFT)]
                        for nt in range(8):
                            s = wdctr[0] % 2
                            wdctr[0] += 1
                            wdb = buf("wd%d" % s)
                            with nc.allow_non_contiguous_dma(reason="512B weight rows"):
                                k.dma("pool", st_wd[s], wd[s][:], wd_d[:, nt * 128:(nt + 1) * 128].rearrange("(ft p) n -> p ft n", p=128),
                                      writes=[wdb])
                            for nb in range(4):
                                pd, pdb = pb[4 + nb % 2], buf("pb%d" % (4 + nb % 2))
                                for ft in range(NFT):
                                    k.op("pe", lambda e, ft=ft, pd=pd, s=s, nb=nb: e.matmul(pd[:], wd[s][:, ft, :], hid[:, ft, nb * 512:(nb + 1) * 512],
                                                                                             start=(ft == 0), stop=(ft == NFT - 1)),
                                         reads=[wdb] + (hall if ft == 0 else []), writes=[pdb])
                                resid_update(nt, tp * 4 + nb, pd, pdb, 0.5)
                            if nt >= 4:
                                nbp = nt - 4
                                if tp + 1 < NPASS:
                                    norm_block(gi_, (tp + 1) * 4 + nbp, nbp * 512, hbs[nbp])
                                    pre_done.add((gi_, (tp + 1) * 4 + nbp))
                                elif next_gi is not None:
                                    norm_block(next_gi, nbp, nbp * 512, hbs[nbp])
                                    pre_done.add((next_gi, nbp))


                qT = bigA[:, 0:4 * SEQ].rearrange("p (j n) -> p j n", j=4)
                KK = bigA[:, 4 * SEQ:6 * SEQ].rearrange("p (j n) -> p j n", j=2)
                vtok = bigA[:, 6 * SEQ:7 * SEQ].rearrange("p (t c) -> p t c", c=128)
                uT = bigA[:, 7 * SEQ:11 * SEQ].rearrange("p (j n) -> p j n", j=4)
                mixedT = bigB[:].rearrange("p (j n) -> p j n", j=4)

                def mixer(l):
                    k.barrier()
                    gi_ = gidx[("mix_norm", l)]
                    win = w["w_in"][l]
                    qb, kb_, vb, ub = buf("qT"), buf("KK"), buf("vtok"), buf("uT")
                    for tp in range(NPASS):
                        hbs = [buf("hT%d" % nb) for nb in range(4)]
                        for nb in range(4):
                            do_norm(gi_, tp * 4 + nb, nb * 512, hbs[nb])
                        c0 = tp * TT
                        evi = 0
                        for kind, j, col in ([("q", j, 128 * j) for j in range(4)] + [("u", j, 768 + 128 * j) for j in range(4)]):
                            wt, wb = load_w(win[:, col:col + 128])
                            dst, db = (qT, qb) if kind == "q" else (uT, ub)
                            for nb in range(4):
                                bank, bb = pb[nb % 2], buf("pb%d" % (nb % 2))
                                for kt in range(8):
                                    k.op("pe", lambda e: e.matmul(bank[:], wt[:, kt, :], hT[:, kt, nb * 512:(nb + 1) * 512], start=(kt == 0), stop=(kt == 7)),
                                         reads=[wb, hbs[nb]], writes=[bb])
                                if kind == "u":
                                    cb0 = (c0 + nb * 512) // T1
                                    o_ap = uT[:, j, :].rearrange("p (i c) -> p i c", c=NCH)[:, :, cb0:cb0 + 16]
                                    i_ap = bank[:].rearrange("p (c i) -> p i c", i=T1)
                                else:
                                    o_ap = dst[:, j, c0 + nb * 512:c0 + (nb + 1) * 512]
                                    i_ap = bank[:]
                                if evi % 2 == 0:
                                    k.op("act", act(o_ap, i_ap, AF.Copy), reads=[bb], writes=[db])
                                else:
                                    k.op("dve", lambda e: e.tensor_copy(out=o_ap, in_=i_ap), reads=[bb], writes=[db])
                                evi += 1
                        for kv in range(2):
                            s_ = wctr[0] % 6
                            wctr[0] += 1
                            wb = buf("wgu%d" % s_)
                            wt = wgu[s_]
                            for half in range(2):
                                k.dma("pool", st_w[s_], wt[:, :, half * 64:(half + 1) * 64],
                                      win[:, 512 + kv * 64:512 + (kv + 1) * 64].rearrange("(kt p) f -> p kt f", p=128), writes=[wb])
                            for nb in range(4):
                                bank, bb = pb[nb % 2], buf("pb%d" % (nb % 2))
                                for kt in range(8):
                                    k.op("pe", lambda e: e.matmul(bank[:], wt[:, kt, :], hT[:, kt, nb * 512:(nb + 1) * 512], start=(kt == 0), stop=(kt == 7)),
                                         reads=[wb, hbs[nb]], writes=[bb])
                                o_ap = KK[:, kv, c0 + nb * 512:c0 + (nb + 1) * 512]
                                k.op("act", act(o_ap, bank[:], AF.Copy), reads=[bb], writes=[kb_])
                        wt, wb = load_w(win[:, 640:768])
                        for t4 in range(TT // 512):
                            bank, bb = pb[2 + t4 % 2], buf("pb%d" % (2 + t4 % 2))
                            for ti in range(4):
                                tcol = t4 * 512 + ti * 128
                                for kt in range(8):
                                    k.op("pe", lambda e: e.matmul(bank[:, ti * 128:(ti + 1) * 128], hT[:, kt, tcol:tcol + 128], wt[:, kt, :], start=(kt == 0), stop=(kt == 7)),
                                         reads=[wb, hbs[t4]], writes=[bb])
                            tt0 = (c0 + t4 * 512) // 128
                            k.op("dve", lambda e: e.tensor_copy(out=vtok[:, tt0:tt0 + 4, :], in_=bank[:].rearrange("p (t c) -> p t c", c=128)),
                                 reads=[bb], writes=[vb])

                    k.barrier()
                    if not do_attn:
                        return
                    k.dma("sp", st_misc, biasm, biasm_d, writes=[buf("biasm")])
                    mb = buf("mixedT")
                    NBLK = SEQ // 128

                    def geom(n):
                        kb0 = max(n - 1, 0)
                        kb1 = min(n + 1, NBLK - 1)
                        nk = (kb1 - kb0 + 1) * 128
                        bc0 = (kb0 - (n - 1)) * 128
                        return kb0, nk, bc0

                    def st_scores(i):
                        n, h = divmod(i, NH)
                        kb0, nk, bc0 = geom(n)
                        hp = (h % 2) * 64
                        rsum, rsb = rowsum[n % 2], buf("rowsum%d" % (n % 2))
                        sbank, sbb = pb[h % 2], buf("pb%d" % (h % 2))
                        k.op("pe", lambda e: e.matmul(sbank[:, 0:nk], qT[hp:hp + 64, h // 2, n * 128:(n + 1) * 128],
                                                      KK[hp:hp + 64, h // 4, kb0 * 128:kb0 * 128 + nk], start=True, stop=True),
                             reads=[qb, kb_], writes=[sbb])
                        scs, scb = sc[h % 2], buf("sc%d" % (h % 2))
                        k.op("dve", lambda e: e.scalar_tensor_tensor(out=scs[:, 0:nk], in0=sbank[:, 0:nk], scalar=0.125,
                                                                     in1=biasm[:, h, bc0:bc0 + nk], op0=ALU.mult, op1=ALU.add),
                             reads=[sbb, buf("biasm")], writes=[scb])
                        sts, stb = stat[h % 2], buf("stat%d" % (h % 2))
                        k.op("dve", lambda e: e.tensor_reduce(out=sts[:, 0:1], in_=scs[:, 0:nk], op=ALU.max, axis=AX.X),
                             reads=[scb], writes=[stb])
                        k.op("dve", lambda e: e.tensor_scalar(out=sts[:, 1:2], in0=sts[:, 0:1], scalar1=sinkb[:, l * NH + h:l * NH + h + 1],
                                                              scalar2=-1.0, op0=ALU.max, op1=ALU.mult),
                             reads=[stb, buf("sinkb")], writes=[stb])
                        pes, peb = pexp[h % 2], buf("pexp%d" % (h % 2))
                        k.op("act", lambda e: e.activation(out=pes[:, 0:nk], in_=scs[:, 0:nk], func=AF.Exp, bias=sts[:, 1:2], scale=1.0,
                                                           accum_out=rsum[:, 0, h:h + 1]),
                             reads=[scb, stb], writes=[peb, rsb])
                        k.op("act", lambda e: e.activation(out=rsum[:, 1, h:h + 1], in_=sinkb[:, l * NH + h:l * NH + h + 1], func=AF.Exp,
                                                           bias=sts[:, 1:2], scale=1.0),
                             reads=[stb, buf("sinkb")], writes=[rsb])

                    def st_transpose(i):
                        n, h = divmod(i, NH)
                        kb0, nk, bc0 = geom(n)
                        pes, peb = pexp[h % 2], buf("pexp%d" % (h % 2))
                        tbank, tbb = pb[2 + h % 2], buf("pb%d" % (2 + h % 2))
                        tview = tbank[:].bitcast(BF16)
                        for kb in range(nk // 128):
                            k.op("pe", lambda e: e.transpose(tview[:, kb * 128:(kb + 1) * 128], pes[:, kb * 128:(kb + 1) * 128], ident_b[:]),
                                 reads=[peb, buf("identb")], writes=[tbb])
                        pts, ptb = pT[h % 2], buf("pT%d" % (h % 2))
                        if h % 2 == 0:
                            k.op("act", act(pts[:, 0:nk], tview[:, 0:nk], AF.Copy), reads=[tbb], writes=[ptb])
                        else:
                            k.op("dve", lambda e: e.tensor_copy(out=pts[:, 0:nk], in_=tview[:, 0:nk]), reads=[tbb], writes=[ptb])

                    def st_pv(i):
                        n, h = divmod(i, NH)
                        kb0, nk, bc0 = geom(n)
                        obank, obb = pb[4 + n % 2], buf("pb%d" % (4 + n % 2))
                        rsum, rsb = rowsum[n % 2], buf("rowsum%d" % (n % 2))
                        pts, ptb = pT[h % 2], buf("pT%d" % (h % 2))
                        kvh = h // 4
                        for kb in range(nk // 128):
                            k.op("pe", lambda e: e.matmul(obank[:, h * 64:(h + 1) * 64], pts[:, kb * 128:(kb + 1) * 128],
                                                          vtok[:, kb0 + kb, kvh * 64:(kvh + 1) * 64], start=(kb == 0), stop=(kb == nk // 128 - 1)),
                                 reads=[ptb, vb], writes=[obb])
                        if h != NH - 1:
                            return
                        k.op("dve", lambda e: e.tensor_tensor(out=rsum[:, 0, :], in0=rsum[:, 0, :], in1=rsum[:, 1, :], op=ALU.add), reads=[rsb], writes=[rsb])
                        k.op("dve", lambda e: e.reciprocal(out=rsum[:, 0, :], in_=rsum[:, 0, :]), reads=[rsb], writes=[rsb])
                        at, atb = atok[n % 2], buf("atok%d" % (n % 2))
                        k.op("dve", lambda e: e.tensor_tensor(out=at[:].rearrange("p (h d) -> p h d", d=64), in0=obank[:].rearrange("p (h d) -> p h d", d=64),
                                                              in1=rsum[:, 0, :].unsqueeze(2).broadcast_to([128, NH, 64]), op=ALU.mult),
                             reads=[obb, rsb], writes=[atb])
                        ns, nsb = nstat[n % 2], buf("nstat%d" % (n % 2))
                        junk, jb = sq[0], buf("sq0")
                        k.op("act", lambda e: e.activation(out=junk[:, 0:512], in_=at[:, 0:512], func=AF.Square, accum_out=ns[:, 2:3]),
                             reads=[atb], writes=[jb, nsb])
                        k.op("act", act(ns[:, 3:4], ns[:, 2:3], AF.Sqrt, scale=1.0 / 512, bias=eps_t[:, 0:1]), reads=[nsb, buf("eps")], writes=[nsb])
                        k.op("dve", lambda e: e.reciprocal(out=ns[:, 3:4], in_=ns[:, 3:4]), reads=[nsb], writes=[nsb])
                        k.op("dve", lambda e: e.tensor_scalar(out=at[:], in0=at[:], scalar1=ns[:, 3:4], scalar2=None, op0=ALU.mult), reads=[atb, nsb], writes=[atb])
                        trb, trbb = pb[6 + n % 2], buf("pb%d" % (6 + n % 2))
                        for c in range(4):
                            k.op("pe", lambda e: e.transpose(trb[:, c * 128:(c + 1) * 128], at[:, c * 128:(c + 1) * 128], ident_f[:]),
                                 reads=[atb, buf("identf")], writes=[trbb])
                        for c in range(4):
                            k.op("dve", lambda e: e.tensor_scalar(out=mixedT[:, c, n * 128:(n + 1) * 128], in0=trb[:, c * 128:(c + 1) * 128],
                                                                  scalar1=gains2[:, l, 0, c:c + 1], scalar2=None, op0=ALU.mult),
                                 reads=[trbb, buf("gains2")], writes=[mb])

                    NI = NBLK * NH
                    for i in range(NI + 2):
                        if i < NI:
                            st_scores(i)
                        if 0 <= i - 1 < NI:
                            st_transpose(i - 1)
                        if 0 <= i - 2 < NI:
                            st_pv(i - 2)
                    out_proj(l, 0, mb)
                    k.barrier()

                def out_proj(l, half, mb):
                    wo = w["w_out"][l]
                    for nt in range(8):
                        s_ = wctr[0] % 6
                        wctr[0] += 1
                        wb = buf("wgu%d" % s_)
                        wt = wgu[s_]
                        k.dma("pool", st_w[s_], wt[:, 0:4, :], wo[half * 512:(half + 1) * 512, nt * 128:(nt + 1) * 128].rearrange("(kt p) f -> p kt f", p=128),
                              writes=[wb])
                        for blk in range(8):
                            bank, bb = pb[blk % 2], buf("pb%d" % (blk % 2))
                            for kt in range(4):
                                k.op("pe", lambda e: e.matmul(bank[:], wt[:, kt, :], mixedT[:, kt, blk * 512:(blk + 1) * 512], start=(kt == 0), stop=(kt == 3)),
                                     reads=[wb, mb], writes=[bb])
                            resid_update(nt, blk, bank, bb, 1.0)

                PA = xn[0][:].rearrange("p a b -> p (a b)")
                PBt = xn[1][:].rearrange("p a b -> p (a b)")

                def pa(off, n):
                    return PA[:, off:off + n]
                are, aim, dtv, rho, tht, den, arn, ain = [pa(32 * i, 32) for i in range(8)]
                L32r, L32i, L31r, L31i, L31in = [pa(256 + 32 * i, 32) for i in range(5)]
                Bre = pa(448, 512).rearrange("p (g h) -> p g h", h=16)
                Bim = pa(960, 512).rearrange("p (g h) -> p g h", h=16)
                Cre = pa(1472, 512).rearrange("p (g h) -> p g h", h=16)
                Cim = pa(1984, 512).rearrange("p (g h) -> p g h", h=16)
                Dv = pa(2496, 32)
                tmpA = pa(2528, 256)
                ldt = pa(2784, 32)
                XS = bigA[:, 0:4 * SEQ].bitcast(F32).rearrange("p (c g t) -> p c g t", g=G, t=2)
                SelM = bigA[:, 4 * SEQ:6 * SEQ].rearrange("p (a b m) -> p a b m", a=8, b=8)
                Zt = bigA[:, 6 * SEQ:7 * SEQ].rearrange("p (g m c) -> p g m c", g=8, m=4)
                WT = [[bigB[:, sl * 4096 + d_ * 2048:sl * 4096 + (d_ + 1) * 2048].rearrange("p (k j) -> p k j", k=4) for d_ in range(2)] for sl in range(2)]
                WYF = [bigB[:, 8192 + sl * 2112:8192 + sl * 2112 + 1056].rearrange("p (t j) -> p t j", t=2) for sl in range(3)]
                WYB = [bigB[:, 8192 + sl * 2112 + 1056:8192 + (sl + 1) * 2112].rearrange("p (t j) -> p t j", t=2) for sl in range(3)]
                WX = [scr[:, sl * 1024:(sl + 1) * 1024].rearrange("p (k t m) -> p k t m", k=4, t=2) for sl in range(2)]
                BX = [[scr[:, 2048 + sl * 1024 + pt * 512:2048 + sl * 1024 + (pt + 1) * 512] for pt in range(2)] for sl in range(2)]
                Ug = [scr[:, 4096 + i * 512:4096 + (i + 1) * 512].rearrange("p (k c) -> p k c", k=4) for i in range(2)]
                SG = [scr[:, 5120 + sl * 256:5120 + (sl + 1) * 256].rearrange("p (c t) -> p c t", t=2) for sl in range(2)]
                DDt = [scr[:, 5632 + sl * 128:5632 + (sl + 1) * 128] for sl in range(2)]
                BB = [[ssm_scr[:, sl * 1024 + pt * 512:sl * 1024 + (pt + 1) * 512] for pt in range(2)] for sl in range(2)]
                POL = [xr[2 * sl][:, 0:396].rearrange("p (t g k) -> p t g k", t=3, g=4) for sl in range(2)]
                POK = [xr[2 * sl + 1][:, 0:256].rearrange("p (t g k) -> p t g k", t=2, g=4) for sl in range(2)]
                P1, P2 = [ptmp[:, i, :] for i in range(2)]
                P3, P4 = pa(2816, 528), pa(3344, 528)
                MASKF, MASKB = cst[:, 0, :], cst[:, 1, :]
                RM = rmt[:, 0:8]
                TWO_PI = 2.0 * math.pi
                NB4 = 4
                def pbt(i, n=NB4 * 99):
                    return PBt[:, i * 396:i * 396 + n]
                T0, T1f, T2, T3, T4, TG = [pbt(i).rearrange("p (g k) -> p g k", g=NB4) for i in range(6)]
                T1i = PBt[:, 6 * 396:7 * 396].bitcast(I32).rearrange("p (g k) -> p g k", g=NB4)
                KAP = [PBt[:, 2772 + i * 128:2772 + (i + 1) * 128].rearrange("p (g k) -> p g k", g=NB4) for i in range(6)]
                D15 = PBt[:, 0:1920].rearrange("p (d m) -> p d m", d=15)

                def ssm_consts():
                    k.barrier()
                    hb_ = buf("hm")
                    k.op("pool", lambda e: e.memset(hm[:], 0.0), writes=[hb_])
                    k.op("pool", lambda e: e.memset(hm[0:64, 0:1], 1.0), reads=[hb_], writes=[hb_])
                    k.op("pool", lambda e: e.memset(hm[64:128, 1:2], 1.0), reads=[hb_], writes=[hb_])
                    pbf = buf("kall")
                    def io(ap, pat, base):
                        k.op("pool", lambda e: e.iota(out=ap, pattern=pat, base=base, channel_multiplier=0), writes=[pbf])
                    io(kalli[0:64, 0:32], [[-1, 32]], 0)
                    io(kalli[64:128, 0:32], [[1, 32]], -31)
                    io(kalli[0:64, 32:64], [[-1, 32]], 1)
                    io(kalli[64:128, 32:64], [[1, 32]], -30)
                    io(kalli[0:64, 64:97], [[1, 33]], 0)
                    io(kalli[64:128, 64:97], [[-1, 33]], 32)
                    io(kalli[:, 97:99], [[1, 2]], 31)
                    k.op("dve", lambda e: e.tensor_copy(out=kall[:], in_=kalli[:]), reads=[pbf], writes=[pbf])
                    k.op("dve", lambda e: e.tensor_copy(out=kall1[:], in_=kalli[:]), reads=[pbf], writes=[pbf])
                    k.op("dve", lambda e: e.tensor_scalar(out=kall1[:, 0:64], in0=kall1[:, 0:64], scalar1=31.0, scalar2=None, op0=ALU.add), reads=[pbf], writes=[pbf])
                    rb_ = buf("rmt")
                    k.op("pool", lambda e: e.iota(out=rmi[:, 8:9], pattern=[[0, 1]], base=0, channel_multiplier=1), writes=[rb_])
                    k.op("dve", lambda e: e.tensor_single_scalar(out=rmi[:, 8:9], in_=rmi[:, 8:9], scalar=4, op=ALU.arith_shift_right), reads=[rb_], writes=[rb_])
                    k.op("pool", lambda e: e.iota(out=rmi[:, 16:24], pattern=[[1, 8]], base=0, channel_multiplier=0), reads=[rb_], writes=[rb_])
                    k.op("dve", lambda e: e.tensor_copy(out=rmt[:, 8:9], in_=rmi[:, 8:9]), reads=[rb_], writes=[rb_])
                    k.op("dve", lambda e: e.tensor_copy(out=rmt[:, 16:24], in_=rmi[:, 16:24]), reads=[rb_], writes=[rb_])
                    k.op("dve", lambda e: e.tensor_scalar(out=rmt[:, 0:8], in0=rmt[:, 16:24], scalar1=rmt[:, 8:9], scalar2=None, op0=ALU.is_equal), reads=[rb_], writes=[rb_])
                    cb = buf("cst")
                    ci = PBt[:, 0:128].bitcast(I32)
                    k.op("pool", lambda e: e.iota(out=ci, pattern=[[1, 8], [0, 16]], base=0, channel_multiplier=0), writes=[buf("PBt")])
                    k.op("dve", lambda e: e.tensor_copy(out=cst[:, 1, :], in_=ci), reads=[buf("PBt")], writes=[cb])
                    k.op("dve", lambda e: e.tensor_scalar(out=cst[:, 0, :], in0=cst[:, 1, :], scalar1=rmt[:, 8:9], scalar2=None, op0=ALU.is_ge), reads=[cb, rb_], writes=[cb])
                    k.op("dve", lambda e: e.tensor_scalar(out=cst[:, 1, :], in0=cst[:, 1, :], scalar1=rmt[:, 8:9], scalar2=None, op0=ALU.is_le), reads=[cb, rb_], writes=[cb])
                    for l_ in range(DEPTH):
                        k.dma("sp", st_misc, bglu[:, l_, :], w["ssm_b_glu"][l_].rearrange("(kt p) -> p kt", p=128), writes=[buf("bglu")],
                              allow_slow_non_contiguous=True)

                def power_batch(bi, pab, phase1):
                    g0 = bi * NB4
                    sl = bi % 2
                    tb = buf("PBt")
                    pob = buf("PO%d" % sl)
                    kb3 = (kall1 if phase1 else kall)[:, :].unsqueeze(1).broadcast_to([128, NB4, 99])
                    def bc(pg):
                        return pg[:, g0:g0 + NB4].unsqueeze(2).broadcast_to([128, NB4, 99])
                    V = lambda fn: k.op("dve", fn, reads=[tb, pab, buf("kall")], writes=[tb])
                    A_ = lambda fn: k.op("act", fn, reads=[tb, pab], writes=[tb])
                    V(lambda e: e.scalar_tensor_tensor(out=T0, in0=bc(tht), scalar=1.0 / TWO_PI, in1=kb3, op0=ALU.mult, op1=ALU.mult))
                    V(lambda e: e.tensor_copy(out=T1i, in_=T0))
                    V(lambda e: e.tensor_copy(out=T1f, in_=T1i))
                    V(lambda e: e.tensor_tensor(out=T0, in0=T0, in1=T1f, op=ALU.subtract))
                    V(lambda e: e.tensor_scalar(out=T0, in0=T0, scalar1=0.49999, scalar2=-0.49999, op0=ALU.min, op1=ALU.max))
                    A_(lambda e: e.activation(out=T3, in_=T0, func=AF.Sin, scale=TWO_PI))
                    V(lambda e: e.tensor_scalar(out=TG, in0=T0, scalar1=0.25, scalar2=None, op0=ALU.is_gt))
                    V(lambda e: e.scalar_tensor_tensor(out=T0, in0=T0, scalar=0.25, in1=TG, op0=ALU.add, op1=ALU.subtract))
                    V(lambda e: e.tensor_scalar(out=T0, in0=T0, scalar1=0.49999, scalar2=-0.49999, op0=ALU.min, op1=ALU.max))
                    A_(lambda e: e.activation(out=T4, in_=T0, func=AF.Sin, scale=TWO_PI))
                    V(lambda e: e.tensor_tensor(out=T2, in0=bc(rho), in1=kb3, op=ALU.mult))
                    A_(lambda e: e.activation(out=T2, in_=T2, func=AF.Exp))
                    V(lambda e: e.tensor_tensor(out=T3, in0=T3, in1=T2, op=ALU.mult))
                    V(lambda e: e.tensor_tensor(out=T4, in0=T4, in1=T2, op=ALU.mult))
                    Nr, Ni, kr_, ki_, t1, t2 = KAP
                    V(lambda e: e.tensor_tensor(out=Nr, in0=T4[:, :, 32:64], in1=T4[:, :, 0:32], op=ALU.subtract))
                    V(lambda e: e.tensor_tensor(out=Ni, in0=T3[:, :, 32:64], in1=T3[:, :, 0:32], op=ALU.subtract))
                    def bc32(pg):
                        return pg[:, g0:g0 + NB4].unsqueeze(2).broadcast_to([128, NB4, 32])
                    VO = lambda fn: k.op("dve", fn, reads=[tb, pab], writes=[pob])
                    V(lambda e: e.tensor_tensor(out=t1, in0=Nr, in1=bc32(arn), op=ALU.mult))
                    V(lambda e: e.tensor_tensor(out=t2, in0=Ni, in1=bc32(ain), op=ALU.mult))
                    VO(lambda e: e.tensor_tensor(out=POK[sl][:, 0], in0=t1, in1=t2, op=ALU.add))
                    V(lambda e: e.tensor_tensor(out=t1, in0=Ni, in1=bc32(arn), op=ALU.mult))
                    V(lambda e: e.tensor_tensor(out=t2, in0=Nr, in1=bc32(ain), op=ALU.mult))
                    VO(lambda e: e.tensor_tensor(out=POK[sl][:, 1], in0=t1, in1=t2, op=ALU.subtract))
                    if phase1:
                        plb = buf("PAL")
                        k.op("dve", lambda e: e.tensor_copy(out=L32r[:, g0:g0 + NB4], in_=T4[:, :, 98]), reads=[tb], writes=[plb])
                        k.op("dve", lambda e: e.tensor_copy(out=L32i[:, g0:g0 + NB4], in_=T3[:, :, 98]), reads=[tb], writes=[plb])
                    else:
                        VO(lambda e: e.tensor_copy(out=POL[sl][:, 0], in_=T4[:, :, 64:97]))
                        VO(lambda e: e.tensor_copy(out=POL[sl][:, 1], in_=T3[:, :, 64:97]))
                        VO(lambda e: e.tensor_scalar(out=POL[sl][:, 2], in0=T3[:, :, 64:97], scalar1=-1.0, scalar2=None, op0=ALU.mult))

                def build_BB(g, pab, dst=None, dname="BB"):
                    sl, bi, gi = g % 2, g // NB4, g % NB4
                    if dst is None:
                        dst = BB
                    pob, bbb, p12 = buf("PO%d" % (bi % 2)), buf("%s%d" % (dname, sl)), buf("P12")
                    kr_, ki_ = POK[bi % 2][:, 0], POK[bi % 2][:, 1]
                    def kb_(t):
                        return t[:, gi, :].unsqueeze(2).broadcast_to([128, 32, 16])
                    def bb_(t):
                        return t[:, g, :].unsqueeze(1).broadcast_to([128, 32, 16])
                    p1v = P1[:, 0:512].rearrange("p (i h) -> p i h", h=16)
                    p2v = P2[:, 0:512].rearrange("p (i h) -> p i h", h=16)
                    V = lambda fn, w_: k.op("dve", fn, reads=[pob, pab, p12], writes=w_)
                    V(lambda e: e.tensor_tensor(out=p1v, in0=kb_(kr_), in1=bb_(Bre), op=ALU.mult), [p12])
                    V(lambda e: e.tensor_tensor(out=p2v, in0=kb_(ki_), in1=bb_(Bim), op=ALU.mult), [p12])
                    V(lambda e: e.tensor_tensor(out=dst[sl][0], in0=P1[:, 0:512], in1=P2[:, 0:512], op=ALU.subtract), [bbb])
                    V(lambda e: e.tensor_tensor(out=p1v, in0=kb_(kr_), in1=bb_(Bim), op=ALU.mult), [p12])
                    V(lambda e: e.tensor_tensor(out=p2v, in0=kb_(ki_), in1=bb_(Bre), op=ALU.mult), [p12])
                    V(lambda e: e.tensor_tensor(out=dst[sl][1], in0=P1[:, 0:512], in1=P2[:, 0:512], op=ALU.add), [bbb])
                    return bbb

                def build_U(g, slot):
                    kc, gl = g // 8, g % 8
                    ub_, ubank, ubb = buf("U%d" % slot), pb[slot], buf("pb%d" % slot)
                    for kti in range(4):
                        for i8 in range(8):
                            off = 8 * kti + i8
                            rhs = uT[:, kc, off * NCH:(off + 1) * NCH]
                            k.op("pe", lambda e: e.matmul(ubank[:, kti * 128:(kti + 1) * 128], SelM[:, gl, i8, :], rhs, start=(i8 == 0), stop=(i8 == 7)),
                                 reads=[buf("uT"), buf("SelM")], writes=[ubb])
                    k.op("act", act(Ug[slot], ubank[:].rearrange("p (k c) -> p k c", k=4), AF.Copy), reads=[ubb], writes=[ub_])
                    return ub_

                def ssm(l):
                    k.barrier()
                    pab = buf("PA")
                    for nm, dst_off in (("ssm_a_re", 0), ("ssm_a_im", 1)):
                        k.dma("sp", st_misc, tmpA[0:32, dst_off * 128:(dst_off + 1) * 128].rearrange("g (d p) -> g d p", d=2),
                              w[nm][l].rearrange("d g p -> g d p"), writes=[pab])
                    for d_ in range(2):
                        k.dma("sp", st_misc, ldt[64 * d_:64 * d_ + 64, :], w["ssm_log_dt"][l, d_].partition_broadcast(64), writes=[pab],
                              allow_slow_non_contiguous=True)
                        k.dma("sp", st_misc, Bre[64 * d_:64 * d_ + 64, :, :], w["ssm_b_re"][l, d_].rearrange("g p h -> p g h"), writes=[pab])
                        k.dma("sp", st_misc, Bim[64 * d_:64 * d_ + 64, :, :], w["ssm_b_im"][l, d_].rearrange("g p h -> p g h"), writes=[pab])
                    for i8 in range(8):
                        k.dma("sp", st_misc, Dv[16 * i8:16 * i8 + 16, :], w["ssm_d"][l].rearrange("(g h) -> h g", h=16), writes=[pab],
                              allow_slow_non_contiguous=True)
                    if SSM_STOP <= 0:
                        raise StopSSM()
                    tb = buf("PBt")
                    Cin = PBt[:, 0:512].rearrange("p (b m) -> p b m", b=4)
                    p6, p6b = pb[6], buf("pb6")
                    for nm, dstC in (("ssm_c_re", Cre), ("ssm_c_im", Cim)):
                        for d_ in range(2):
                            k.dma("sp", st_misc, Cin[:, :, 64 * d_:64 * d_ + 64], w[nm][l, d_].rearrange("(gb g8) h p -> (g8 h) gb p", g8=8), writes=[tb])
                        for gb in range(4):
                            k.op("pe", lambda e: e.transpose(p6[:, gb * 128:(gb + 1) * 128], Cin[:, gb, :], ident_f[:]), reads=[tb, buf("identf")], writes=[p6b])
                        k.op("dve", lambda e: e.tensor_copy(out=dstC.rearrange("p g h -> p (g h)"), in_=p6[:]), reads=[p6b], writes=[pab])
                    if SSM_STOP <= 1:
                        raise StopSSM()
                    for i_, dstA in ((0, are), (1, aim)):
                        k.op("pe", lambda e: e.transpose(p6[:, i_ * 32:(i_ + 1) * 32], tmpA[0:32, i_ * 128:(i_ + 1) * 128], ident_f[0:32, 0:32]),
                             reads=[pab, buf("identf")], writes=[p6b])
                    k.op("dve", lambda e: e.tensor_copy(out=are, in_=p6[:, 0:32]), reads=[p6b], writes=[pab])
                    k.op("dve", lambda e: e.tensor_copy(out=aim, in_=p6[:, 32:64]), reads=[p6b], writes=[pab])
                    if SSM_STOP <= 2:
                        raise StopSSM()
                    V = lambda fn: k.op("dve", fn, reads=[pab], writes=[pab])
                    k.op("act", act(dtv, ldt, AF.Exp), reads=[pab], writes=[pab])
                    V(lambda e: e.tensor_tensor(out=rho, in0=dtv, in1=are, op=ALU.mult))
                    V(lambda e: e.tensor_tensor(out=tht, in0=dtv, in1=aim, op=ALU.mult))
                    V(lambda e: e.tensor_tensor(out=den, in0=are, in1=are, op=ALU.mult))
                    V(lambda e: e.tensor_tensor(out=arn, in0=aim, in1=aim, op=ALU.mult))
                    V(lambda e: e.tensor_tensor(out=den, in0=den, in1=arn, op=ALU.add))
                    V(lambda e: e.reciprocal(out=den, in_=den))
                    V(lambda e: e.tensor_tensor(out=arn, in0=are, in1=den, op=ALU.mult))
                    V(lambda e: e.tensor_tensor(out=ain, in0=aim, in1=den, op=ALU.mult))
                    if SSM_STOP <= 3:
                        raise StopSSM()
                    selb = buf("SelM")
                    k.op("pool", lambda e: e.memset(D15, 0.0), reads=[tb], writes=[tb])
                    k.op("pool", lambda e: e.affine_select(out=D15, in_=D15, pattern=[[-16, 15], [1, 128]], base=112, channel_multiplier=-1,
                                                           compare_op=ALU.not_equal, fill=1.0), reads=[tb], writes=[tb])
                    for a_ in range(8):
                        for b_ in range(8):
                            k.op("dve", lambda e: e.tensor_scalar(out=SelM[:, a_, b_, :], in0=D15[:, b_ - a_ + 7, :], scalar1=RM[:, a_:a_ + 1], scalar2=None, op0=ALU.mult),
                                 reads=[tb, buf("rmt")], writes=[selb])
                    if SSM_STOP <= 4:
                        raise StopSSM()
                    xsb = buf("XS")
                    plb = buf("PAL")
                    p6b = None

                    def p1_A(g):
                        build_BB(g, pab, BX, "BX")

                    def p1_B(g):
                        sl = g % 2
                        bxb, wxb = buf("BX%d" % sl), buf("WX%d" % sl)
                        tb_, tbb_ = pb[6 + sl], buf("pb%d" % (6 + sl))
                        tv = tb_[:].bitcast(BF16)
                        for kt in range(4):
                            for part in range(2):
                                col = (kt * 2 + part) * 128
                                k.op("pe", lambda e: e.transpose(tv[:, col:col + 128], BX[sl][part][:, kt * 128:(kt + 1) * 128], ident_b[:]), reads=[bxb, buf("identb")], writes=[tbb_])
                        k.op("act", act(WX[sl].rearrange("p k t m -> p (k t m)"), tv, AF.Copy), reads=[tbb_], writes=[wxb])
                        ub_ = build_U(g, sl)
                        xbank, xbb = pb[2 + sl], buf("pb%d" % (2 + sl))
                        for part in range(2):
                            for kt in range(4):
                                k.op("pe", lambda e: e.matmul(xbank[:, part * 128:(part + 1) * 128], WX[sl][:, kt, part, :], Ug[sl][:, kt, :], start=(kt == 0), stop=(kt == 3)),
                                     reads=[wxb, ub_], writes=[xbb])
                        k.op("act", act(XS[:, :, g, :].rearrange("p c t -> p t c"), xbank[:, 0:256].rearrange("p (t c) -> p t c", t=2), AF.Copy),
                             reads=[xbb], writes=[xsb])

                    for step in range(G + 1):
                        if 0 <= step - 1 < G:
                            p1_B(step - 1)
                        if step < G:
                            if step % NB4 == 0:
                                power_batch(step // NB4, pab, True)
                            p1_A(step)
                    if SSM_STOP <= 6:
                        raise StopSSM()
                    LrB = sctmp[:, 2, :].rearrange("p (g t) -> p g t", t=2)
                    LiS = sctmp[:, 3, :].rearrange("p (g t) -> p g t", t=2)
                    scb = buf("sctmp")
                    k.op("dve", lambda e: e.tensor_copy(out=LrB, in_=L32r.unsqueeze(2).broadcast_to([128, G, 2])), reads=[plb], writes=[scb])
                    k.op("dve", lambda e: e.tensor_scalar(out=LiS[:, :, 0], in0=L32i, scalar1=-1.0, scalar2=None, op0=ALU.mult), reads=[plb], writes=[scb])
                    k.op("dve", lambda e: e.tensor_copy(out=LiS[:, :, 1], in_=L32i), reads=[plb], writes=[scb])
                    for s_ in range(1, NCH):
                        for d_ in range(2):
                            en_ = "dve" if d_ == 0 else "pool"
                            lo, hi = 64 * d_, 64 * d_ + 64
                            c = s_ if d_ == 0 else NCH - 1 - s_
                            cp = c - 1 if d_ == 0 else c + 1
                            prev = XS[lo:hi, cp, :, :]
                            cur = XS[lo:hi, c, :, :]
                            t1_ = sctmp[lo:hi, 0, :].rearrange("p (g t) -> p g t", t=2)
                            t2_ = sctmp[lo:hi, 1, :].rearrange("p (g t) -> p g t", t=2)
                            tb1, tb2 = buf("sct1_%d" % d_), buf("sct2_%d" % d_)
                            xh = buf("XS%d" % d_)
                            k.op(en_, lambda e: e.tensor_tensor(out=t1_, in0=prev, in1=LrB[lo:hi], op=ALU.mult), reads=[xh, xsb, scb], writes=[tb1])
                            k.op(en_, lambda e: e.tensor_tensor(out=t2_[:, :, 0], in0=prev[:, :, 1], in1=LiS[lo:hi, :, 0], op=ALU.mult), reads=[xh, xsb, scb], writes=[tb2])
                            k.op(en_, lambda e: e.tensor_tensor(out=t2_[:, :, 1], in0=prev[:, :, 0], in1=LiS[lo:hi, :, 1], op=ALU.mult), reads=[xh, xsb, scb], writes=[tb2])
                            k.op(en_, lambda e: e.tensor_tensor(out=t1_, in0=t1_, in1=t2_, op=ALU.add), reads=[tb1, tb2], writes=[tb1])
                            k.op(en_, lambda e: e.tensor_tensor(out=cur, in0=cur, in1=t1_, op=ALU.add), reads=[tb1, xh, xsb], writes=[xh])
                    xs_done = [buf("XS0"), buf("XS1"), xsb]
                    if SSM_STOP <= 7:
                        raise StopSSM()
                    zb = buf("Zt")
                    for s3_ in range(3):
                        k.op("pool", lambda e: e.memset(WYF[s3_][64:128, :, :], 0.0), writes=[buf("WYFB%d" % s3_)])
                        k.op("pool", lambda e: e.memset(WYB[s3_][0:64, :, :], 0.0), writes=[buf("WYFB%d" % s3_)])

                    def p2_A(g):
                        sl, bi, gi = g % 2, g // NB4, g % NB4
                        pob = buf("PO%d" % (bi % 2))
                        build_BB(g, pab)
                        ddb = buf("DD%d" % sl)
                        k.op("dve", lambda e: e.tensor_scalar(out=DDt[sl], in0=ident_f[:], scalar1=Dv[:, g:g + 1], scalar2=None, op0=ALU.mult), reads=[pab, buf("identf")], writes=[ddb])
                        s3 = g % 3
                        wyfb, p34 = buf("WYFB%d" % s3), buf("P34")
                        LPr, LPi, LPin = POL[bi % 2][:, 0], POL[bi % 2][:, 1], POL[bi % 2][:, 2]
                        def lp(t):
                            return t[:, gi, :].unsqueeze(2).broadcast_to([128, 33, 16])
                        def cb_(t):
                            return t[:, g, :].unsqueeze(1).broadcast_to([128, 33, 16])
                        p3v = P3.rearrange("p (t h) -> p t h", h=16)
                        p4v = P4.rearrange("p (t h) -> p t h", h=16)
                        Pl = lambda fn, w_: k.op("pool", fn, reads=[pob, pab, p34], writes=w_)
                        Pl(lambda e: e.tensor_tensor(out=p3v, in0=cb_(Cre), in1=lp(LPr), op=ALU.mult), [p34])
                        Pl(lambda e: e.tensor_tensor(out=p4v, in0=cb_(Cim), in1=lp(LPi), op=ALU.mult), [p34])
                        Pl(lambda e: e.tensor_tensor(out=WYF[s3][0:64, 0, :], in0=P3[0:64], in1=P4[0:64], op=ALU.subtract), [wyfb])
                        Pl(lambda e: e.tensor_tensor(out=WYB[s3][64:128, 0, :], in0=P3[64:128], in1=P4[64:128], op=ALU.subtract), [wyfb])
                        Pl(lambda e: e.tensor_tensor(out=p3v, in0=cb_(Cre), in1=lp(LPin), op=ALU.mult), [p34])
                        Pl(lambda e: e.tensor_tensor(out=p4v, in0=cb_(Cim), in1=lp(LPr), op=ALU.mult), [p34])
                        Pl(lambda e: e.tensor_tensor(out=WYF[s3][0:64, 1, :], in0=P3[0:64], in1=P4[0:64], op=ALU.subtract), [wyfb])
                        Pl(lambda e: e.tensor_tensor(out=WYB[s3][64:128, 1, :], in0=P3[64:128], in1=P4[64:128], op=ALU.subtract), [wyfb])

                    def p2_B(g):
                        sl = g % 2
                        s3 = g % 3
                        bbb, wyfb = buf("BB%d" % sl), buf("WYFB%d" % s3)
                        wtb = [buf("WT%d_%d" % (sl, d_)) for d_ in range(2)]
                        for d_ in range(2):
                            toff = 0 if d_ == 0 else 16
                            wyp = WYF[s3] if d_ == 0 else WYB[s3]
                            for kt in range(4):
                                jlo, jhi = (kt * 128, 512) if d_ == 0 else (0, (kt + 1) * 128)
                                bi_ = 4 + 2 * d_ + kt % 2
                                wbank, wbb = pb[bi_], buf("pb%d" % bi_)
                                k.op("pe", lambda e: e.matmul(wbank[:, jlo:jhi], BB[sl][0][:, kt * 128:(kt + 1) * 128], wyp[:, 0, toff + jlo:toff + jhi], start=True, stop=False),
                                     reads=[bbb, wyfb], writes=[wbb])
                                k.op("pe", lambda e: e.matmul(wbank[:, jlo:jhi], BB[sl][1][:, kt * 128:(kt + 1) * 128], wyp[:, 1, toff + jlo:toff + jhi], start=False, stop=True),
                                     reads=[bbb, wyfb], writes=[wbb])
                                dlo = kt * 128
                                olo, ohi = (dlo + 128, 512) if d_ == 0 else (0, dlo)
                                if ohi > olo:
                                    k.op("act", act(WT[sl][d_][:, kt, olo:ohi], wbank[:, olo:ohi], AF.Copy), reads=[wbb], writes=[wtb[d_]])
                                msk = MASKF if d_ == 0 else MASKB
                                k.op("dve", lambda e: e.tensor_tensor(out=WT[sl][d_][:, kt, dlo:dlo + 128], in0=wbank[:, dlo:dlo + 128], in1=msk, op=ALU.mult), reads=[wbb, buf("cst")], writes=[wtb[d_]])
                        sgb = buf("SG%d" % sl)
                        k.op("act", act(SG[sl], XS[:, :, g, :], AF.Copy), reads=xs_done, writes=[sgb])
                        build_U(g, sl)

                    def p2_C(g):
                        sl = g % 2
                        kc, gl = g // 8, g % 8
                        s3 = g % 3
                        wyfb, sgb, ddb, ub_ = buf("WYFB%d" % s3), buf("SG%d" % sl), buf("DD%d" % sl), buf("U%d" % sl)
                        wtb = [buf("WT%d_%d" % (sl, d_)) for d_ in range(2)]
                        ybank, ybb = pb[2 + sl], buf("pb%d" % (2 + sl))
                        for mt in range(4):
                            mlo = mt * 128
                            first = True
                            for kt in range(0, mt + 1):
                                k.op("pe", lambda e: e.matmul(ybank[:, mlo:mlo + 128], WT[sl][0][:, kt, mlo:mlo + 128], Ug[sl][:, kt, :], start=first, stop=False),
                                     reads=[wtb[0], ub_], writes=[ybb])
                                first = False
                            for kt in range(mt, 4):
                                k.op("pe", lambda e: e.matmul(ybank[:, mlo:mlo + 128], WT[sl][1][:, kt, mlo:mlo + 128], Ug[sl][:, kt, :], start=False, stop=False),
                                     reads=[wtb[1], ub_], writes=[ybb])
                            k.op("pe", lambda e: e.matmul(ybank[:, mlo:mlo + 128], DDt[sl], Ug[sl][:, mt, :], start=False, stop=False), reads=[ddb, ub_], writes=[ybb])
                            for part in range(2):
                                k.op("pe", lambda e: e.matmul(ybank[:, mlo + 1:mlo + 128], WYF[s3][:, part, 16 + mlo:16 + mlo + 128], SG[sl][:, 0:NCH - 1, part], start=False, stop=False),
                                     reads=[wyfb, sgb], writes=[ybb])
                            for part in range(2):
                                k.op("pe", lambda e: e.matmul(ybank[:, mlo:mlo + 127], WYB[s3][:, part, mlo:mlo + 128], SG[sl][:, 1:NCH, part], start=False, stop=(part == 1)),
                                     reads=[wyfb, sgb], writes=[ybb])
                        k.op("act", act(Zt[:, gl, :, :].rearrange("p m c -> p (m c)"), ybank[:], AF.Gelu_apprx_tanh), reads=[ybb], writes=[zb])
                        if gl == 7:
                            ubuf = buf("uT")
                            for j4 in range(8):
                                ibank, ibb = pb[j4 % 2], buf("pb%d" % (j4 % 2))
                                for jj in range(4):
                                    j = j4 * 4 + jj
                                    for gl2 in range(8):
                                        k.op("pe", lambda e: e.matmul(ibank[:, jj * 128:(jj + 1) * 128], SelM[:, j % 8, gl2, :], Zt[:, gl2, j // 8, :], start=(gl2 == 0), stop=(gl2 == 7)),
                                             reads=[zb, buf("SelM")], writes=[ibb])
                                dst = uT[:, kc, :].rearrange("p (c j) -> p j c", j=T1)[:, j4 * 4:(j4 + 1) * 4, :]
                                if j4 % 2 == 0:
                                    k.op("act", act(dst, ibank[:].rearrange("p (j c) -> p j c", j=4), AF.Copy), reads=[ibb], writes=[ubuf])
                                else:
                                    k.op("dve", lambda e: e.tensor_copy(out=dst, in_=ibank[:].rearrange("p (j c) -> p j c", j=4)), reads=[ibb], writes=[ubuf])

                    for step in range(G + 2):
                        if 0 <= step - 1 < G:
                            p2_B(step - 1)
                        if 0 <= step - 2 < G:
                            p2_C(step - 2)
                        if step < G:
                            if step % NB4 == 0:
                                power_batch(step // NB4, pab, False)
                            p2_A(step)
                    if SSM_STOP <= 8:
                        raise StopSSM()
                    k.barrier()
                    mb = buf("mixedT")
                    zTb = buf("uT")
                    for nt in range(4):
                        s_ = wctr[0] % 6
                        wctr[0] += 1
                        wb = buf("wgu%d" % s_)
                        wt = wgu[s_]
                        k.dma("pool", st_w[s_], wt[:, 0:4, :], w["ssm_w_glu"][l][:, nt * 128:(nt + 1) * 128].rearrange("(kt p) f -> p kt f", p=128), writes=[wb])
                        for blk in range(8):
                            bank, bb = pb[blk % 2], buf("pb%d" % (blk % 2))
                            for kt in range(4):
                                k.op("pe", lambda e: e.matmul(bank[:], wt[:, kt, :], uT[:, kt, blk * 512:(blk + 1) * 512], start=(kt == 0), stop=(kt == 3)),
                                     reads=[wb, zTb], writes=[bb])
                            sgs, sgbf = sg[blk % 2], buf("sg%d" % (blk % 2))
                            k.op("act", lambda e: e.activation(out=sgs, in_=bank[:], func=AF.Sigmoid, bias=bglu[:, l, nt:nt + 1], scale=1.0), reads=[bb, buf("bglu")], writes=[sgbf])
                            k.op("dve", lambda e: e.tensor_tensor(out=mixedT[:, nt, blk * 512:(blk + 1) * 512], in0=uT[:, nt, blk * 512:(blk + 1) * 512], in1=sgs, op=ALU.mult),
                                 reads=[sgbf, zTb], writes=[mb])
                    for blk in range(8):
                        pbn, pbb = pb[6], buf("pb6")
                        for kt in range(4):
                            q2 = kt % 2
                            sqb = buf("sq%d" % q2)
                            k.op("act", act(sq[q2][:], mixedT[:, kt, blk * 512:(blk + 1) * 512], AF.Square), reads=[mb], writes=[sqb])
                            k.op("pe", lambda e: e.matmul(pbn[:], ones_bf[:], sq[q2][:], start=(kt == 0), stop=(kt == 3)), reads=[sqb, buf("ones")], writes=[pbb])
                        s2 = blk % 2
                        rb = buf("rs%d" % s2)
                        k.op("act", act(rs[s2][:], pbn[:], AF.Sqrt, scale=1.0 / 512, bias=eps_t[:, 0:1]), reads=[pbb, buf("eps")], writes=[rb])
                        k.op("dve", lambda e: e.reciprocal(out=rs[s2][:], in_=rs[s2][:]), reads=[rb], writes=[rb])
                        for kt in range(4):
                            k.op("dve", lambda e: e.scalar_tensor_tensor(out=mixedT[:, kt, blk * 512:(blk + 1) * 512], in0=mixedT[:, kt, blk * 512:(blk + 1) * 512],
                                                                         scalar=gains2[:, l, 1, kt:kt + 1], in1=rs[s2][:], op0=ALU.mult, op1=ALU.mult),
                                 reads=[mb, rb, buf("gains2")], writes=[mb])
                    out_proj(l, 1, mb)
                    k.barrier()

                if do_ssm:
                    ssm_consts()
                    k.barrier()
                for l in range(depth):
                    if do_ffn:
                        ffn(l, "ffn1", gidx[("mix_norm", l)] if do_mixer else None)
                    if do_mixer:
                        mixer(l)
                        if do_ssm:
                            try:
                                ssm(l)
                            except StopSSM:
                                k.barrier()
                    if do_ffn:
                        ffn(l, "ffn2", gidx[("ffn1_norm", l + 1)] if l + 1 < depth else None)

                gfin = gidx["final"]
                for blk in range(8):
                    s = blk % 2
                    xb = buf("xn%d" % s)
                    k.dma("sp", st_x[s], xn[s][:], xres[:, :, blk * 512:(blk + 1) * 512].rearrange("kt p n -> p kt n"),
                          reads=[XB[kt][blk] for kt in range(8)], writes=[xb])
                    pbn, pbb = pb[6], buf("pb6")
                    for kt in range(8):
                        q2 = kt % 2
                        sqb = buf("sq%d" % q2)
                        k.op("act", act(sq[q2][:], xn[s][:, kt, :], AF.Square), reads=[xb], writes=[sqb])
                        k.op("pe", lambda e, kt=kt, q2=q2: e.matmul(pbn[:], ones_bf[:], sq[q2][:], start=(kt == 0), stop=(kt == 7)),
                             reads=[sqb, buf("ones")], writes=[pbb])
                    rb = buf("rs%d" % s)
                    k.op("act", act(rs[s][:], pbn[:], AF.Sqrt, scale=1.0 / D, bias=eps_t[:, 0:1]), reads=[pbb, buf("eps")], writes=[rb])
                    k.op("dve", lambda e: e.reciprocal(out=rs[s][:], in_=rs[s][:]), reads=[rb], writes=[rb])
                    for kt in range(8):
                        k.op("dve", lambda e, kt=kt: e.scalar_tensor_tensor(out=xn[s][:, kt, :], in0=xn[s][:, kt, :],
                                                                           scalar=gains[:, gfin, kt:kt + 1], in1=rs[s][:],
                                                                           op0=ALU.mult, op1=ALU.mult),
                             reads=[xb, rb, buf("gains")], writes=[xb])
                    for t4 in range(4):
                        tt = blk * 4 + t4
                        xt, xtb = xtok[tt % 4], buf("xtok%d" % (tt % 4))
                        for kt in range(8):
                            bank, bb = pb[kt], buf("pb%d" % kt)
                            k.op("pe", lambda e, bank=bank, kt=kt, t4=t4: e.transpose(bank[:, 0:128], xn[s][:, kt, t4 * 128:(t4 + 1) * 128], ident_f[:]),
                                 reads=[xb, buf("identf")], writes=[bb])
                            if kt % 2 == 0:
                                k.op("act", act(xt[:, kt * 128:(kt + 1) * 128], bank[:, 0:128], AF.Copy), reads=[bb], writes=[xtb])
                            else:
                                k.op("dve", lambda e, bank=bank, xt=xt, kt=kt: e.tensor_copy(out=xt[:, kt * 128:(kt + 1) * 128], in_=bank[:, 0:128]),
                                     reads=[bb], writes=[xtb])
                        k.dma("pool", st_out[tt % 4], y_out[tt * 128:(tt + 1) * 128, :], xt[:], reads=[xtb], writes=[buf("yout")])
                for s_ in st_out:
                    k.wait_stream("pool", s_)
                if k.active == "pe":
                    print("instructions:", k.ninstr, k.cnt)

        with nc.Block() as block:
            @block.tensor
            def _(en):
                gen(K(nc, sems, dsems, "pe", en))

            @block.scalar
            def _(en):
                gen(K(nc, sems, dsems, "act", en))

            @block.vector
            def _(en):
                gen(K(nc, sems, dsems, "dve", en))

            @block.gpsimd
            def _(en):
                gen(K(nc, sems, dsems, "pool", en))

            @block.sync
            def _(en):
                gen(K(nc, sems, dsems, "sp", en))
    return nc


_BUCKET = None


def _bias_host(rel_bias_table):
    rel = (np.arange(384)[None, :] - 128) - np.arange(128)[:, None]
    bucket = t5_bucket_np(rel)
    b = np.asarray(rel_bias_table)[bucket]
    b = np.transpose(b, (0, 2, 1)).copy()
    band = np.abs(rel) <= 128
    b = np.where(band[:, None, :], b, np.float32(NEG)).astype(np.float32)
    return np.ascontiguousarray(b)


def kernel(**inputs):
    x = np.asarray(inputs["x"], dtype=np.float32)
    nb = x.shape[0]
    nc = build()
    shared = {nm: np.ascontiguousarray(np.asarray(v, dtype=np.float32)) for nm, v in inputs.items()
              if nm not in ("x", "rel_bias_table")}
    shared["biasm"] = _bias_host(inputs["rel_bias_table"])
    in_maps = []
    for b in range(nb):
        m = dict(shared)
        m["x"] = np.ascontiguousarray(x[b])
        in_maps.append(m)
    res = run_bass_kernel_spmd(nc, in_maps, core_ids=list(range(nb)))
    return np.stack([r["y"] for r in res.results], axis=0).astype(np.float32)
```

```python
import math
import numpy as np
import concourse.bass as bass
import concourse.mybir as mybir
from concourse.bass_utils import run_bass_kernel_spmd

F32 = mybir.dt.float32
BF16 = mybir.dt.bfloat16
I32 = mybir.dt.int32
AF = mybir.ActivationFunctionType
ALU = mybir.AluOpType
AX = mybir.AxisListType

D = 1024
SEQ = 4096
DEPTH = 4
DFF = 2816
NFT = DFF // 128
DIN = 1280
NH = 8
HD = 64
G = 32
P = 64
HS = 16
T1 = 32
NCH = SEQ // T1
EPS = 1e-6
NEG = -1e30
N_BUCKETS = 32
MAX_DISTANCE = 128
TT = 2048
SSM_STOP = 99


class StopSSM(Exception):
    pass
NPASS = SEQ // TT


class Buf:
    __slots__ = ("name", "w", "r")

    def __init__(self, name):
        self.name = name
        self.w = None
        self.r = {}


class Stream:
    __slots__ = ("sem", "total", "name")

    def __init__(self, sem, name):
        self.sem = sem
        self.total = 0
        self.name = name


class K:
    def __init__(self, nc, sems, dsems, active=None, handle=None):
        self.nc = nc
        self.active = active
        self.eng = {"pe": nc.tensor, "act": nc.scalar, "dve": nc.vector, "pool": nc.gpsimd, "sp": nc.sync}
        if handle is not None:
            self.eng[active] = handle
        self.sem = sems
        self.cnt = {e: 0 for e in self.eng}
        self.waited = {e: {} for e in self.eng}
        self.dsems = list(dsems)
        self.streams = []
        self.ninstr = 0
        self.prog = {e: [] for e in self.eng}

    def stream(self, name):
        s = Stream(self.dsems.pop(), name)
        self.streams.append(s)
        return s

    def _wait(self, e, dep):
        if dep is None:
            return
        w = self.waited[e]
        if dep[0] == "e":
            _, e2, c = dep
            if e2 == e and e == "pe":
                return
            if w.get(e2, 0) >= c:
                return
            sem = self.sem[e2]
            if e == self.active:
                self.eng[e].wait_ge(sem, c)
            w[e2] = c
        else:
            s = dep[1]
            key = ("d", id(s))
            if w.get(key, 0) >= s.total:
                return
            tot = s.total
            if e == self.active:
                self.eng[e].wait_ge(s.sem, tot)
            w[key] = s.total

    def _deps(self, e, reads, writes):
        for b in reads:
            self._wait(e, b.w)
        for b in writes:
            self._wait(e, b.w)
            for dep in b.r.values():
                self._wait(e, dep)

    def op(self, e, fn, reads=(), writes=()):
        self._deps(e, reads, writes)
        sem = self.sem[e]
        if e == self.active:
            fn(self.eng[e]).then_inc(sem, 1)
        self.cnt[e] += 1
        dep = ("e", e, self.cnt[e])
        for b in reads:
            b.r[e] = dep
        for b in writes:
            b.w = dep
            b.r = {}
        self.ninstr += 1

    def dma(self, q, stream, out, in_, reads=(), writes=(), **kw):
        self._deps(q, reads, writes)
        if q == self.active:
            self.eng[q].dma_start(out=out, in_=in_, **kw).then_inc(stream.sem, 16)
        stream.total += 16
        dep = ("d", stream)
        for b in reads:
            b.r[("d", id(stream))] = dep
        for b in writes:
            b.w = dep
            b.r = {}
        self.ninstr += 1

    def barrier(self):
        for e in self.eng:
            for e2 in self.eng:
                if e2 != e and self.cnt[e2] > 0:
                    self._wait(e, ("e", e2, self.cnt[e2]))
            for s in self.streams:
                if s.total > 0:
                    self._wait(e, ("d", s))

    def wait_stream(self, e, s):
        if e == self.active:
            self.eng[e].wait_ge(s.sem, s.total)


def t5_bucket_np(rel):
    half = N_BUCKETS // 2
    max_exact = half // 2
    ret = np.where(rel > 0, half, 0)
    n = np.abs(rel)
    nf = np.maximum(n, 1).astype(np.float32)
    large = max_exact + (np.log(nf / max_exact) / math.log(MAX_DISTANCE / max_exact)
                         * (half - max_exact)).astype(np.int32)
    large = np.minimum(large, half - 1)
    return ret + np.where(n < max_exact, n, large)


def act(out, in_, func, **kw):
    return lambda e: e.activation(out=out, in_=in_, func=func, **kw)


def build(depth=DEPTH, do_mixer=True, do_ssm=True, dbg=None, do_ffn=True, do_attn=True):
    nc = bass.Bass("TRN2", target_bir_lowering=False)
    dt = nc.dram_tensor
    x_in = dt("x", [SEQ, D], F32, kind="ExternalInput").ap()
    biasm_d = dt("biasm", [128, NH, 384], F32, kind="ExternalInput").ap()
    w = {}
    for nm, shp in [("ffn1_norm", [DEPTH, D]), ("ffn1_w_gate", [DEPTH, D, DFF]), ("ffn1_w_up", [DEPTH, D, DFF]),
                    ("ffn1_w_down", [DEPTH, DFF, D]), ("mix_norm", [DEPTH, D]), ("w_in", [DEPTH, D, DIN]),
                    ("attn_sink", [DEPTH, NH]), ("ssm_a_re", [DEPTH, 2, G, P]), ("ssm_a_im", [DEPTH, 2, G, P]),
                    ("ssm_log_dt", [DEPTH, 2, G]), ("ssm_b_re", [DEPTH, 2, G, P, HS]),
                    ("ssm_b_im", [DEPTH, 2, G, P, HS]), ("ssm_c_re", [DEPTH, 2, G, HS, P]),
                    ("ssm_c_im", [DEPTH, 2, G, HS, P]), ("ssm_d", [DEPTH, 512]), ("ssm_w_glu", [DEPTH, 512, 512]),
                    ("ssm_b_glu", [DEPTH, 512]), ("attn_out_norm", [DEPTH, 512]), ("ssm_out_norm", [DEPTH, 512]),
                    ("w_out", [DEPTH, D, D]), ("ffn2_norm", [DEPTH, D]), ("ffn2_w_gate", [DEPTH, D, DFF]),
                    ("ffn2_w_up", [DEPTH, D, DFF]), ("ffn2_w_down", [DEPTH, DFF, D]), ("final_norm", [D])]:
        w[nm] = dt(nm, shp, F32, kind="ExternalInput").ap()
    y_out = dt("y", [SEQ, D], F32, kind="ExternalOutput").ap()
    xres = dt("xres", [8, 128, SEQ], F32, kind="Internal").ap()
    dbg_out = None
    if dbg is not None:
        dbg_out = dt("dbg", list(dbg), F32, kind="ExternalOutput").ap()

    from contextlib import ExitStack
    es = ExitStack()
    with es:
        def sb(name, shape, dtype):
            return es.enter_context(nc.sbuf_tensor(name, shape, dtype))

        def ps(name, shape, dtype):
            return es.enter_context(nc.psum_tensor(name, shape, dtype))

        sems = {e: es.enter_context(nc.semaphore("s_" + e)) for e in ["pe", "act", "dve", "pool", "sp"]}
        dsems = [es.enter_context(nc.semaphore("d%d" % i)) for i in range(40)]

        bigA = sb("bigA", [128, NFT * TT], BF16)
        bigB = sb("bigB", [128, 8 * TT], BF16)
        xn = [sb("xn%d" % i, [128, 8, 512], F32) for i in range(2)]
        sq = [sb("sq%d" % i, [128, 512], BF16) for i in range(2)]
        rs = [sb("rs%d" % i, [128, 512], F32) for i in range(2)]
        scr = sb("scr", [128, 7680], BF16)
        sg = [scr[:, 5632 + i * 1024:5632 + (i + 1) * 1024].bitcast(F32) for i in range(2)]
        wgu = [sb("wgu%d" % i, [128, 8, 128], BF16) for i in range(6)]
        wd = [scr[:, i * 2816:(i + 1) * 2816].rearrange("p (ft n) -> p ft n", n=128) for i in range(2)]
        xr = [sb("xr%d" % i, [128, 512], F32) for i in range(4)]
        gains = sb("gains", [128, DEPTH * 3 + 1, 8], F32)
        ones_bf = sb("ones_bf", [128, 128], BF16)
        ident_f = sb("ident_f", [128, 128], F32)
        ident_b = sb("ident_b", [128, 128], BF16)
        xtok = [bigB[:, i * 2048:(i + 1) * 2048].bitcast(F32) for i in range(4)]
        eps_t = sb("eps_t", [128, 1], F32)
        biasm = xn[1][:, :, 0:384]
        sinkb = sb("sinkb", [128, DEPTH * NH], F32)
        gains2 = sb("gains2", [128, DEPTH, 2, 4], F32)
        sc = [scr[:, i * 768:(i + 1) * 768].bitcast(F32) for i in range(2)]
        pexp = [scr[:, 1536 + i * 384:1536 + (i + 1) * 384] for i in range(2)]
        pT = [scr[:, 2304 + i * 384:2304 + (i + 1) * 384] for i in range(2)]
        atok = [scr[:, 3072 + i * 1024:3072 + (i + 1) * 1024].bitcast(F32) for i in range(2)]
        stat = [sb("stat%d" % i, [128, 4], F32) for i in range(2)]
        rowsum = [sb("rowsum%d" % i, [128, 2, NH], F32) for i in range(2)]
        nstat = [sb("nstat%d" % i, [128, 4], F32) for i in range(2)]
        kall = sb("kall", [128, 99], F32)
        kall1 = sb("kall1", [128, 99], F32)
        kalli = sb("kalli", [128, 99], I32)
        cst = sb("cst", [128, 2, 128], F32)
        rmt = sb("rmt", [128, 24], F32)
        rmi = sb("rmi", [128, 24], I32)
        bglu = sb("bglu", [128, DEPTH, 4], F32)
        ssm_scr = sb("ssm_scr", [128, 2048], BF16)
        ptmp = sb("ptmp", [128, 2, 528], F32)
        sctmp = sb("sctmp", [128, 4, 64], F32)
        hm = sb("hm", [128, 2], F32)

        pb = [ps("pb%d" % i, [128, 512], F32) for i in range(8)]

        def gen(k):
            B = {}

            def buf(name):
                if name not in B:
                    B[name] = Buf(name)
                return B[name]

            st_x = [k.stream("xn%d" % i) for i in range(2)]
            st_w = [k.stream("wgu%d" % i) for i in range(6)]
            st_wd = [k.stream("wd%d" % i) for i in range(2)]
            st_xr_in = [k.stream("xri%d" % i) for i in range(4)]
            st_xr_out = [k.stream("xro%d" % i) for i in range(4)]
            st_misc = k.stream("misc")
            st_tok = [k.stream("tok%d" % i) for i in range(4)]
            st_out = [k.stream("out%d" % i) for i in range(4)]

            if True:
                k.op("pool", lambda e: e.memset(ones_bf[:], 1.0), writes=[buf("ones")])
                k.op("pool", lambda e: e.memset(eps_t[:], EPS), writes=[buf("eps")])
                k.op("pool", lambda e: e.memset(ident_f[:], 0.0), writes=[buf("identf")])
                k.op("pool", lambda e: e.affine_select(out=ident_f[:], in_=ident_f[:], pattern=[[1, 128]], base=0,
                                                       channel_multiplier=-1, compare_op=ALU.not_equal, fill=1.0),
                     reads=[buf("identf")], writes=[buf("identf")])
                k.op("dve", lambda e: e.tensor_copy(out=ident_b[:], in_=ident_f[:]), reads=[buf("identf")],
                     writes=[buf("identb")])
                gidx = {}
                gi = 0
                with nc.allow_non_contiguous_dma(reason="small param loads"):
                    for l in range(DEPTH):
                        for nm in ["ffn1_norm", "mix_norm", "ffn2_norm"]:
                            gidx[(nm, l)] = gi
                            k.dma("sp", st_misc, gains[:, gi, :], w[nm][l].rearrange("(kt p) -> p kt", p=128),
                                  writes=[buf("gains")], allow_slow_non_contiguous=True)
                            gi += 1
                    gidx["final"] = gi
                    k.dma("sp", st_misc, gains[:, gi, :], w["final_norm"].rearrange("(kt p) -> p kt", p=128),
                          writes=[buf("gains")], allow_slow_non_contiguous=True)

                k.dma("sp", st_misc, sinkb[:], w["attn_sink"].rearrange("l h -> (l h)").partition_broadcast(128), writes=[buf("sinkb")],
                      allow_slow_non_contiguous=True)
                for l in range(DEPTH):
                    for i2, nm in enumerate(["attn_out_norm", "ssm_out_norm"]):
                        k.dma("sp", st_misc, gains2[:, l, i2, :], w[nm][l].rearrange("(kt p) -> p kt", p=128), writes=[buf("gains2")],
                              allow_slow_non_contiguous=True)
                XB = [[buf("xres_%d_%d" % (kt, nb)) for nb in range(8)] for kt in range(8)]

                for g4 in range(SEQ // 512):
                    stg = xn[g4 % 2]
                    stgb = buf("xn%d" % (g4 % 2))
                    for t4 in range(4):
                        tt = g4 * 4 + t4
                        xt = xtok[tt % 4]
                        xtb = buf("xtok%d" % (tt % 4))
                        k.dma("pool", st_tok[tt % 4], xt[:], x_in[tt * 128:(tt + 1) * 128, :], writes=[xtb])
                        for kt in range(8):
                            bank = pb[kt]
                            bb = buf("pb%d" % kt)
                            k.op("pe", lambda e, bank=bank, xt=xt, kt=kt: e.transpose(bank[:, 0:128], xt[:, kt * 128:(kt + 1) * 128], ident_f[:]),
                                 reads=[xtb, buf("identf")], writes=[bb])
                            eng = "act" if kt % 2 == 0 else "dve"
                            if eng == "act":
                                k.op("act", act(stg[:, kt, t4 * 128:(t4 + 1) * 128], bank[:, 0:128], AF.Copy),
                                     reads=[bb], writes=[stgb])
                            else:
                                k.op("dve", lambda e, bank=bank, stg=stg, kt=kt, t4=t4: e.tensor_copy(out=stg[:, kt, t4 * 128:(t4 + 1) * 128], in_=bank[:, 0:128]),
                                     reads=[bb], writes=[stgb])
                    k.dma("sp", st_x[g4 % 2], xres[:, :, g4 * 512:(g4 + 1) * 512].rearrange("kt p n -> p kt n"), stg[:],
                          reads=[stgb], writes=[XB[kt][g4] for kt in range(8)])

                hT = bigB[:].rearrange("p (kt n) -> p kt n", kt=8)

                def norm_block(gi_, blk, hcol, hb):
                    s = blk % 2
                    xb = buf("xn%d" % s)
                    k.dma("sp", st_x[s], xn[s][:], xres[:, :, blk * 512:(blk + 1) * 512].rearrange("kt p n -> p kt n"),
                          reads=[XB[kt][blk] for kt in range(8)], writes=[xb])
                    pbn = pb[6]
                    pbb = buf("pb6")
                    for kt in range(8):
                        q2 = kt % 2
                        sqb = buf("sq%d" % q2)
                        k.op("act", act(sq[q2][:], xn[s][:, kt, :], AF.Square), reads=[xb], writes=[sqb])
                        k.op("pe", lambda e, kt=kt, q2=q2: e.matmul(pbn[:], ones_bf[:], sq[q2][:], start=(kt == 0), stop=(kt == 7)),
                             reads=[sqb, buf("ones")], writes=[pbb])
                    rb = buf("rs%d" % s)
                    k.op("act", act(rs[s][:], pbn[:], AF.Sqrt, scale=1.0 / D, bias=eps_t[:, 0:1]), reads=[pbb, buf("eps")], writes=[rb])
                    k.op("dve", lambda e: e.reciprocal(out=rs[s][:], in_=rs[s][:]), reads=[rb], writes=[rb])
                    for kt in range(8):
                        k.op("dve", lambda e, kt=kt: e.scalar_tensor_tensor(out=hT[:, kt, hcol:hcol + 512], in0=xn[s][:, kt, :],
                                                                           scalar=gains[:, gi_, kt:kt + 1], in1=rs[s][:],
                                                                           op0=ALU.mult, op1=ALU.mult),
                             reads=[xb, rb, buf("gains")], writes=[hb])

                hid = bigA[:].rearrange("p (ft n) -> p ft n", ft=NFT)
                wctr = [0]
                wdctr = [0]
                xrctr = [0]

                def load_w(src_ap):
                    s = wctr[0] % 6
                    wctr[0] += 1
                    wb = buf("wgu%d" % s)
                    with nc.allow_non_contiguous_dma(reason="512B weight rows"):
                        k.dma("pool", st_w[s], wgu[s][:], src_ap.rearrange("(kt p) f -> p kt f", p=128), writes=[wb])
                    return wgu[s], wb

                def resid_update(nt, blk, psum_bank, pbb, scale):
                    s = xrctr[0] % 4
                    xrctr[0] += 1
                    xb = buf("xr%d" % s)
                    k.dma("sp", st_xr_in[s], xr[s][:], xres[nt, :, blk * 512:(blk + 1) * 512], reads=[XB[nt][blk]], writes=[xb])
                    k.op("dve", lambda e: e.scalar_tensor_tensor(out=xr[s][:], in0=psum_bank[:], scalar=float(scale), in1=xr[s][:],
                                                                 op0=ALU.mult, op1=ALU.add),
                         reads=[pbb, xb], writes=[xb])
                    k.dma("act", st_xr_out[s], xres[nt, :, blk * 512:(blk + 1) * 512], xr[s][:], reads=[xb], writes=[XB[nt][blk]])

                pre_done = set()

                def do_norm(gi_, blk, hcol, hb):
                    if (gi_, blk) in pre_done:
                        pre_done.discard((gi_, blk))
                        return
                    norm_block(gi_, blk, hcol, hb)

                def ffn(l, pre, next_gi=None):
                    wg_d, wu_d, wd_d = w[pre + "_w_gate"][l], w[pre + "_w_up"][l], w[pre + "_w_down"][l]
                    gi_ = gidx[(pre + "_norm", l)]
                    for tp in range(NPASS):
                        hbs = [buf("hT%d" % nb) for nb in range(4)]
                        for nb in range(4):
                            do_norm(gi_, tp * 4 + nb, nb * 512, hbs[nb])
                        for ft in range(NFT):
                            wg_t, wg_b = load_w(wg_d[:, ft * 128:(ft + 1) * 128])
                            wu_t, wu_b = load_w(wu_d[:, ft * 128:(ft + 1) * 128])
                            hidb = buf("hid%d" % ft)
                            for nb in range(4):
                                pg, pgb = pb[nb % 2], buf("pb%d" % (nb % 2))
                                pu, pub = pb[2 + nb % 2], buf("pb%d" % (2 + nb % 2))
                                for kt in range(8):
                                    k.op("pe", lambda e, kt=kt, pg=pg, wg_t=wg_t, nb=nb: e.matmul(pg[:], wg_t[:, kt, :], hT[:, kt, nb * 512:(nb + 1) * 512],
                                                                                                   start=(kt == 0), stop=(kt == 7)),
                                         reads=[wg_b, hbs[nb]], writes=[pgb])
                                for kt in range(8):
                                    k.op("pe", lambda e, kt=kt, pu=pu, wu_t=wu_t, nb=nb: e.matmul(pu[:], wu_t[:, kt, :], hT[:, kt, nb * 512:(nb + 1) * 512],
                                                                                                   start=(kt == 0), stop=(kt == 7)),
                                         reads=[wu_b, hbs[nb]], writes=[pub])
                                sgs = sg[nb % 2]
                                sgb = buf("sg%d" % (nb % 2))
                                k.op("act", act(sgs[:], pg[:], AF.Silu), reads=[pgb], writes=[sgb])
                                k.op("dve", lambda e, sgs=sgs, pu=pu, ft=ft, nb=nb: e.tensor_tensor(out=hid[:, ft, nb * 512:(nb + 1) * 512], in0=sgs[:], in1=pu[:], op=ALU.mult),
                                     reads=[sgb, pub], writes=[hidb])
                        hall = [buf("hid%d" % ft) for ft in range(NFT)]
                        for nt in range(8):
                            s = wdctr[0] % 2
                            wdctr[0] += 1
                            wdb = buf("wd%d" % s)
                            with nc.allow_non_contiguous_dma(reason="512B weight rows"):
                                k.dma("pool", st_wd[s], wd[s][:], wd_d[:, nt * 128:(nt + 1) * 128].rearrange("(ft p) n -> p ft n", p=128),
                                      writes=[wdb])
                            for nb in range(4):
                                pd, pdb = pb[4 + nb % 2], buf("pb%d" % (4 + nb % 2))
                                for ft in range(NFT):
                                    k.op("pe", lambda e, ft=ft, pd=pd, s=s, nb=nb: e.matmul(pd[:], wd[s][:, ft, :], hid[:, ft, nb * 512:(nb + 1) * 512],
                                                                                             start=(ft == 0), stop=(ft == NFT - 1)),
                                         reads=[wdb] + (hall if ft == 0 else []), writes=[pdb])
                                resid_update(nt, tp * 4 + nb, pd, pdb, 0.5)
                            if nt >= 4:
                                nbp = nt - 4
                                if tp + 1 < NPASS:
                                    norm_block(gi_, (tp + 1) * 4 + nbp, nbp * 512, hbs[nbp])
                                    pre_done.add((gi_, (tp + 1) * 4 + nbp))
                                elif next_gi is not None:
                                    norm_block(next_gi, nbp, nbp * 512, hbs[nbp])
                                    pre_done.add((next_gi, nbp))


                qT = bigA[:, 0:4 * SEQ].rearrange("p (j n) -> p j n", j=4)
                KK = bigA[:, 4 * SEQ:6 * SEQ].rearrange("p (j n) -> p j n", j=2)
                vtok = bigA[:, 6 * SEQ:7 * SEQ].rearrange("p (t c) -> p t c", c=128)
                uT = bigA[:, 7 * SEQ:11 * SEQ].rearrange("p (j n) -> p j n", j=4)
                mixedT = bigB[:].rearrange("p (j n) -> p j n", j=4)

                def mixer(l):
                    k.barrier()
                    gi_ = gidx[("mix_norm", l)]
                    win = w["w_in"][l]
                    qb, kb_, vb, ub = buf("qT"), buf("KK"), buf("vtok"), buf("uT")
                    for tp in range(NPASS):
                        hbs = [buf("hT%d" % nb) for nb in range(4)]
                        for nb in range(4):
                            do_norm(gi_, tp * 4 + nb, nb * 512, hbs[nb])
                        c0 = tp * TT
                        evi = 0
                        for kind, j, col in ([("q", j, 128 * j) for j in range(4)] + [("u", j, 768 + 128 * j) for j in range(4)]):
                            wt, wb = load_w(win[:, col:col + 128])
                            dst, db = (qT, qb) if kind == "q" else (uT, ub)
                            for nb in range(4):
                                bank, bb = pb[nb % 2], buf("pb%d" % (nb % 2))
                                for kt in range(8):
                                    k.op("pe", lambda e: e.matmul(bank[:], wt[:, kt, :], hT[:, kt, nb * 512:(nb + 1) * 512], start=(kt == 0), stop=(kt == 7)),
                                         reads=[wb, hbs[nb]], writes=[bb])
                                if kind == "u":
                                    cb0 = (c0 + nb * 512) // T1
                                    o_ap = uT[:, j, :].rearrange("p (i c) -> p i c", c=NCH)[:, :, cb0:cb0 + 16]
                                    i_ap = bank[:].rearrange("p (c i) -> p i c", i=T1)
                                else:
                                    o_ap = dst[:, j, c0 + nb * 512:c0 + (nb + 1) * 512]
                                    i_ap = bank[:]
                                if kind == "q":
                                    if evi % 2 == 0:
                                        k.op("act", act(o_ap, i_ap, AF.Copy, scale=0.125), reads=[bb], writes=[db])
                                    else:
                                        k.op("dve", lambda e: e.tensor_scalar(out=o_ap, in0=i_ap, scalar1=0.125, scalar2=None, op0=ALU.mult), reads=[bb], writes=[db])
                                elif evi % 2 == 0:
                                    k.op("act", act(o_ap, i_ap, AF.Copy), reads=[bb], writes=[db])
                                else:
                                    k.op("dve", lambda e: e.tensor_copy(out=o_ap, in_=i_ap), reads=[bb], writes=[db])
                                evi += 1
                        for kv in range(2):
                            s_ = wctr[0] % 6
                            wctr[0] += 1
                            wb = buf("wgu%d" % s_)
                            wt = wgu[s_]
                            for half in range(2):
                                k.dma("pool", st_w[s_], wt[:, :, half * 64:(half + 1) * 64],
                                      win[:, 512 + kv * 64:512 + (kv + 1) * 64].rearrange("(kt p) f -> p kt f", p=128), writes=[wb])
                            for nb in range(4):
                                bank, bb = pb[nb % 2], buf("pb%d" % (nb % 2))
                                for kt in range(8):
                                    k.op("pe", lambda e: e.matmul(bank[:], wt[:, kt, :], hT[:, kt, nb * 512:(nb + 1) * 512], start=(kt == 0), stop=(kt == 7)),
                                         reads=[wb, hbs[nb]], writes=[bb])
                                o_ap = KK[:, kv, c0 + nb * 512:c0 + (nb + 1) * 512]
                                k.op("act", act(o_ap, bank[:], AF.Copy), reads=[bb], writes=[kb_])
                        wt, wb = load_w(win[:, 640:768])
                        for t4 in range(TT // 512):
                            bank, bb = pb[2 + t4 % 2], buf("pb%d" % (2 + t4 % 2))
                            for ti in range(4):
                                tcol = t4 * 512 + ti * 128
                                for kt in range(8):
                                    k.op("pe", lambda e: e.matmul(bank[:, ti * 128:(ti + 1) * 128], hT[:, kt, tcol:tcol + 128], wt[:, kt, :], start=(kt == 0), stop=(kt == 7)),
                                         reads=[wb, hbs[t4]], writes=[bb])
                            tt0 = (c0 + t4 * 512) // 128
                            k.op("dve", lambda e: e.tensor_copy(out=vtok[:, tt0:tt0 + 4, :], in_=bank[:].rearrange("p (t c) -> p t c", c=128)),
                                 reads=[bb], writes=[vb])

                    k.barrier()
                    if not do_attn:
                        return
                    k.dma("sp", st_misc, biasm, biasm_d, writes=[buf("biasm")])
                    bhl = xn[0][:].rearrange("p a b -> p (a b)").bitcast(BF16)
                    bhi = bhl[:, 0:NH * 384].rearrange("p (h c) -> p h c", h=NH)
                    blo = bhl[:, NH * 384:2 * NH * 384].rearrange("p (h c) -> p h c", h=NH)
                    bhb = buf("bhl")
                    k.op("dve", lambda e: e.tensor_copy(out=bhi, in_=biasm), reads=[buf("biasm")], writes=[bhb])
                    k.op("dve", lambda e: e.tensor_tensor(out=blo, in0=biasm, in1=bhi, op=ALU.subtract), reads=[buf("biasm"), bhb], writes=[bhb])
                    mb = buf("mixedT")
                    NBLK = SEQ // 128

                    def geom(n):
                        kb0 = max(n - 1, 0)
                        kb1 = min(n + 1, NBLK - 1)
                        nk = (kb1 - kb0 + 1) * 128
                        bc0 = (kb0 - (n - 1)) * 128
                        return kb0, nk, bc0

                    def st_scores(i):
                        n, h = divmod(i, NH)
                        kb0, nk, bc0 = geom(n)
                        hp = (h % 2) * 64
                        rsum, rsb = rowsum[n % 2], buf("rowsum%d" % (n % 2))
                        sbank, sbb = pb[h % 2], buf("pb%d" % (h % 2))
                        k.op("pe", lambda e: e.matmul(sbank[:, 0:nk], qT[hp:hp + 64, h // 2, n * 128:(n + 1) * 128],
                                                      KK[hp:hp + 64, h // 4, kb0 * 128:kb0 * 128 + nk], start=True, stop=False),
                             reads=[qb, kb_], writes=[sbb])
                        k.op("pe", lambda e: e.matmul(sbank[:, 0:nk], ident_b[:], bhi[:, h, bc0:bc0 + nk], start=False, stop=False),
                             reads=[buf("bhl"), buf("identb")], writes=[sbb])
                        k.op("pe", lambda e: e.matmul(sbank[:, 0:nk], ident_b[:], blo[:, h, bc0:bc0 + nk], start=False, stop=True),
                             reads=[buf("bhl"), buf("identb")], writes=[sbb])
                        scs, scb = sbank, sbb
                        sts, stb = stat[h % 2], buf("stat%d" % (h % 2))
                        k.op("dve", lambda e: e.tensor_reduce(out=sts[:, 0:1], in_=scs[:, 0:nk], op=ALU.max, axis=AX.X),
                             reads=[scb], writes=[stb])
                        k.op("dve", lambda e: e.tensor_scalar(out=sts[:, 1:2], in0=sts[:, 0:1], scalar1=sinkb[:, l * NH + h:l * NH + h + 1],
                                                              scalar2=-1.0, op0=ALU.max, op1=ALU.mult),
                             reads=[stb, buf("sinkb")], writes=[stb])
                        pes, peb = pexp[h % 2], buf("pexp%d" % (h % 2))
                        k.op("act", lambda e: e.activation(out=pes[:, 0:nk], in_=scs[:, 0:nk], func=AF.Exp, bias=sts[:, 1:2], scale=1.0,
                                                           accum_out=rsum[:, 0, h:h + 1]),
                             reads=[scb, stb], writes=[peb, rsb])
                        k.op("act", lambda e: e.activation(out=rsum[:, 1, h:h + 1], in_=sinkb[:, l * NH + h:l * NH + h + 1], func=AF.Exp,
                                                           bias=sts[:, 1:2], scale=1.0),
                             reads=[stb, buf("sinkb")], writes=[rsb])

                    def st_transpose(i):
                        n, h = divmod(i, NH)
                        kb0, nk, bc0 = geom(n)
                        pes, peb = pexp[h % 2], buf("pexp%d" % (h % 2))
                        tbank, tbb = pb[2 + h % 2], buf("pb%d" % (2 + h % 2))
                        tview = tbank[:].bitcast(BF16)
                        for kb in range(nk // 128):
                            k.op("pe", lambda e: e.transpose(tview[:, kb * 128:(kb + 1) * 128], pes[:, kb * 128:(kb + 1) * 128], ident_b[:]),
                                 reads=[peb, buf("identb")], writes=[tbb])
                        pts, ptb = pT[h % 2], buf("pT%d" % (h % 2))
                        if h % 2 == 0:
                            k.op("act", act(pts[:, 0:nk], tview[:, 0:nk], AF.Copy), reads=[tbb], writes=[ptb])
                        else:
                            k.op("dve", lambda e: e.tensor_copy(out=pts[:, 0:nk], in_=tview[:, 0:nk]), reads=[tbb], writes=[ptb])

                    def st_pv(i):
                        n, h = divmod(i, NH)
                        kb0, nk, bc0 = geom(n)
                        obank, obb = pb[4 + n % 2], buf("pb%d" % (4 + n % 2))
                        rsum, rsb = rowsum[n % 2], buf("rowsum%d" % (n % 2))
                        pts, ptb = pT[h % 2], buf("pT%d" % (h % 2))
                        kvh = h // 4
                        for kb in range(nk // 128):
                            k.op("pe", lambda e: e.matmul(obank[:, h * 64:(h + 1) * 64], pts[:, kb * 128:(kb + 1) * 128],
                                                          vtok[:, kb0 + kb, kvh * 64:(kvh + 1) * 64], start=(kb == 0), stop=(kb == nk // 128 - 1)),
                                 reads=[ptb, vb], writes=[obb])
                        if h != NH - 1:
                            return
                        k.op("dve", lambda e: e.tensor_tensor(out=rsum[:, 0, :], in0=rsum[:, 0, :], in1=rsum[:, 1, :], op=ALU.add), reads=[rsb], writes=[rsb])
                        k.op("dve", lambda e: e.reciprocal(out=rsum[:, 0, :], in_=rsum[:, 0, :]), reads=[rsb], writes=[rsb])
                        at, atb = atok[n % 2], buf("atok%d" % (n % 2))
                        k.op("dve", lambda e: e.tensor_tensor(out=at[:].rearrange("p (h d) -> p h d", d=64), in0=obank[:].rearrange("p (h d) -> p h d", d=64),
                                                              in1=rsum[:, 0, :].unsqueeze(2).broadcast_to([128, NH, 64]), op=ALU.mult),
                             reads=[obb, rsb], writes=[atb])
                        ns, nsb = nstat[n % 2], buf("nstat%d" % (n % 2))
                        junk, jb = sq[0], buf("sq0")
                        k.op("act", lambda e: e.activation(out=junk[:, 0:512], in_=at[:, 0:512], func=AF.Square, accum_out=ns[:, 2:3]),
                             reads=[atb], writes=[jb, nsb])
                        k.op("act", act(ns[:, 3:4], ns[:, 2:3], AF.Sqrt, scale=1.0 / 512, bias=eps_t[:, 0:1]), reads=[nsb, buf("eps")], writes=[nsb])
                        k.op("dve", lambda e: e.reciprocal(out=ns[:, 3:4], in_=ns[:, 3:4]), reads=[nsb], writes=[nsb])
                        k.op("dve", lambda e: e.tensor_scalar(out=at[:], in0=at[:], scalar1=ns[:, 3:4], scalar2=None, op0=ALU.mult), reads=[atb, nsb], writes=[atb])
                        trb, trbb = pb[6 + n % 2], buf("pb%d" % (6 + n % 2))
                        for c in range(4):
                            k.op("pe", lambda e: e.transpose(trb[:, c * 128:(c + 1) * 128], at[:, c * 128:(c + 1) * 128], ident_f[:]),
                                 reads=[atb, buf("identf")], writes=[trbb])
                        for c in range(4):
                            k.op("dve", lambda e: e.tensor_scalar(out=mixedT[:, c, n * 128:(n + 1) * 128], in0=trb[:, c * 128:(c + 1) * 128],
                                                                  scalar1=gains2[:, l, 0, c:c + 1], scalar2=None, op0=ALU.mult),
                                 reads=[trbb, buf("gains2")], writes=[mb])

                    NI = NBLK * NH
                    for i in range(NI + 2):
                        if i < NI:
                            st_scores(i)
                        if 0 <= i - 1 < NI:
                            st_transpose(i - 1)
                        if 0 <= i - 2 < NI:
                            st_pv(i - 2)
                    out_proj(l, 0, mb)
                    k.barrier()

                def out_proj(l, half, mb):
                    wo = w["w_out"][l]
                    for nt in range(8):
                        s_ = wctr[0] % 6
                        wctr[0] += 1
                        wb = buf("wgu%d" % s_)
                        wt = wgu[s_]
                        k.dma("pool", st_w[s_], wt[:, 0:4, :], wo[half * 512:(half + 1) * 512, nt * 128:(nt + 1) * 128].rearrange("(kt p) f -> p kt f", p=128),
                              writes=[wb])
                        for blk in range(8):
                            bank, bb = pb[blk % 2], buf("pb%d" % (blk % 2))
                            for kt in range(4):
                                k.op("pe", lambda e: e.matmul(bank[:], wt[:, kt, :], mixedT[:, kt, blk * 512:(blk + 1) * 512], start=(kt == 0), stop=(kt == 3)),
                                     reads=[wb, mb], writes=[bb])
                            resid_update(nt, blk, bank, bb, 1.0)

                PA = xn[0][:].rearrange("p a b -> p (a b)")
                PBt = xn[1][:].rearrange("p a b -> p (a b)")

                def pa(off, n):
                    return PA[:, off:off + n]
                are, aim, dtv, rho, tht, den, arn, ain = [pa(32 * i, 32) for i in range(8)]
                L32r, L32i, L31r, L31i, L31in = [pa(256 + 32 * i, 32) for i in range(5)]
                Bre = pa(448, 512).rearrange("p (g h) -> p g h", h=16)
                Bim = pa(960, 512).rearrange("p (g h) -> p g h", h=16)
                Cre = pa(1472, 512).rearrange("p (g h) -> p g h", h=16)
                Cim = pa(1984, 512).rearrange("p (g h) -> p g h", h=16)
                Dv = pa(2496, 32)
                tmpA = pa(2528, 256)
                ldt = pa(2784, 32)
                XS = bigA[:, 0:4 * SEQ].bitcast(F32).rearrange("p (c g t) -> p c g t", g=G, t=2)
                SelM = bigA[:, 4 * SEQ:6 * SEQ].rearrange("p (a b m) -> p a b m", a=8, b=8)
                Zt = bigA[:, 6 * SEQ:7 * SEQ].rearrange("p (g m c) -> p g m c", g=8, m=4)
                WT = [[bigB[:, sl * 4096 + d_ * 2048:sl * 4096 + (d_ + 1) * 2048].rearrange("p (k j) -> p k j", k=4) for d_ in range(2)] for sl in range(2)]
                WYF = [bigB[:, 8192 + sl * 2112:8192 + sl * 2112 + 1056].rearrange("p (t j) -> p t j", t=2) for sl in range(3)]
                WYB = [bigB[:, 8192 + sl * 2112 + 1056:8192 + (sl + 1) * 2112].rearrange("p (t j) -> p t j", t=2) for sl in range(3)]
                WX = [scr[:, sl * 1024:(sl + 1) * 1024].rearrange("p (k t m) -> p k t m", k=4, t=2) for sl in range(2)]
                BX = [[scr[:, 2048 + sl * 1024 + pt * 512:2048 + sl * 1024 + (pt + 1) * 512] for pt in range(2)] for sl in range(2)]
                Ug = [scr[:, 4096 + i * 512:4096 + (i + 1) * 512].rearrange("p (k c) -> p k c", k=4) for i in range(2)]
                SG = [scr[:, 5120 + sl * 256:5120 + (sl + 1) * 256].rearrange("p (c t) -> p c t", t=2) for sl in range(2)]
                DDt = [scr[:, 5632 + sl * 128:5632 + (sl + 1) * 128] for sl in range(2)]
                BB = [[ssm_scr[:, sl * 1024 + pt * 512:sl * 1024 + (pt + 1) * 512] for pt in range(2)] for sl in range(2)]
                POL = [xr[2 * sl][:, 0:396].rearrange("p (t g k) -> p t g k", t=3, g=4) for sl in range(2)]
                POK = [xr[2 * sl + 1][:, 0:256].rearrange("p (t g k) -> p t g k", t=2, g=4) for sl in range(2)]
                P1, P2 = [ptmp[:, i, :] for i in range(2)]
                P3, P4 = pa(2816, 528), pa(3344, 528)
                MASKF, MASKB = cst[:, 0, :], cst[:, 1, :]
                RM = rmt[:, 0:8]
                TWO_PI = 2.0 * math.pi
                NB4 = 4
                def pbt(i, n=NB4 * 99):
                    return PBt[:, i * 396:i * 396 + n]
                T0, T1f, T2, T3, T4, TG = [pbt(i).rearrange("p (g k) -> p g k", g=NB4) for i in range(6)]
                T1i = PBt[:, 6 * 396:7 * 396].bitcast(I32).rearrange("p (g k) -> p g k", g=NB4)
                KAP = [PBt[:, 2772 + i * 128:2772 + (i + 1) * 128].rearrange("p (g k) -> p g k", g=NB4) for i in range(6)]
                D15 = PBt[:, 0:1920].rearrange("p (d m) -> p d m", d=15)

                def ssm_consts():
                    k.barrier()
                    hb_ = buf("hm")
                    k.op("pool", lambda e: e.memset(hm[:], 0.0), writes=[hb_])
                    k.op("pool", lambda e: e.memset(hm[0:64, 0:1], 1.0), reads=[hb_], writes=[hb_])
                    k.op("pool", lambda e: e.memset(hm[64:128, 1:2], 1.0), reads=[hb_], writes=[hb_])
                    pbf = buf("kall")
                    def io(ap, pat, base):
                        k.op("pool", lambda e: e.iota(out=ap, pattern=pat, base=base, channel_multiplier=0), writes=[pbf])
                    io(kalli[0:64, 0:32], [[-1, 32]], 0)
                    io(kalli[64:128, 0:32], [[1, 32]], -31)
                    io(kalli[0:64, 32:64], [[-1, 32]], 1)
                    io(kalli[64:128, 32:64], [[1, 32]], -30)
                    io(kalli[0:64, 64:97], [[1, 33]], 0)
                    io(kalli[64:128, 64:97], [[-1, 33]], 32)
                    io(kalli[:, 97:99], [[1, 2]], 31)
                    k.op("dve", lambda e: e.tensor_copy(out=kall[:], in_=kalli[:]), reads=[pbf], writes=[pbf])
                    k.op("dve", lambda e: e.tensor_copy(out=kall1[:], in_=kalli[:]), reads=[pbf], writes=[pbf])
                    k.op("dve", lambda e: e.tensor_scalar(out=kall1[:, 0:64], in0=kall1[:, 0:64], scalar1=31.0, scalar2=None, op0=ALU.add), reads=[pbf], writes=[pbf])
                    rb_ = buf("rmt")
                    k.op("pool", lambda e: e.iota(out=rmi[:, 8:9], pattern=[[0, 1]], base=0, channel_multiplier=1), writes=[rb_])
                    k.op("dve", lambda e: e.tensor_single_scalar(out=rmi[:, 8:9], in_=rmi[:, 8:9], scalar=4, op=ALU.arith_shift_right), reads=[rb_], writes=[rb_])
                    k.op("pool", lambda e: e.iota(out=rmi[:, 16:24], pattern=[[1, 8]], base=0, channel_multiplier=0), reads=[rb_], writes=[rb_])
                    k.op("dve", lambda e: e.tensor_copy(out=rmt[:, 8:9], in_=rmi[:, 8:9]), reads=[rb_], writes=[rb_])
                    k.op("dve", lambda e: e.tensor_copy(out=rmt[:, 16:24], in_=rmi[:, 16:24]), reads=[rb_], writes=[rb_])
                    k.op("dve", lambda e: e.tensor_scalar(out=rmt[:, 0:8], in0=rmt[:, 16:24], scalar1=rmt[:, 8:9], scalar2=None, op0=ALU.is_equal), reads=[rb_], writes=[rb_])
                    cb = buf("cst")
                    ci = PBt[:, 0:128].bitcast(I32)
                    k.op("pool", lambda e: e.iota(out=ci, pattern=[[1, 8], [0, 16]], base=0, channel_multiplier=0), writes=[buf("PBt")])
                    k.op("dve", lambda e: e.tensor_copy(out=cst[:, 1, :], in_=ci), reads=[buf("PBt")], writes=[cb])
                    k.op("dve", lambda e: e.tensor_scalar(out=cst[:, 0, :], in0=cst[:, 1, :], scalar1=rmt[:, 8:9], scalar2=None, op0=ALU.is_ge), reads=[cb, rb_], writes=[cb])
                    k.op("dve", lambda e: e.tensor_scalar(out=cst[:, 1, :], in0=cst[:, 1, :], scalar1=rmt[:, 8:9], scalar2=None, op0=ALU.is_le), reads=[cb, rb_], writes=[cb])
                    for l_ in range(DEPTH):
                        k.dma("sp", st_misc, bglu[:, l_, :], w["ssm_b_glu"][l_].rearrange("(kt p) -> p kt", p=128), writes=[buf("bglu")],
                              allow_slow_non_contiguous=True)

                def power_batch(bi, pab, phase1):
                    g0 = bi * NB4
                    sl = bi % 2
                    tb = buf("PBt")
                    pob = buf("PO%d" % sl)
                    kb3 = (kall1 if phase1 else kall)[:, :].unsqueeze(1).broadcast_to([128, NB4, 99])
                    def bc(pg):
                        return pg[:, g0:g0 + NB4].unsqueeze(2).broadcast_to([128, NB4, 99])
                    V = lambda fn: k.op("dve", fn, reads=[tb, pab, buf("kall")], writes=[tb])
                    A_ = lambda fn: k.op("act", fn, reads=[tb, pab], writes=[tb])
                    V(lambda e: e.scalar_tensor_tensor(out=T0, in0=bc(tht), scalar=1.0 / TWO_PI, in1=kb3, op0=ALU.mult, op1=ALU.mult))
                    V(lambda e: e.tensor_copy(out=T1i, in_=T0))
                    V(lambda e: e.tensor_copy(out=T1f, in_=T1i))
                    V(lambda e: e.tensor_tensor(out=T0, in0=T0, in1=T1f, op=ALU.subtract))
                    V(lambda e: e.tensor_scalar(out=T0, in0=T0, scalar1=0.49999, scalar2=-0.49999, op0=ALU.min, op1=ALU.max))
                    A_(lambda e: e.activation(out=T3, in_=T0, func=AF.Sin, scale=TWO_PI))
                    V(lambda e: e.tensor_scalar(out=TG, in0=T0, scalar1=0.25, scalar2=None, op0=ALU.is_gt))
                    V(lambda e: e.scalar_tensor_tensor(out=T0, in0=T0, scalar=0.25, in1=TG, op0=ALU.add, op1=ALU.subtract))
                    V(lambda e: e.tensor_scalar(out=T0, in0=T0, scalar1=0.49999, scalar2=-0.49999, op0=ALU.min, op1=ALU.max))
                    A_(lambda e: e.activation(out=T4, in_=T0, func=AF.Sin, scale=TWO_PI))
                    V(lambda e: e.tensor_tensor(out=T2, in0=bc(rho), in1=kb3, op=ALU.mult))
                    A_(lambda e: e.activation(out=T2, in_=T2, func=AF.Exp))
                    V(lambda e: e.tensor_tensor(out=T3, in0=T3, in1=T2, op=ALU.mult))
                    V(lambda e: e.tensor_tensor(out=T4, in0=T4, in1=T2, op=ALU.mult))
                    Nr, Ni, kr_, ki_, t1, t2 = KAP
                    V(lambda e: e.tensor_tensor(out=Nr, in0=T4[:, :, 32:64], in1=T4[:, :, 0:32], op=ALU.subtract))
                    V(lambda e: e.tensor_tensor(out=Ni, in0=T3[:, :, 32:64], in1=T3[:, :, 0:32], op=ALU.subtract))
                    def bc32(pg):
                        return pg[:, g0:g0 + NB4].unsqueeze(2).broadcast_to([128, NB4, 32])
                    VO = lambda fn: k.op("dve", fn, reads=[tb, pab], writes=[pob])
                    V(lambda e: e.tensor_tensor(out=t1, in0=Nr, in1=bc32(arn), op=ALU.mult))
                    V(lambda e: e.tensor_tensor(out=t2, in0=Ni, in1=bc32(ain), op=ALU.mult))
                    VO(lambda e: e.tensor_tensor(out=POK[sl][:, 0], in0=t1, in1=t2, op=ALU.add))
                    V(lambda e: e.tensor_tensor(out=t1, in0=Ni, in1=bc32(arn), op=ALU.mult))
                    V(lambda e: e.tensor_tensor(out=t2, in0=Nr, in1=bc32(ain), op=ALU.mult))
                    VO(lambda e: e.tensor_tensor(out=POK[sl][:, 1], in0=t1, in1=t2, op=ALU.subtract))
                    if phase1:
                        plb = buf("PAL")
                        k.op("dve", lambda e: e.tensor_copy(out=L32r[:, g0:g0 + NB4], in_=T4[:, :, 98]), reads=[tb], writes=[plb])
                        k.op("dve", lambda e: e.tensor_copy(out=L32i[:, g0:g0 + NB4], in_=T3[:, :, 98]), reads=[tb], writes=[plb])
                    else:
                        VO(lambda e: e.tensor_copy(out=POL[sl][:, 0], in_=T4[:, :, 64:97]))
                        VO(lambda e: e.tensor_copy(out=POL[sl][:, 1], in_=T3[:, :, 64:97]))
                        VO(lambda e: e.tensor_scalar(out=POL[sl][:, 2], in0=T3[:, :, 64:97], scalar1=-1.0, scalar2=None, op0=ALU.mult))

                def build_BB(g, pab, dst=None, dname="BB"):
                    sl, bi, gi = g % 2, g // NB4, g % NB4
                    if dst is None:
                        dst = BB
                    pob, bbb, p12 = buf("PO%d" % (bi % 2)), buf("%s%d" % (dname, sl)), buf("P12")
                    kr_, ki_ = POK[bi % 2][:, 0], POK[bi % 2][:, 1]
                    def kb_(t):
                        return t[:, gi, :].unsqueeze(2).broadcast_to([128, 32, 16])
                    def bb_(t):
                        return t[:, g, :].unsqueeze(1).broadcast_to([128, 32, 16])
                    p1v = P1[:, 0:512].rearrange("p (i h) -> p i h", h=16)
                    p2v = P2[:, 0:512].rearrange("p (i h) -> p i h", h=16)
                    V = lambda fn, w_: k.op("dve", fn, reads=[pob, pab, p12], writes=w_)
                    V(lambda e: e.tensor_tensor(out=p1v, in0=kb_(kr_), in1=bb_(Bre), op=ALU.mult), [p12])
                    V(lambda e: e.tensor_tensor(out=p2v, in0=kb_(ki_), in1=bb_(Bim), op=ALU.mult), [p12])
                    V(lambda e: e.tensor_tensor(out=dst[sl][0], in0=P1[:, 0:512], in1=P2[:, 0:512], op=ALU.subtract), [bbb])
                    V(lambda e: e.tensor_tensor(out=p1v, in0=kb_(kr_), in1=bb_(Bim), op=ALU.mult), [p12])
                    V(lambda e: e.tensor_tensor(out=p2v, in0=kb_(ki_), in1=bb_(Bre), op=ALU.mult), [p12])
                    V(lambda e: e.tensor_tensor(out=dst[sl][1], in0=P1[:, 0:512], in1=P2[:, 0:512], op=ALU.add), [bbb])
                    return bbb

                def build_U(g, slot):
                    kc, gl = g // 8, g % 8
                    ub_, ubank, ubb = buf("U%d" % slot), pb[slot], buf("pb%d" % slot)
                    for kti in range(4):
                        for i8 in range(8):
                            off = 8 * kti + i8
                            rhs = uT[:, kc, off * NCH:(off + 1) * NCH]
                            k.op("pe", lambda e: e.matmul(ubank[:, kti * 128:(kti + 1) * 128], SelM[:, gl, i8, :], rhs, start=(i8 == 0), stop=(i8 == 7)),
                                 reads=[buf("uT"), buf("SelM")], writes=[ubb])
                    k.op("act", act(Ug[slot], ubank[:].rearrange("p (k c) -> p k c", k=4), AF.Copy), reads=[ubb], writes=[ub_])
                    return ub_

                def ssm(l):
                    k.barrier()
                    pab = buf("PA")
                    for nm, dst_off in (("ssm_a_re", 0), ("ssm_a_im", 1)):
                        k.dma("sp", st_misc, tmpA[0:32, dst_off * 128:(dst_off + 1) * 128].rearrange("g (d p) -> g d p", d=2),
                              w[nm][l].rearrange("d g p -> g d p"), writes=[pab])
                    for d_ in range(2):
                        k.dma("sp", st_misc, ldt[64 * d_:64 * d_ + 64, :], w["ssm_log_dt"][l, d_].partition_broadcast(64), writes=[pab],
                              allow_slow_non_contiguous=True)
                        k.dma("sp", st_misc, Bre[64 * d_:64 * d_ + 64, :, :], w["ssm_b_re"][l, d_].rearrange("g p h -> p g h"), writes=[pab])
                        k.dma("sp", st_misc, Bim[64 * d_:64 * d_ + 64, :, :], w["ssm_b_im"][l, d_].rearrange("g p h -> p g h"), writes=[pab])
                    for i8 in range(8):
                        k.dma("sp", st_misc, Dv[16 * i8:16 * i8 + 16, :], w["ssm_d"][l].rearrange("(g h) -> h g", h=16), writes=[pab],
                              allow_slow_non_contiguous=True)
                    if SSM_STOP <= 0:
                        raise StopSSM()
                    tb = buf("PBt")
                    Cin = PBt[:, 0:512].rearrange("p (b m) -> p b m", b=4)
                    p6, p6b = pb[6], buf("pb6")
                    for nm, dstC in (("ssm_c_re", Cre), ("ssm_c_im", Cim)):
                        for d_ in range(2):
                            k.dma("sp", st_misc, Cin[:, :, 64 * d_:64 * d_ + 64], w[nm][l, d_].rearrange("(gb g8) h p -> (g8 h) gb p", g8=8), writes=[tb])
                        for gb in range(4):
                            k.op("pe", lambda e: e.transpose(p6[:, gb * 128:(gb + 1) * 128], Cin[:, gb, :], ident_f[:]), reads=[tb, buf("identf")], writes=[p6b])
                        k.op("dve", lambda e: e.tensor_copy(out=dstC.rearrange("p g h -> p (g h)"), in_=p6[:]), reads=[p6b], writes=[pab])
                    if SSM_STOP <= 1:
                        raise StopSSM()
                    for i_, dstA in ((0, are), (1, aim)):
                        k.op("pe", lambda e: e.transpose(p6[:, i_ * 32:(i_ + 1) * 32], tmpA[0:32, i_ * 128:(i_ + 1) * 128], ident_f[0:32, 0:32]),
                             reads=[pab, buf("identf")], writes=[p6b])
                    k.op("dve", lambda e: e.tensor_copy(out=are, in_=p6[:, 0:32]), reads=[p6b], writes=[pab])
                    k.op("dve", lambda e: e.tensor_copy(out=aim, in_=p6[:, 32:64]), reads=[p6b], writes=[pab])
                    if SSM_STOP <= 2:
                        raise StopSSM()
                    V = lambda fn: k.op("dve", fn, reads=[pab], writes=[pab])
                    k.op("act", act(dtv, ldt, AF.Exp), reads=[pab], writes=[pab])
                    V(lambda e: e.tensor_tensor(out=rho, in0=dtv, in1=are, op=ALU.mult))
                    V(lambda e: e.tensor_tensor(out=tht, in0=dtv, in1=aim, op=ALU.mult))
                    V(lambda e: e.tensor_tensor(out=den, in0=are, in1=are, op=ALU.mult))
                    V(lambda e: e.tensor_tensor(out=arn, in0=aim, in1=aim, op=ALU.mult))
                    V(lambda e: e.tensor_tensor(out=den, in0=den, in1=arn, op=ALU.add))
                    V(lambda e: e.reciprocal(out=den, in_=den))
                    V(lambda e: e.tensor_tensor(out=arn, in0=are, in1=den, op=ALU.mult))
                    V(lambda e: e.tensor_tensor(out=ain, in0=aim, in1=den, op=ALU.mult))
                    if SSM_STOP <= 3:
                        raise StopSSM()
                    selb = buf("SelM")
                    k.op("pool", lambda e: e.memset(D15, 0.0), reads=[tb], writes=[tb])
                    k.op("pool", lambda e: e.affine_select(out=D15, in_=D15, pattern=[[-16, 15], [1, 128]], base=112, channel_multiplier=-1,
                                                           compare_op=ALU.not_equal, fill=1.0), reads=[tb], writes=[tb])
                    for a_ in range(8):
                        for b_ in range(8):
                            k.op("dve", lambda e: e.tensor_scalar(out=SelM[:, a_, b_, :], in0=D15[:, b_ - a_ + 7, :], scalar1=RM[:, a_:a_ + 1], scalar2=None, op0=ALU.mult),
                                 reads=[tb, buf("rmt")], writes=[selb])
                    if SSM_STOP <= 4:
                        raise StopSSM()
                    xsb = buf("XS")
                    plb = buf("PAL")
                    p6b = None

                    def p1_A(g):
                        build_BB(g, pab, BX, "BX")

                    def p1_B(g):
                        sl = g % 2
                        bxb, wxb = buf("BX%d" % sl), buf("WX%d" % sl)
                        tb_, tbb_ = pb[6 + sl], buf("pb%d" % (6 + sl))
                        tv = tb_[:].bitcast(BF16)
                        for kt in range(4):
                            for part in range(2):
                                col = (kt * 2 + part) * 128
                                k.op("pe", lambda e: e.transpose(tv[:, col:col + 128], BX[sl][part][:, kt * 128:(kt + 1) * 128], ident_b[:]), reads=[bxb, buf("identb")], writes=[tbb_])
                        k.op("act", act(WX[sl].rearrange("p k t m -> p (k t m)"), tv, AF.Copy), reads=[tbb_], writes=[wxb])
                        ub_ = build_U(g, sl)
                        xbank, xbb = pb[2 + sl], buf("pb%d" % (2 + sl))
                        for part in range(2):
                            for kt in range(4):
                                k.op("pe", lambda e: e.matmul(xbank[:, part * 128:(part + 1) * 128], WX[sl][:, kt, part, :], Ug[sl][:, kt, :], start=(kt == 0), stop=(kt == 3)),
                                     reads=[wxb, ub_], writes=[xbb])
                        k.op("act", act(XS[:, :, g, :].rearrange("p c t -> p t c"), xbank[:, 0:256].rearrange("p (t c) -> p t c", t=2), AF.Copy),
                             reads=[xbb], writes=[xsb])

                    for step in range(G + 1):
                        if 0 <= step - 1 < G:
                            p1_B(step - 1)
                        if step < G:
                            if step % NB4 == 0:
                                power_batch(step // NB4, pab, True)
                            p1_A(step)
                    if SSM_STOP <= 6:
                        raise StopSSM()
                    LrB = sctmp[:, 2, :].rearrange("p (g t) -> p g t", t=2)
                    LiS = sctmp[:, 3, :].rearrange("p (g t) -> p g t", t=2)
                    scb = buf("sctmp")
                    k.op("dve", lambda e: e.tensor_copy(out=LrB, in_=L32r.unsqueeze(2).broadcast_to([128, G, 2])), reads=[plb], writes=[scb])
                    k.op("dve", lambda e: e.tensor_scalar(out=LiS[:, :, 0], in0=L32i, scalar1=-1.0, scalar2=None, op0=ALU.mult), reads=[plb], writes=[scb])
                    k.op("dve", lambda e: e.tensor_copy(out=LiS[:, :, 1], in_=L32i), reads=[plb], writes=[scb])
                    for s_ in range(1, NCH):
                        for d_ in range(2):
                            en_ = "dve" if d_ == 0 else "pool"
                            lo, hi = 64 * d_, 64 * d_ + 64
                            c = s_ if d_ == 0 else NCH - 1 - s_
                            cp = c - 1 if d_ == 0 else c + 1
                            prev = XS[lo:hi, cp, :, :]
                            cur = XS[lo:hi, c, :, :]
                            t1_ = sctmp[lo:hi, 0, :].rearrange("p (g t) -> p g t", t=2)
                            t2_ = sctmp[lo:hi, 1, :].rearrange("p (g t) -> p g t", t=2)
                            tb1, tb2 = buf("sct1_%d" % d_), buf("sct2_%d" % d_)
                            xh = buf("XS%d" % d_)
                            k.op(en_, lambda e: e.tensor_tensor(out=t1_, in0=prev, in1=LrB[lo:hi], op=ALU.mult), reads=[xh, xsb, scb], writes=[tb1])
                            k.op(en_, lambda e: e.tensor_tensor(out=t2_[:, :, 0], in0=prev[:, :, 1], in1=LiS[lo:hi, :, 0], op=ALU.mult), reads=[xh, xsb, scb], writes=[tb2])
                            k.op(en_, lambda e: e.tensor_tensor(out=t2_[:, :, 1], in0=prev[:, :, 0], in1=LiS[lo:hi, :, 1], op=ALU.mult), reads=[xh, xsb, scb], writes=[tb2])
                            k.op(en_, lambda e: e.tensor_tensor(out=t1_, in0=t1_, in1=t2_, op=ALU.add), reads=[tb1, tb2], writes=[tb1])
                            k.op(en_, lambda e: e.tensor_tensor(out=cur, in0=cur, in1=t1_, op=ALU.add), reads=[tb1, xh, xsb], writes=[xh])
                    xs_done = [buf("XS0"), buf("XS1"), xsb]
                    if SSM_STOP <= 7:
                        raise StopSSM()
                    zb = buf("Zt")
                    for s3_ in range(3):
                        k.op("pool", lambda e: e.memset(WYF[s3_][64:128, :, :], 0.0), writes=[buf("WYFB%d" % s3_)])
                        k.op("pool", lambda e: e.memset(WYB[s3_][0:64, :, :], 0.0), writes=[buf("WYFB%d" % s3_)])

                    def p2_A(g):
                        sl, bi, gi = g % 2, g // NB4, g % NB4
                        pob = buf("PO%d" % (bi % 2))
                        build_BB(g, pab)
                        ddb = buf("DD%d" % sl)
                        k.op("dve", lambda e: e.tensor_scalar(out=DDt[sl], in0=ident_f[:], scalar1=Dv[:, g:g + 1], scalar2=None, op0=ALU.mult), reads=[pab, buf("identf")], writes=[ddb])
                        s3 = g % 3
                        wyfb, p34 = buf("WYFB%d" % s3), buf("P34")
                        LPr, LPi, LPin = POL[bi % 2][:, 0], POL[bi % 2][:, 1], POL[bi % 2][:, 2]
                        def lp(t):
                            return t[:, gi, :].unsqueeze(2).broadcast_to([128, 33, 16])
                        def cb_(t):
                            return t[:, g, :].unsqueeze(1).broadcast_to([128, 33, 16])
                        p3v = P3.rearrange("p (t h) -> p t h", h=16)
                        p4v = P4.rearrange("p (t h) -> p t h", h=16)
                        Pl = lambda fn, w_: k.op("pool", fn, reads=[pob, pab, p34], writes=w_)
                        Pl(lambda e: e.tensor_tensor(out=p3v, in0=cb_(Cre), in1=lp(LPr), op=ALU.mult), [p34])
                        Pl(lambda e: e.tensor_tensor(out=p4v, in0=cb_(Cim), in1=lp(LPi), op=ALU.mult), [p34])
                        Pl(lambda e: e.tensor_tensor(out=WYF[s3][0:64, 0, :], in0=P3[0:64], in1=P4[0:64], op=ALU.subtract), [wyfb])
                        Pl(lambda e: e.tensor_tensor(out=WYB[s3][64:128, 0, :], in0=P3[64:128], in1=P4[64:128], op=ALU.subtract), [wyfb])
                        Pl(lambda e: e.tensor_tensor(out=p3v, in0=cb_(Cre), in1=lp(LPin), op=ALU.mult), [p34])
                        Pl(lambda e: e.tensor_tensor(out=p4v, in0=cb_(Cim), in1=lp(LPr), op=ALU.mult), [p34])
                        Pl(lambda e: e.tensor_tensor(out=WYF[s3][0:64, 1, :], in0=P3[0:64], in1=P4[0:64], op=ALU.subtract), [wyfb])
                        Pl(lambda e: e.tensor_tensor(out=WYB[s3][64:128, 1, :], in0=P3[64:128], in1=P4[64:128], op=ALU.subtract), [wyfb])

                    def p2_B(g):
                        sl = g % 2
                        s3 = g % 3
                        bbb, wyfb = buf("BB%d" % sl), buf("WYFB%d" % s3)
                        wtb = [buf("WT%d_%d" % (sl, d_)) for d_ in range(2)]
                        for d_ in range(2):
                            toff = 0 if d_ == 0 else 16
                            wyp = WYF[s3] if d_ == 0 else WYB[s3]
                            for kt in range(4):
                                jlo, jhi = (kt * 128, 512) if d_ == 0 else (0, (kt + 1) * 128)
                                bi_ = 4 + 2 * d_ + kt % 2
                                wbank, wbb = pb[bi_], buf("pb%d" % bi_)
                                k.op("pe", lambda e: e.matmul(wbank[:, jlo:jhi], BB[sl][0][:, kt * 128:(kt + 1) * 128], wyp[:, 0, toff + jlo:toff + jhi], start=True, stop=False),
                                     reads=[bbb, wyfb], writes=[wbb])
                                k.op("pe", lambda e: e.matmul(wbank[:, jlo:jhi], BB[sl][1][:, kt * 128:(kt + 1) * 128], wyp[:, 1, toff + jlo:toff + jhi], start=False, stop=True),
                                     reads=[bbb, wyfb], writes=[wbb])
                                dlo = kt * 128
                                olo, ohi = (dlo + 128, 512) if d_ == 0 else (0, dlo)
                                if ohi > olo:
                                    k.op("act", act(WT[sl][d_][:, kt, olo:ohi], wbank[:, olo:ohi], AF.Copy), reads=[wbb], writes=[wtb[d_]])
                                msk = MASKF if d_ == 0 else MASKB
                                k.op("dve", lambda e: e.tensor_tensor(out=WT[sl][d_][:, kt, dlo:dlo + 128], in0=wbank[:, dlo:dlo + 128], in1=msk, op=ALU.mult), reads=[wbb, buf("cst")], writes=[wtb[d_]])
                        sgb = buf("SG%d" % sl)
                        k.op("act", act(SG[sl], XS[:, :, g, :], AF.Copy), reads=xs_done, writes=[sgb])
                        build_U(g, sl)

                    def p2_C(g):
                        sl = g % 2
                        kc, gl = g // 8, g % 8
                        s3 = g % 3
                        wyfb, sgb, ddb, ub_ = buf("WYFB%d" % s3), buf("SG%d" % sl), buf("DD%d" % sl), buf("U%d" % sl)
                        wtb = [buf("WT%d_%d" % (sl, d_)) for d_ in range(2)]
                        ybank, ybb = pb[2 + sl], buf("pb%d" % (2 + sl))
                        for mt in range(4):
                            mlo = mt * 128
                            first = True
                            for kt in range(0, mt + 1):
                                k.op("pe", lambda e: e.matmul(ybank[:, mlo:mlo + 128], WT[sl][0][:, kt, mlo:mlo + 128], Ug[sl][:, kt, :], start=first, stop=False),
                                     reads=[wtb[0], ub_], writes=[ybb])
                                first = False
                            for kt in range(mt, 4):
                                k.op("pe", lambda e: e.matmul(ybank[:, mlo:mlo + 128], WT[sl][1][:, kt, mlo:mlo + 128], Ug[sl][:, kt, :], start=False, stop=False),
                                     reads=[wtb[1], ub_], writes=[ybb])
                            k.op("pe", lambda e: e.matmul(ybank[:, mlo:mlo + 128], DDt[sl], Ug[sl][:, mt, :], start=False, stop=False), reads=[ddb, ub_], writes=[ybb])
                            for part in range(2):
                                k.op("pe", lambda e: e.matmul(ybank[:, mlo + 1:mlo + 128], WYF[s3][:, part, 16 + mlo:16 + mlo + 128], SG[sl][:, 0:NCH - 1, part], start=False, stop=False),
                                     reads=[wyfb, sgb], writes=[ybb])
                            for part in range(2):
                                k.op("pe", lambda e: e.matmul(ybank[:, mlo:mlo + 127], WYB[s3][:, part, mlo:mlo + 128], SG[sl][:, 1:NCH, part], start=False, stop=(part == 1)),
                                     reads=[wyfb, sgb], writes=[ybb])
                        k.op("act", act(Zt[:, gl, :, :].rearrange("p m c -> p (m c)"), ybank[:], AF.Gelu_apprx_tanh), reads=[ybb], writes=[zb])
                        if gl == 7:
                            ubuf = buf("uT")
                            for j4 in range(8):
                                ibank, ibb = pb[j4 % 2], buf("pb%d" % (j4 % 2))
                                for jj in range(4):
                                    j = j4 * 4 + jj
                                    for gl2 in range(8):
                                        k.op("pe", lambda e: e.matmul(ibank[:, jj * 128:(jj + 1) * 128], SelM[:, j % 8, gl2, :], Zt[:, gl2, j // 8, :], start=(gl2 == 0), stop=(gl2 == 7)),
                                             reads=[zb, buf("SelM")], writes=[ibb])
                                dst = uT[:, kc, :].rearrange("p (c j) -> p j c", j=T1)[:, j4 * 4:(j4 + 1) * 4, :]
                                if j4 % 2 == 0:
                                    k.op("act", act(dst, ibank[:].rearrange("p (j c) -> p j c", j=4), AF.Copy), reads=[ibb], writes=[ubuf])
                                else:
                                    k.op("dve", lambda e: e.tensor_copy(out=dst, in_=ibank[:].rearrange("p (j c) -> p j c", j=4)), reads=[ibb], writes=[ubuf])

                    for step in range(G + 2):
                        if 0 <= step - 1 < G:
                            p2_B(step - 1)
                        if 0 <= step - 2 < G:
                            p2_C(step - 2)
                        if step < G:
                            if step % NB4 == 0:
                                power_batch(step // NB4, pab, False)
                            p2_A(step)
                    if SSM_STOP <= 8:
                        raise StopSSM()
                    k.barrier()
                    mb = buf("mixedT")
                    zTb = buf("uT")
                    for nt in range(4):
                        s_ = wctr[0] % 6
                        wctr[0] += 1
                        wb = buf("wgu%d" % s_)
                        wt = wgu[s_]
                        k.dma("pool", st_w[s_], wt[:, 0:4, :], w["ssm_w_glu"][l][:, nt * 128:(nt + 1) * 128].rearrange("(kt p) f -> p kt f", p=128), writes=[wb])
                        for blk in range(8):
                            bank, bb = pb[blk % 2], buf("pb%d" % (blk % 2))
                            for kt in range(4):
                                k.op("pe", lambda e: e.matmul(bank[:], wt[:, kt, :], uT[:, kt, blk * 512:(blk + 1) * 512], start=(kt == 0), stop=(kt == 3)),
                                     reads=[wb, zTb], writes=[bb])
                            sgs, sgbf = sg[blk % 2], buf("sg%d" % (blk % 2))
                            k.op("act", lambda e: e.activation(out=sgs, in_=bank[:], func=AF.Sigmoid, bias=bglu[:, l, nt:nt + 1], scale=1.0), reads=[bb, buf("bglu")], writes=[sgbf])
                            k.op("dve", lambda e: e.tensor_tensor(out=mixedT[:, nt, blk * 512:(blk + 1) * 512], in0=uT[:, nt, blk * 512:(blk + 1) * 512], in1=sgs, op=ALU.mult),
                                 reads=[sgbf, zTb], writes=[mb])
                    for blk in range(8):
                        pbn, pbb = pb[6], buf("pb6")
                        for kt in range(4):
                            q2 = kt % 2
                            sqb = buf("sq%d" % q2)
                            k.op("act", act(sq[q2][:], mixedT[:, kt, blk * 512:(blk + 1) * 512], AF.Square), reads=[mb], writes=[sqb])
                            k.op("pe", lambda e: e.matmul(pbn[:], ones_bf[:], sq[q2][:], start=(kt == 0), stop=(kt == 3)), reads=[sqb, buf("ones")], writes=[pbb])
                        s2 = blk % 2
                        rb = buf("rs%d" % s2)
                        k.op("act", act(rs[s2][:], pbn[:], AF.Sqrt, scale=1.0 / 512, bias=eps_t[:, 0:1]), reads=[pbb, buf("eps")], writes=[rb])
                        k.op("dve", lambda e: e.reciprocal(out=rs[s2][:], in_=rs[s2][:]), reads=[rb], writes=[rb])
                        for kt in range(4):
                            k.op("dve", lambda e: e.scalar_tensor_tensor(out=mixedT[:, kt, blk * 512:(blk + 1) * 512], in0=mixedT[:, kt, blk * 512:(blk + 1) * 512],
                                                                         scalar=gains2[:, l, 1, kt:kt + 1], in1=rs[s2][:], op0=ALU.mult, op1=ALU.mult),
                                 reads=[mb, rb, buf("gains2")], writes=[mb])
                    out_proj(l, 1, mb)
                    k.barrier()

                if do_ssm:
                    ssm_consts()
                    k.barrier()
                for l in range(depth):
                    if do_ffn:
                        ffn(l, "ffn1", gidx[("mix_norm", l)] if do_mixer else None)
                    if do_mixer:
                        mixer(l)
                        if do_ssm:
                            try:
                                ssm(l)
                            except StopSSM:
                                k.barrier()
                    if do_ffn:
                        ffn(l, "ffn2", gidx[("ffn1_norm", l + 1)] if l + 1 < depth else None)

                gfin = gidx["final"]
                for blk in range(8):
                    s = blk % 2
                    xb = buf("xn%d" % s)
                    k.dma("sp", st_x[s], xn[s][:], xres[:, :, blk * 512:(blk + 1) * 512].rearrange("kt p n -> p kt n"),
                          reads=[XB[kt][blk] for kt in range(8)], writes=[xb])
                    pbn, pbb = pb[6], buf("pb6")
                    for kt in range(8):
                        q2 = kt % 2
                        sqb = buf("sq%d" % q2)
                        k.op("act", act(sq[q2][:], xn[s][:, kt, :], AF.Square), reads=[xb], writes=[sqb])
                        k.op("pe", lambda e, kt=kt, q2=q2: e.matmul(pbn[:], ones_bf[:], sq[q2][:], start=(kt == 0), stop=(kt == 7)),
                             reads=[sqb, buf("ones")], writes=[pbb])
                    rb = buf("rs%d" % s)
                    k.op("act", act(rs[s][:], pbn[:], AF.Sqrt, scale=1.0 / D, bias=eps_t[:, 0:1]), reads=[pbb, buf("eps")], writes=[rb])
                    k.op("dve", lambda e: e.reciprocal(out=rs[s][:], in_=rs[s][:]), reads=[rb], writes=[rb])
                    for kt in range(8):
                        k.op("dve", lambda e, kt=kt: e.scalar_tensor_tensor(out=xn[s][:, kt, :], in0=xn[s][:, kt, :],
                                                                           scalar=gains[:, gfin, kt:kt + 1], in1=rs[s][:],
                                                                           op0=ALU.mult, op1=ALU.mult),
                             reads=[xb, rb, buf("gains")], writes=[xb])
                    for t4 in range(4):
                        tt = blk * 4 + t4
                        xt, xtb = xtok[tt % 4], buf("xtok%d" % (tt % 4))
                        for kt in range(8):
                            bank, bb = pb[kt], buf("pb%d" % kt)
                            k.op("pe", lambda e, bank=bank, kt=kt, t4=t4: e.transpose(bank[:, 0:128], xn[s][:, kt, t4 * 128:(t4 + 1) * 128], ident_f[:]),
                                 reads=[xb, buf("identf")], writes=[bb])
                            if kt % 2 == 0:
                                k.op("act", act(xt[:, kt * 128:(kt + 1) * 128], bank[:, 0:128], AF.Copy), reads=[bb], writes=[xtb])
                            else:
                                k.op("dve", lambda e, bank=bank, xt=xt, kt=kt: e.tensor_copy(out=xt[:, kt * 128:(kt + 1) * 128], in_=bank[:, 0:128]),
                                     reads=[bb], writes=[xtb])
                        k.dma("pool", st_out[tt % 4], y_out[tt * 128:(tt + 1) * 128, :], xt[:], reads=[xtb], writes=[buf("yout")])
                for s_ in st_out:
                    k.wait_stream("pool", s_)
                if k.active == "pe":
                    print("instructions:", k.ninstr, k.cnt)

        with nc.Block() as block:
            @block.tensor
            def _(en):
                gen(K(nc, sems, dsems, "pe", en))

            @block.scalar
            def _(en):
                gen(K(nc, sems, dsems, "act", en))

            @block.vector
            def _(en):
                gen(K(nc, sems, dsems, "dve", en))

            @block.gpsimd
            def _(en):
                gen(K(nc, sems, dsems, "pool", en))

            @block.sync
            def _(en):
                gen(K(nc, sems, dsems, "sp", en))
    return nc


_BUCKET = None


def _bias_host(rel_bias_table):
    rel = (np.arange(384)[None, :] - 128) - np.arange(128)[:, None]
    bucket = t5_bucket_np(rel)
    b = np.asarray(rel_bias_table)[bucket]
    b = np.transpose(b, (0, 2, 1)).copy()
    band = np.abs(rel) <= 128
    b = np.where(band[:, None, :], b, np.float32(NEG)).astype(np.float32)
    return np.ascontiguousarray(b)


def kernel(**inputs):
    x = np.asarray(inputs["x"], dtype=np.float32)
    nb = x.shape[0]
    nc = build()
    shared = {nm: np.ascontiguousarray(np.asarray(v, dtype=np.float32)) for nm, v in inputs.items()
              if nm not in ("x", "rel_bias_table")}
    shared["biasm"] = _bias_host(inputs["rel_bias_table"])
    in_maps = []
    for b in range(nb):
        m = dict(shared)
        m["x"] = np.ascontiguousarray(x[b])
        in_maps.append(m)
    res = run_bass_kernel_spmd(nc, in_maps, core_ids=list(range(nb)))
    return np.stack([r["y"] for r in res.results], axis=0).astype(np.float32)
```

```python
import math
import numpy as np
import concourse.bass as bass
import concourse.mybir as mybir
from concourse.bass_utils import run_bass_kernel_spmd

F32 = mybir.dt.float32
BF16 = mybir.dt.bfloat16
I32 = mybir.dt.int32
AF = mybir.ActivationFunctionType
ALU = mybir.AluOpType
AX = mybir.AxisListType

D = 1024
SEQ = 4096
DEPTH = 4
DFF = 2816
NFT = DFF // 128
DIN = 1280
NH = 8
HD = 64
G = 32
P = 64
HS = 16
T1 = 32
NCH = SEQ // T1
EPS = 1e-6
NEG = -1e30
N_BUCKETS = 32
MAX_DISTANCE = 128
TT = 2048
SSM_STOP = 99


class StopSSM(Exception):
    pass
NPASS = SEQ // TT


class Buf:
    __slots__ = ("name", "w", "r")

    def __init__(self, name):
        self.name = name
        self.w = None
        self.r = {}


class Stream:
    __slots__ = ("sem", "total", "name")

    def __init__(self, sem, name):
        self.sem = sem
        self.total = 0
        self.name = name


class K:
    def __init__(self, nc, sems, dsems, active=None, handle=None):
        self.nc = nc
        self.active = active
        self.eng = {"pe": nc.tensor, "act": nc.scalar, "dve": nc.vector, "pool": nc.gpsimd, "sp": nc.sync}
        if handle is not None:
            self.eng[active] = handle
        self.sem = sems
        self.cnt = {e: 0 for e in self.eng}
        self.waited = {e: {} for e in self.eng}
        self.dsems = list(dsems)
        self.streams = []
        self.ninstr = 0
        self.prog = {e: [] for e in self.eng}

    def stream(self, name):
        s = Stream(self.dsems.pop(), name)
        self.streams.append(s)
        return s

    def _wait(self, e, dep):
        if dep is None:
            return
        w = self.waited[e]
        if dep[0] == "e":
            _, e2, c = dep
            if e2 == e and e == "pe":
                return
            if w.get(e2, 0) >= c:
                return
            sem = self.sem[e2]
            if e == self.active:
                self.eng[e].wait_ge(sem, c)
            w[e2] = c
        else:
            s = dep[1]
            key = ("d", id(s))
            if w.get(key, 0) >= s.total:
                return
            tot = s.total
            if e == self.active:
                self.eng[e].wait_ge(s.sem, tot)
            w[key] = s.total

    def _deps(self, e, reads, writes):
        for b in reads:
            self._wait(e, b.w)
        for b in writes:
            self._wait(e, b.w)
            for dep in b.r.values():
                self._wait(e, dep)

    def op(self, e, fn, reads=(), writes=()):
        self._deps(e, reads, writes)
        sem = self.sem[e]
        if e == self.active:
            fn(self.eng[e]).then_inc(sem, 1)
        self.cnt[e] += 1
        dep = ("e", e, self.cnt[e])
        for b in reads:
            b.r[e] = dep
        for b in writes:
            b.w = dep
            b.r = {}
        self.ninstr += 1

    def dma(self, q, stream, out, in_, reads=(), writes=(), **kw):
        self._deps(q, reads, writes)
        if q == self.active:
            self.eng[q].dma_start(out=out, in_=in_, **kw).then_inc(stream.sem, 16)
        stream.total += 16
        dep = ("d", stream)
        for b in reads:
            b.r[("d", id(stream))] = dep
        for b in writes:
            b.w = dep
            b.r = {}
        self.ninstr += 1

    def barrier(self):
        for e in self.eng:
            for e2 in self.eng:
                if e2 != e and self.cnt[e2] > 0:
                    self._wait(e, ("e", e2, self.cnt[e2]))
            for s in self.streams:
                if s.total > 0:
                    self._wait(e, ("d", s))

    def wait_stream(self, e, s):
        if e == self.active:
            self.eng[e].wait_ge(s.sem, s.total)


def t5_bucket_np(rel):
    half = N_BUCKETS // 2
    max_exact = half // 2
    ret = np.where(rel > 0, half, 0)
    n = np.abs(rel)
    nf = np.maximum(n, 1).astype(np.float32)
    large = max_exact + (np.log(nf / max_exact) / math.log(MAX_DISTANCE / max_exact)
                         * (half - max_exact)).astype(np.int32)
    large = np.minimum(large, half - 1)
    return ret + np.where(n < max_exact, n, large)


def act(out, in_, func, **kw):
    return lambda e: e.activation(out=out, in_=in_, func=func, **kw)


def build(depth=DEPTH, do_mixer=True, do_ssm=True, dbg=None, do_ffn=True, do_attn=True):
    nc = bass.Bass("TRN2", target_bir_lowering=False)
    dt = nc.dram_tensor
    x_in = dt("x", [SEQ, D], F32, kind="ExternalInput").ap()
    biasm_d = dt("biasm", [128, NH, 384], F32, kind="ExternalInput").ap()
    w = {}
    for nm, shp in [("ffn1_norm", [DEPTH, D]), ("ffn1_w_gate", [DEPTH, D, DFF]), ("ffn1_w_up", [DEPTH, D, DFF]),
                    ("ffn1_w_down", [DEPTH, DFF, D]), ("mix_norm", [DEPTH, D]), ("w_in", [DEPTH, D, DIN]),
                    ("attn_sink", [DEPTH, NH]), ("ssm_a_re", [DEPTH, 2, G, P]), ("ssm_a_im", [DEPTH, 2, G, P]),
                    ("ssm_log_dt", [DEPTH, 2, G]), ("ssm_b_re", [DEPTH, 2, G, P, HS]),
                    ("ssm_b_im", [DEPTH, 2, G, P, HS]), ("ssm_c_re", [DEPTH, 2, G, HS, P]),
                    ("ssm_c_im", [DEPTH, 2, G, HS, P]), ("ssm_d", [DEPTH, 512]), ("ssm_w_glu", [DEPTH, 512, 512]),
                    ("ssm_b_glu", [DEPTH, 512]), ("attn_out_norm", [DEPTH, 512]), ("ssm_out_norm", [DEPTH, 512]),
                    ("w_out", [DEPTH, D, D]), ("ffn2_norm", [DEPTH, D]), ("ffn2_w_gate", [DEPTH, D, DFF]),
                    ("ffn2_w_up", [DEPTH, D, DFF]), ("ffn2_w_down", [DEPTH, DFF, D]), ("final_norm", [D])]:
        w[nm] = dt(nm, shp, F32, kind="ExternalInput").ap()
    y_out = dt("y", [SEQ, D], F32, kind="ExternalOutput").ap()
    xres = dt("xres", [8, 128, SEQ], F32, kind="Internal").ap()
    dbg_out = None
    if dbg is not None:
        dbg_out = dt("dbg", list(dbg), F32, kind="ExternalOutput").ap()

    from contextlib import ExitStack
    es = ExitStack()
    with es:
        def sb(name, shape, dtype):
            return es.enter_context(nc.sbuf_tensor(name, shape, dtype))

        def ps(name, shape, dtype):
            return es.enter_context(nc.psum_tensor(name, shape, dtype))

        sems = {e: es.enter_context(nc.semaphore("s_" + e)) for e in ["pe", "act", "dve", "pool", "sp"]}
        dsems = [es.enter_context(nc.semaphore("d%d" % i)) for i in range(40)]

        bigA = sb("bigA", [128, NFT * TT], BF16)
        bigB = sb("bigB", [128, 8 * TT], BF16)
        xn = [sb("xn%d" % i, [128, 8, 512], F32) for i in range(2)]
        sq = [sb("sq%d" % i, [128, 512], BF16) for i in range(2)]
        rs = [sb("rs%d" % i, [128, 512], F32) for i in range(2)]
        scr = sb("scr", [128, 7680], BF16)
        sg = [scr[:, 5632 + i * 1024:5632 + (i + 1) * 1024].bitcast(F32) for i in range(2)]
        wgu = [sb("wgu%d" % i, [128, 8, 128], BF16) for i in range(6)]
        wd = [scr[:, i * 2816:(i + 1) * 2816].rearrange("p (ft n) -> p ft n", n=128) for i in range(2)]
        xr = [sb("xr%d" % i, [128, 512], F32) for i in range(4)]
        gains = sb("gains", [128, DEPTH * 3 + 1, 8], F32)
        ones_bf = sb("ones_bf", [128, 128], BF16)
        ident_f = sb("ident_f", [128, 128], F32)
        ident_b = sb("ident_b", [128, 128], BF16)
        xtok = [bigB[:, i * 2048:(i + 1) * 2048].bitcast(F32) for i in range(4)]
        eps_t = sb("eps_t", [128, 1], F32)
        biasm = xn[1][:, :, 0:384]
        sinkb = sb("sinkb", [128, DEPTH * NH], F32)
        gains2 = sb("gains2", [128, DEPTH, 2, 4], F32)
        sc = [scr[:, i * 768:(i + 1) * 768].bitcast(F32) for i in range(2)]
        pexp = [scr[:, 1536 + i * 384:1536 + (i + 1) * 384] for i in range(2)]
        pT = [scr[:, 2304 + i * 384:2304 + (i + 1) * 384] for i in range(2)]
        atok = [scr[:, 3072 + i * 1024:3072 + (i + 1) * 1024].bitcast(F32) for i in range(2)]
        stat = [sb("stat%d" % i, [128, 4], F32) for i in range(2)]
        rowsum = [sb("rowsum%d" % i, [128, 2, NH], F32) for i in range(2)]
        nstat = [sb("nstat%d" % i, [128, 4], F32) for i in range(2)]
        kall = sb("kall", [128, 99], F32)
        kall1 = sb("kall1", [128, 99], F32)
        kalli = sb("kalli", [128, 99], I32)
        cst = sb("cst", [128, 2, 128], F32)
        rmt = sb("rmt", [128, 24], F32)
        rmi = sb("rmi", [128, 24], I32)
        bglu = sb("bglu", [128, DEPTH, 4], F32)
        ssm_scr = sb("ssm_scr", [128, 2048], BF16)
        ptmp = sb("ptmp", [128, 2, 528], F32)
        sctmp = sb("sctmp", [128, 4, 64], F32)
        hm = sb("hm", [128, 2], F32)

        pb = [ps("pb%d" % i, [128, 512], F32) for i in range(8)]

        def gen(k):
            B = {}

            def buf(name):
                if name not in B:
                    B[name] = Buf(name)
                return B[name]

            st_x = [k.stream("xn%d" % i) for i in range(2)]
            st_w = [k.stream("wgu%d" % i) for i in range(6)]
            st_wd = [k.stream("wd%d" % i) for i in range(2)]
            st_xr_in = [k.stream("xri%d" % i) for i in range(4)]
            st_xr_out = [k.stream("xro%d" % i) for i in range(4)]
            st_misc = k.stream("misc")
            st_tok = [k.stream("tok%d" % i) for i in range(4)]
            st_out = [k.stream("out%d" % i) for i in range(4)]

            if True:
                k.op("pool", lambda e: e.memset(ones_bf[:], 1.0), writes=[buf("ones")])
                k.op("pool", lambda e: e.memset(eps_t[:], EPS), writes=[buf("eps")])
                k.op("pool", lambda e: e.memset(ident_f[:], 0.0), writes=[buf("identf")])
                k.op("pool", lambda e: e.affine_select(out=ident_f[:], in_=ident_f[:], pattern=[[1, 128]], base=0,
                                                       channel_multiplier=-1, compare_op=ALU.not_equal, fill=1.0),
                     reads=[buf("identf")], writes=[buf("identf")])
                k.op("dve", lambda e: e.tensor_copy(out=ident_b[:], in_=ident_f[:]), reads=[buf("identf")],
                     writes=[buf("identb")])
                gidx = {}
                gi = 0
                with nc.allow_non_contiguous_dma(reason="small param loads"):
                    for l in range(DEPTH):
                        for nm in ["ffn1_norm", "mix_norm", "ffn2_norm"]:
                            gidx[(nm, l)] = gi
                            k.dma("sp", st_misc, gains[:, gi, :], w[nm][l].rearrange("(kt p) -> p kt", p=128),
                                  writes=[buf("gains")], allow_slow_non_contiguous=True)
                            gi += 1
                    gidx["final"] = gi
                    k.dma("sp", st_misc, gains[:, gi, :], w["final_norm"].rearrange("(kt p) -> p kt", p=128),
                          writes=[buf("gains")], allow_slow_non_contiguous=True)

                k.dma("sp", st_misc, sinkb[:], w["attn_sink"].rearrange("l h -> (l h)").partition_broadcast(128), writes=[buf("sinkb")],
                      allow_slow_non_contiguous=True)
                for l in range(DEPTH):
                    for i2, nm in enumerate(["attn_out_norm", "ssm_out_norm"]):
                        k.dma("sp", st_misc, gains2[:, l, i2, :], w[nm][l].rearrange("(kt p) -> p kt", p=128), writes=[buf("gains2")],
                              allow_slow_non_contiguous=True)
                XB = [[buf("xres_%d_%d" % (kt, nb)) for nb in range(8)] for kt in range(8)]

                for g4 in range(SEQ // 512):
                    stg = xn[g4 % 2]
                    stgb = buf("xn%d" % (g4 % 2))
                    for t4 in range(4):
                        tt = g4 * 4 + t4
                        xt = xtok[tt % 4]
                        xtb = buf("xtok%d" % (tt % 4))
                        k.dma("pool", st_tok[tt % 4], xt[:], x_in[tt * 128:(tt + 1) * 128, :], writes=[xtb])
                        for kt in range(8):
                            bank = pb[kt]
                            bb = buf("pb%d" % kt)
                            k.op("pe", lambda e, bank=bank, xt=xt, kt=kt: e.transpose(bank[:, 0:128], xt[:, kt * 128:(kt + 1) * 128], ident_f[:]),
                                 reads=[xtb, buf("identf")], writes=[bb])
                            eng = "act" if kt % 2 == 0 else "dve"
                            if eng == "act":
                                k.op("act", act(stg[:, kt, t4 * 128:(t4 + 1) * 128], bank[:, 0:128], AF.Copy),
                                     reads=[bb], writes=[stgb])
                            else:
                                k.op("dve", lambda e, bank=bank, stg=stg, kt=kt, t4=t4: e.tensor_copy(out=stg[:, kt, t4 * 128:(t4 + 1) * 128], in_=bank[:, 0:128]),
                                     reads=[bb], writes=[stgb])
                    k.dma("sp", st_x[g4 % 2], xres[:, :, g4 * 512:(g4 + 1) * 512].rearrange("kt p n -> p kt n"), stg[:],
                          reads=[stgb], writes=[XB[kt][g4] for kt in range(8)])

                hT = bigB[:].rearrange("p (kt n) -> p kt n", kt=8)

                def norm_block(gi_, blk, hcol, hb):
                    s = blk % 2
                    xb = buf("xn%d" % s)
                    k.dma("sp", st_x[s], xn[s][:], xres[:, :, blk * 512:(blk + 1) * 512].rearrange("kt p n -> p kt n"),
                          reads=[XB[kt][blk] for kt in range(8)], writes=[xb])
                    pbn = pb[6]
                    pbb = buf("pb6")
                    for kt in range(8):
                        q2 = kt % 2
                        sqb = buf("sq%d" % q2)
                        k.op("act", act(sq[q2][:], xn[s][:, kt, :], AF.Square), reads=[xb], writes=[sqb])
                        k.op("pe", lambda e, kt=kt, q2=q2: e.matmul(pbn[:], ones_bf[:], sq[q2][:], start=(kt == 0), stop=(kt == 7)),
                             reads=[sqb, buf("ones")], writes=[pbb])
                    rb = buf("rs%d" % s)
                    k.op("act", act(rs[s][:], pbn[:], AF.Sqrt, scale=1.0 / D, bias=eps_t[:, 0:1]), reads=[pbb, buf("eps")], writes=[rb])
                    k.op("dve", lambda e: e.reciprocal(out=rs[s][:], in_=rs[s][:]), reads=[rb], writes=[rb])
                    for kt in range(8):
                        k.op("dve", lambda e, kt=kt: e.scalar_tensor_tensor(out=hT[:, kt, hcol:hcol + 512], in0=xn[s][:, kt, :],
                                                                           scalar=gains[:, gi_, kt:kt + 1], in1=rs[s][:],
                                                                           op0=ALU.mult, op1=ALU.mult),
                             reads=[xb, rb, buf("gains")], writes=[hb])

                hid = bigA[:].rearrange("p (ft n) -> p ft n", ft=NFT)
                wctr = [0]
                wdctr = [0]
                xrctr = [0]

                def load_w(src_ap):
                    s = wctr[0] % 6
                    wctr[0] += 1
                    wb = buf("wgu%d" % s)
                    with nc.allow_non_contiguous_dma(reason="512B weight rows"):
                        k.dma("pool", st_w[s], wgu[s][:], src_ap.rearrange("(kt p) f -> p kt f", p=128), writes=[wb])
                    return wgu[s], wb

                def resid_update(nt, blk, psum_bank, pbb, scale):
                    s = xrctr[0] % 4
                    xrctr[0] += 1
                    xb = buf("xr%d" % s)
                    k.dma("sp", st_xr_in[s], xr[s][:], xres[nt, :, blk * 512:(blk + 1) * 512], reads=[XB[nt][blk]], writes=[xb])
                    k.op("dve", lambda e: e.scalar_tensor_tensor(out=xr[s][:], in0=psum_bank[:], scalar=float(scale), in1=xr[s][:],
                                                                 op0=ALU.mult, op1=ALU.add),
                         reads=[pbb, xb], writes=[xb])
                    k.dma("act", st_xr_out[s], xres[nt, :, blk * 512:(blk + 1) * 512], xr[s][:], reads=[xb], writes=[XB[nt][blk]])

                pre_done = set()

                def do_norm(gi_, blk, hcol, hb):
                    if (gi_, blk) in pre_done:
                        pre_done.discard((gi_, blk))
                        return
                    norm_block(gi_, blk, hcol, hb)

                def ffn(l, pre, next_gi=None):
                    wg_d, wu_d, wd_d = w[pre + "_w_gate"][l], w[pre + "_w_up"][l], w[pre + "_w_down"][l]
                    gi_ = gidx[(pre + "_norm", l)]
                    for tp in range(NPASS):
                        hbs = [buf("hT%d" % nb) for nb in range(4)]
                        for nb in range(4):
                            do_norm(gi_, tp * 4 + nb, nb * 512, hbs[nb])
                        for ft in range(NFT):
                            wg_t, wg_b = load_w(wg_d[:, ft * 128:(ft + 1) * 128])
                            wu_t, wu_b = load_w(wu_d[:, ft * 128:(ft + 1) * 128])
                            hidb = buf("hid%d" % ft)
                            for nb in range(4):
                                pg, pgb = pb[nb % 2], buf("pb%d" % (nb % 2))
                                pu, pub = pb[2 + nb % 2], buf("pb%d" % (2 + nb % 2))
                                for kt in range(8):
                                    k.op("pe", lambda e, kt=kt, pg=pg, wg_t=wg_t, nb=nb: e.matmul(pg[:], wg_t[:, kt, :], hT[:, kt, nb * 512:(nb + 1) * 512],
                                                                                                   start=(kt == 0), stop=(kt == 7)),
                                         reads=[wg_b, hbs[nb]], writes=[pgb])
                                for kt in range(8):
                                    k.op("pe", lambda e, kt=kt, pu=pu, wu_t=wu_t, nb=nb: e.matmul(pu[:], wu_t[:, kt, :], hT[:, kt, nb * 512:(nb + 1) * 512],
                                                                                                   start=(kt == 0), stop=(kt == 7)),
                                         reads=[wu_b, hbs[nb]], writes=[pub])
                                sgs = sg[nb % 2]
                                sgb = buf("sg%d" % (nb % 2))
                                k.op("act", act(sgs[:], pg[:], AF.Silu), reads=[pgb], writes=[sgb])
                                k.op("dve", lambda e, sgs=sgs, pu=pu, ft=ft, nb=nb: e.tensor_tensor(out=hid[:, ft, nb * 512:(nb + 1) * 512], in0=sgs[:], in1=pu[:], op=ALU.mult),
                                     reads=[sgb, pub], writes=[hidb])
                        hall = [buf("hid%d" % ft) for ft in range(NFT)]
                        for nt in range(8):
                            s = wdctr[0] % 2
                            wdctr[0] += 1
                            wdb = buf("wd%d" % s)
                            with nc.allow_non_contiguous_dma(reason="512B weight rows"):
                                k.dma("pool", st_wd[s], wd[s][:], wd_d[:, nt * 128:(nt + 1) * 128].rearrange("(ft p) n -> p ft n", p=128),
                                      writes=[wdb])
                            for nb in range(4):
                                pd, pdb = pb[4 + nb % 2], buf("pb%d" % (4 + nb % 2))
                                for ft in range(NFT):
                                    k.op("pe", lambda e, ft=ft, pd=pd, s=s, nb=nb: e.matmul(pd[:], wd[s][:, ft, :], hid[:, ft, nb * 512:(nb + 1) * 512],
                                                                                             start=(ft == 0), stop=(ft == NFT - 1)),
                                         reads=[wdb] + (hall if ft == 0 else []), writes=[pdb])
                                resid_update(nt, tp * 4 + nb, pd, pdb, 0.5)
                            if nt >= 4:
                                nbp = nt - 4
                                if tp + 1 < NPASS:
                                    norm_block(gi_, (tp + 1) * 4 + nbp, nbp * 512, hbs[nbp])
                                    pre_done.add((gi_, (tp + 1) * 4 + nbp))
                                elif next_gi is not None:
                                    norm_block(next_gi, nbp, nbp * 512, hbs[nbp])
                                    pre_done.add((next_gi, nbp))


                qT = bigA[:, 0:4 * SEQ].rearrange("p (j n) -> p j n", j=4)
                KK = bigA[:, 4 * SEQ:6 * SEQ].rearrange("p (j n) -> p j n", j=2)
                vtok = bigA[:, 6 * SEQ:7 * SEQ].rearrange("p (t c) -> p t c", c=128)
                uT = bigA[:, 7 * SEQ:11 * SEQ].rearrange("p (j n) -> p j n", j=4)
                mixedT = bigB[:].rearrange("p (j n) -> p j n", j=4)

                def mixer(l):
                    k.barrier()
                    gi_ = gidx[("mix_norm", l)]
                    win = w["w_in"][l]
                    qb, kb_, vb, ub = buf("qT"), buf("KK"), buf("vtok"), buf("uT")
                    for tp in range(NPASS):
                        hbs = [buf("hT%d" % nb) for nb in range(4)]
                        for nb in range(4):
                            do_norm(gi_, tp * 4 + nb, nb * 512, hbs[nb])
                        c0 = tp * TT
                        evi = 0
                        for kind, j, col in ([("q", j, 128 * j) for j in range(4)] + [("u", j, 768 + 128 * j) for j in range(4)]):
                            wt, wb = load_w(win[:, col:col + 128])
                            dst, db = (qT, qb) if kind == "q" else (uT, ub)
                            for nb in range(4):
                                bank, bb = pb[nb % 2], buf("pb%d" % (nb % 2))
                                for kt in range(8):
                                    k.op("pe", lambda e: e.matmul(bank[:], wt[:, kt, :], hT[:, kt, nb * 512:(nb + 1) * 512], start=(kt == 0), stop=(kt == 7)),
                                         reads=[wb, hbs[nb]], writes=[bb])
                                if kind == "u":
                                    cb0 = (c0 + nb * 512) // T1
                                    o_ap = uT[:, j, :].rearrange("p (i c) -> p i c", c=NCH)[:, :, cb0:cb0 + 16]
                                    i_ap = bank[:].rearrange("p (c i) -> p i c", i=T1)
                                else:
                                    o_ap = dst[:, j, c0 + nb * 512:c0 + (nb + 1) * 512]
                                    i_ap = bank[:]
                                if kind == "q":
                                    if evi % 2 == 0:
                                        k.op("act", act(o_ap, i_ap, AF.Copy, scale=0.125), reads=[bb], writes=[db])
                                    else:
                                        k.op("dve", lambda e: e.tensor_scalar(out=o_ap, in0=i_ap, scalar1=0.125, scalar2=None, op0=ALU.mult), reads=[bb], writes=[db])
                                elif evi % 2 == 0:
                                    k.op("act", act(o_ap, i_ap, AF.Copy), reads=[bb], writes=[db])
                                else:
                                    k.op("dve", lambda e: e.tensor_copy(out=o_ap, in_=i_ap), reads=[bb], writes=[db])
                                evi += 1
                        for kv in range(2):
                            s_ = wctr[0] % 6
                            wctr[0] += 1
                            wb = buf("wgu%d" % s_)
                            wt = wgu[s_]
                            for half in range(2):
                                k.dma("pool", st_w[s_], wt[:, :, half * 64:(half + 1) * 64],
                                      win[:, 512 + kv * 64:512 + (kv + 1) * 64].rearrange("(kt p) f -> p kt f", p=128), writes=[wb])
                            for nb in range(4):
                                bank, bb = pb[nb % 2], buf("pb%d" % (nb % 2))
                                for kt in range(8):
                                    k.op("pe", lambda e: e.matmul(bank[:], wt[:, kt, :], hT[:, kt, nb * 512:(nb + 1) * 512], start=(kt == 0), stop=(kt == 7)),
                                         reads=[wb, hbs[nb]], writes=[bb])
                                o_ap = KK[:, kv, c0 + nb * 512:c0 + (nb + 1) * 512]
                                k.op("act", act(o_ap, bank[:], AF.Copy), reads=[bb], writes=[kb_])
                        wt, wb = load_w(win[:, 640:768])
                        for t4 in range(TT // 512):
                            bank, bb = pb[2 + t4 % 2], buf("pb%d" % (2 + t4 % 2))
                            for ti in range(4):
                                tcol = t4 * 512 + ti * 128
                                for kt in range(8):
                                    k.op("pe", lambda e: e.matmul(bank[:, ti * 128:(ti + 1) * 128], hT[:, kt, tcol:tcol + 128], wt[:, kt, :], start=(kt == 0), stop=(kt == 7)),
                                         reads=[wb, hbs[t4]], writes=[bb])
                            tt0 = (c0 + t4 * 512) // 128
                            k.op("dve", lambda e: e.tensor_copy(out=vtok[:, tt0:tt0 + 4, :], in_=bank[:].rearrange("p (t c) -> p t c", c=128)),
                                 reads=[bb], writes=[vb])

                    k.barrier()
                    if not do_attn:
                        return
                    k.dma("sp", st_misc, biasm, biasm_d, writes=[buf("biasm")])
                    bhl = xn[0][:].rearrange("p a b -> p (a b)").bitcast(BF16)
                    bhi = bhl[:, 0:NH * 384].rearrange("p (h c) -> p h c", h=NH)
                    blo = bhl[:, NH * 384:2 * NH * 384].rearrange("p (h c) -> p h c", h=NH)
                    bhb = buf("bhl")
                    k.op("dve", lambda e: e.tensor_copy(out=bhi, in_=biasm), reads=[buf("biasm")], writes=[bhb])
                    k.op("dve", lambda e: e.tensor_tensor(out=blo, in0=biasm, in1=bhi, op=ALU.subtract), reads=[buf("biasm"), bhb], writes=[bhb])
                    mb = buf("mixedT")
                    NBLK = SEQ // 128

                    def geom(n):
                        kb0 = max(n - 1, 0)
                        kb1 = min(n + 1, NBLK - 1)
                        nk = (kb1 - kb0 + 1) * 128
                        bc0 = (kb0 - (n - 1)) * 128
                        return kb0, nk, bc0

                    def st_scores(i):
                        n, h = divmod(i, NH)
                        kb0, nk, bc0 = geom(n)
                        hp = (h % 2) * 64
                        rsum, rsb = rowsum[n % 2], buf("rowsum%d" % (n % 2))
                        sbank, sbb = pb[h % 2], buf("pb%d" % (h % 2))
                        k.op("pe", lambda e: e.matmul(sbank[:, 0:nk], qT[hp:hp + 64, h // 2, n * 128:(n + 1) * 128],
                                                      KK[hp:hp + 64, h // 4, kb0 * 128:kb0 * 128 + nk], start=True, stop=False),
                             reads=[qb, kb_], writes=[sbb])
                        k.op("pe", lambda e: e.matmul(sbank[:, 0:nk], ident_b[:], bhi[:, h, bc0:bc0 + nk], start=False, stop=False),
                             reads=[buf("bhl"), buf("identb")], writes=[sbb])
                        k.op("pe", lambda e: e.matmul(sbank[:, 0:nk], ident_b[:], blo[:, h, bc0:bc0 + nk], start=False, stop=True),
                             reads=[buf("bhl"), buf("identb")], writes=[sbb])
                        scs, scb = sbank, sbb
                        sts, stb = stat[h % 2], buf("stat%d" % (h % 2))
                        k.op("dve", lambda e: e.tensor_reduce(out=sts[:, 0:1], in_=scs[:, 0:nk], op=ALU.max, axis=AX.X),
                             reads=[scb], writes=[stb])
                        k.op("dve", lambda e: e.tensor_scalar(out=sts[:, 1:2], in0=sts[:, 0:1], scalar1=sinkb[:, l * NH + h:l * NH + h + 1],
                                                              scalar2=-1.0, op0=ALU.max, op1=ALU.mult),
                             reads=[stb, buf("sinkb")], writes=[stb])
                        pes, peb = pexp[h % 2], buf("pexp%d" % (h % 2))
                        k.op("act", lambda e: e.activation(out=pes[:, 0:nk], in_=scs[:, 0:nk], func=AF.Exp, bias=sts[:, 1:2], scale=1.0,
                                                           accum_out=rsum[:, 0, h:h + 1]),
                             reads=[scb, stb], writes=[peb, rsb])
                        k.op("act", lambda e: e.activation(out=rsum[:, 1, h:h + 1], in_=sinkb[:, l * NH + h:l * NH + h + 1], func=AF.Exp,
                                                           bias=sts[:, 1:2], scale=1.0),
                             reads=[stb, buf("sinkb")], writes=[rsb])

                    def st_transpose(i):
                        n, h = divmod(i, NH)
                        kb0, nk, bc0 = geom(n)
                        pes, peb = pexp[h % 2], buf("pexp%d" % (h % 2))
                        tbank, tbb = pb[2 + h % 2], buf("pb%d" % (2 + h % 2))
                        tview = tbank[:].bitcast(BF16)
                        for kb in range(nk // 128):
                            k.op("pe", lambda e: e.transpose(tview[:, kb * 128:(kb + 1) * 128], pes[:, kb * 128:(kb + 1) * 128], ident_b[:]),
                                 reads=[peb, buf("identb")], writes=[tbb])
                        pts, ptb = pT[h % 2], buf("pT%d" % (h % 2))
                        if h % 2 == 0:
                            k.op("act", act(pts[:, 0:nk], tview[:, 0:nk], AF.Copy), reads=[tbb], writes=[ptb])
                        else:
                            k.op("dve", lambda e: e.tensor_copy(out=pts[:, 0:nk], in_=tview[:, 0:nk]), reads=[tbb], writes=[ptb])

                    def st_pv(i):
                        n, h = divmod(i, NH)
                        kb0, nk, bc0 = geom(n)
                        obank, obb = pb[4 + n % 2], buf("pb%d" % (4 + n % 2))
                        rsum, rsb = rowsum[n % 2], buf("rowsum%d" % (n % 2))
                        pts, ptb = pT[h % 2], buf("pT%d" % (h % 2))
                        kvh = h // 4
                        for kb in range(nk // 128):
                            k.op("pe", lambda e: e.matmul(obank[:, h * 64:(h + 1) * 64], pts[:, kb * 128:(kb + 1) * 128],
                                                          vtok[:, kb0 + kb, kvh * 64:(kvh + 1) * 64], start=(kb == 0), stop=(kb == nk // 128 - 1)),
                                 reads=[ptb, vb], writes=[obb])
                        if h != NH - 1:
                            return
                        k.op("dve", lambda e: e.tensor_tensor(out=rsum[:, 0, :], in0=rsum[:, 0, :], in1=rsum[:, 1, :], op=ALU.add), reads=[rsb], writes=[rsb])
                        k.op("dve", lambda e: e.reciprocal(out=rsum[:, 0, :], in_=rsum[:, 0, :]), reads=[rsb], writes=[rsb])
                        at, atb = atok[n % 2], buf("atok%d" % (n % 2))
                        k.op("dve", lambda e: e.tensor_tensor(out=at[:].rearrange("p (h d) -> p h d", d=64), in0=obank[:].rearrange("p (h d) -> p h d", d=64),
                                                              in1=rsum[:, 0, :].unsqueeze(2).broadcast_to([128, NH, 64]), op=ALU.mult),
                             reads=[obb, rsb], writes=[atb])
                        ns, nsb = nstat[n % 2], buf("nstat%d" % (n % 2))
                        junk, jb = sq[0], buf("sq0")
                        k.op("act", lambda e: e.activation(out=junk[:, 0:512], in_=at[:, 0:512], func=AF.Square, accum_out=ns[:, 2:3]),
                             reads=[atb], writes=[jb, nsb])
                        k.op("act", act(ns[:, 3:4], ns[:, 2:3], AF.Sqrt, scale=1.0 / 512, bias=eps_t[:, 0:1]), reads=[nsb, buf("eps")], writes=[nsb])
                        k.op("dve", lambda e: e.reciprocal(out=ns[:, 3:4], in_=ns[:, 3:4]), reads=[nsb], writes=[nsb])
                        k.op("dve", lambda e: e.tensor_scalar(out=at[:], in0=at[:], scalar1=ns[:, 3:4], scalar2=None, op0=ALU.mult), reads=[atb, nsb], writes=[atb])
                        trb, trbb = pb[6 + n % 2], buf("pb%d" % (6 + n % 2))
                        for c in range(4):
                            k.op("pe", lambda e: e.transpose(trb[:, c * 128:(c + 1) * 128], at[:, c * 128:(c + 1) * 128], ident_f[:]),
                                 reads=[atb, buf("identf")], writes=[trbb])
                        for c in range(4):
                            k.op("dve", lambda e: e.tensor_scalar(out=mixedT[:, c, n * 128:(n + 1) * 128], in0=trb[:, c * 128:(c + 1) * 128],
                                                                  scalar1=gains2[:, l, 0, c:c + 1], scalar2=None, op0=ALU.mult),
                                 reads=[trbb, buf("gains2")], writes=[mb])

                    NI = NBLK * NH
                    for i in range(NI + 2):
                        if i < NI:
                            st_scores(i)
                        if 0 <= i - 1 < NI:
                            st_transpose(i - 1)
                        if 0 <= i - 2 < NI:
                            st_pv(i - 2)
                    out_proj(l, 0, mb)
                    k.barrier()

                def out_proj(l, half, mb):
                    wo = w["w_out"][l]
                    for nt in range(8):
                        s_ = wctr[0] % 6
                        wctr[0] += 1
                        wb = buf("wgu%d" % s_)
                        wt = wgu[s_]
                        k.dma("pool", st_w[s_], wt[:, 0:4, :], wo[half * 512:(half + 1) * 512, nt * 128:(nt + 1) * 128].rearrange("(kt p) f -> p kt f", p=128),
                              writes=[wb])
                        for blk in range(8):
                            bank, bb = pb[blk % 2], buf("pb%d" % (blk % 2))
                            for kt in range(4):
                                k.op("pe", lambda e: e.matmul(bank[:], wt[:, kt, :], mixedT[:, kt, blk * 512:(blk + 1) * 512], start=(kt == 0), stop=(kt == 3)),
                                     reads=[wb, mb], writes=[bb])
                            resid_update(nt, blk, bank, bb, 1.0)

                PA = xn[0][:].rearrange("p a b -> p (a b)")
                PBt = xn[1][:].rearrange("p a b -> p (a b)")

                def pa(off, n):
                    return PA[:, off:off + n]
                are, aim, dtv, rho, tht, den, arn, ain = [pa(32 * i, 32) for i in range(8)]
                L32r, L32i, L31r, L31i, L31in = [pa(256 + 32 * i, 32) for i in range(5)]
                Bre = pa(448, 512).rearrange("p (g h) -> p g h", h=16)
                Bim = pa(960, 512).rearrange("p (g h) -> p g h", h=16)
                Cre = pa(1472, 512).rearrange("p (g h) -> p g h", h=16)
                Cim = pa(1984, 512).rearrange("p (g h) -> p g h", h=16)
                Dv = pa(2496, 32)
                tmpA = pa(2528, 256)
                ldt = pa(2784, 32)
                XS = bigA[:, 0:4 * SEQ].bitcast(F32).rearrange("p (c t g) -> p c t g", g=G, t=2)
                SelM = bigA[:, 4 * SEQ:6 * SEQ].rearrange("p (a b m) -> p a b m", a=8, b=8)
                Zt = bigA[:, 6 * SEQ:7 * SEQ].rearrange("p (g m c) -> p g m c", g=8, m=4)
                WT = [[bigB[:, sl * 4096 + d_ * 2048:sl * 4096 + (d_ + 1) * 2048].rearrange("p (k j) -> p k j", k=4) for d_ in range(2)] for sl in range(2)]
                WYF = [bigB[:, 8192 + sl * 2112:8192 + sl * 2112 + 1056].rearrange("p (t j) -> p t j", t=2) for sl in range(3)]
                WYB = [bigB[:, 8192 + sl * 2112 + 1056:8192 + (sl + 1) * 2112].rearrange("p (t j) -> p t j", t=2) for sl in range(3)]
                WX = [scr[:, sl * 1024:(sl + 1) * 1024].rearrange("p (k t m) -> p k t m", k=4, t=2) for sl in range(2)]
                BX = [[scr[:, 2048 + sl * 1024 + pt * 512:2048 + sl * 1024 + (pt + 1) * 512] for pt in range(2)] for sl in range(2)]
                Ug = [scr[:, 4096 + i * 512:4096 + (i + 1) * 512].rearrange("p (k c) -> p k c", k=4) for i in range(2)]
                SG = [scr[:, 5120 + sl * 256:5120 + (sl + 1) * 256].rearrange("p (c t) -> p c t", t=2) for sl in range(2)]
                DDt = [scr[:, 5632 + sl * 128:5632 + (sl + 1) * 128] for sl in range(2)]
                BB = [[ssm_scr[:, sl * 1024 + pt * 512:sl * 1024 + (pt + 1) * 512] for pt in range(2)] for sl in range(2)]
                POL = [xr[2 * sl][:, 0:396].rearrange("p (t g k) -> p t g k", t=3, g=4) for sl in range(2)]
                POK = [xr[2 * sl + 1][:, 0:256].rearrange("p (t g k) -> p t g k", t=2, g=4) for sl in range(2)]
                P1, P2 = [ptmp[:, i, :] for i in range(2)]
                P3, P4 = pa(2816, 528), pa(3344, 528)
                MASKF, MASKB = cst[:, 0, :], cst[:, 1, :]
                RM = rmt[:, 0:8]
                TWO_PI = 2.0 * math.pi
                NB4 = 4
                def pbt(i, n=NB4 * 99):
                    return PBt[:, i * 396:i * 396 + n]
                T0, T1f, T2, T3, T4, TG = [pbt(i).rearrange("p (g k) -> p g k", g=NB4) for i in range(6)]
                T1i = PBt[:, 6 * 396:7 * 396].bitcast(I32).rearrange("p (g k) -> p g k", g=NB4)
                KAP = [PBt[:, 2772 + i * 128:2772 + (i + 1) * 128].rearrange("p (g k) -> p g k", g=NB4) for i in range(6)]
                D15 = PBt[:, 0:1920].rearrange("p (d m) -> p d m", d=15)

                def ssm_consts():
                    k.barrier()
                    hb_ = buf("hm")
                    k.op("pool", lambda e: e.memset(hm[:], 0.0), writes=[hb_])
                    k.op("pool", lambda e: e.memset(hm[0:64, 0:1], 1.0), reads=[hb_], writes=[hb_])
                    k.op("pool", lambda e: e.memset(hm[64:128, 1:2], 1.0), reads=[hb_], writes=[hb_])
                    pbf = buf("kall")
                    def io(ap, pat, base):
                        k.op("pool", lambda e: e.iota(out=ap, pattern=pat, base=base, channel_multiplier=0), writes=[pbf])
                    io(kalli[0:64, 0:32], [[-1, 32]], 0)
                    io(kalli[64:128, 0:32], [[1, 32]], -31)
                    io(kalli[0:64, 32:64], [[-1, 32]], 1)
                    io(kalli[64:128, 32:64], [[1, 32]], -30)
                    io(kalli[0:64, 64:97], [[1, 33]], 0)
                    io(kalli[64:128, 64:97], [[-1, 33]], 32)
                    io(kalli[:, 97:99], [[1, 2]], 31)
                    k.op("dve", lambda e: e.tensor_copy(out=kall[:], in_=kalli[:]), reads=[pbf], writes=[pbf])
                    k.op("dve", lambda e: e.tensor_copy(out=kall1[:], in_=kalli[:]), reads=[pbf], writes=[pbf])
                    k.op("dve", lambda e: e.tensor_scalar(out=kall1[:, 0:64], in0=kall1[:, 0:64], scalar1=31.0, scalar2=None, op0=ALU.add), reads=[pbf], writes=[pbf])
                    rb_ = buf("rmt")
                    k.op("pool", lambda e: e.iota(out=rmi[:, 8:9], pattern=[[0, 1]], base=0, channel_multiplier=1), writes=[rb_])
                    k.op("dve", lambda e: e.tensor_single_scalar(out=rmi[:, 8:9], in_=rmi[:, 8:9], scalar=4, op=ALU.arith_shift_right), reads=[rb_], writes=[rb_])
                    k.op("pool", lambda e: e.iota(out=rmi[:, 16:24], pattern=[[1, 8]], base=0, channel_multiplier=0), reads=[rb_], writes=[rb_])
                    k.op("dve", lambda e: e.tensor_copy(out=rmt[:, 8:9], in_=rmi[:, 8:9]), reads=[rb_], writes=[rb_])
                    k.op("dve", lambda e: e.tensor_copy(out=rmt[:, 16:24], in_=rmi[:, 16:24]), reads=[rb_], writes=[rb_])
                    k.op("dve", lambda e: e.tensor_scalar(out=rmt[:, 0:8], in0=rmt[:, 16:24], scalar1=rmt[:, 8:9], scalar2=None, op0=ALU.is_equal), reads=[rb_], writes=[rb_])
                    cb = buf("cst")
                    ci = PBt[:, 0:128].bitcast(I32)
                    k.op("pool", lambda e: e.iota(out=ci, pattern=[[1, 8], [0, 16]], base=0, channel_multiplier=0), writes=[buf("PBt")])
                    k.op("dve", lambda e: e.tensor_copy(out=cst[:, 1, :], in_=ci), reads=[buf("PBt")], writes=[cb])
                    k.op("dve", lambda e: e.tensor_scalar(out=cst[:, 0, :], in0=cst[:, 1, :], scalar1=rmt[:, 8:9], scalar2=None, op0=ALU.is_ge), reads=[cb, rb_], writes=[cb])
                    k.op("dve", lambda e: e.tensor_scalar(out=cst[:, 1, :], in0=cst[:, 1, :], scalar1=rmt[:, 8:9], scalar2=None, op0=ALU.is_le), reads=[cb, rb_], writes=[cb])
                    for l_ in range(DEPTH):
                        k.dma("sp", st_misc, bglu[:, l_, :], w["ssm_b_glu"][l_].rearrange("(kt p) -> p kt", p=128), writes=[buf("bglu")],
                              allow_slow_non_contiguous=True)

                def power_batch(bi, pab, phase1):
                    g0 = bi * NB4
                    sl = bi % 2
                    tb = buf("PBt")
                    pob = buf("PO%d" % sl)
                    kb3 = (kall1 if phase1 else kall)[:, :].unsqueeze(1).broadcast_to([128, NB4, 99])
                    def bc(pg):
                        return pg[:, g0:g0 + NB4].unsqueeze(2).broadcast_to([128, NB4, 99])
                    V = lambda fn: k.op("dve", fn, reads=[tb, pab, buf("kall")], writes=[tb])
                    A_ = lambda fn: k.op("act", fn, reads=[tb, pab], writes=[tb])
                    V(lambda e: e.scalar_tensor_tensor(out=T0, in0=bc(tht), scalar=1.0 / TWO_PI, in1=kb3, op0=ALU.mult, op1=ALU.mult))
                    V(lambda e: e.tensor_copy(out=T1i, in_=T0))
                    V(lambda e: e.tensor_copy(out=T1f, in_=T1i))
                    V(lambda e: e.tensor_tensor(out=T0, in0=T0, in1=T1f, op=ALU.subtract))
                    V(lambda e: e.tensor_scalar(out=T0, in0=T0, scalar1=0.49999, scalar2=-0.49999, op0=ALU.min, op1=ALU.max))
                    A_(lambda e: e.activation(out=T3, in_=T0, func=AF.Sin, scale=TWO_PI))
                    V(lambda e: e.tensor_scalar(out=TG, in0=T0, scalar1=0.25, scalar2=None, op0=ALU.is_gt))
                    V(lambda e: e.scalar_tensor_tensor(out=T0, in0=T0, scalar=0.25, in1=TG, op0=ALU.add, op1=ALU.subtract))
                    V(lambda e: e.tensor_scalar(out=T0, in0=T0, scalar1=0.49999, scalar2=-0.49999, op0=ALU.min, op1=ALU.max))
                    A_(lambda e: e.activation(out=T4, in_=T0, func=AF.Sin, scale=TWO_PI))
                    V(lambda e: e.tensor_tensor(out=T2, in0=bc(rho), in1=kb3, op=ALU.mult))
                    A_(lambda e: e.activation(out=T2, in_=T2, func=AF.Exp))
                    V(lambda e: e.tensor_tensor(out=T3, in0=T3, in1=T2, op=ALU.mult))
                    V(lambda e: e.tensor_tensor(out=T4, in0=T4, in1=T2, op=ALU.mult))
                    Nr, Ni, kr_, ki_, t1, t2 = KAP
                    V(lambda e: e.tensor_tensor(out=Nr, in0=T4[:, :, 32:64], in1=T4[:, :, 0:32], op=ALU.subtract))
                    V(lambda e: e.tensor_tensor(out=Ni, in0=T3[:, :, 32:64], in1=T3[:, :, 0:32], op=ALU.subtract))
                    def bc32(pg):
                        return pg[:, g0:g0 + NB4].unsqueeze(2).broadcast_to([128, NB4, 32])
                    VO = lambda fn: k.op("dve", fn, reads=[tb, pab], writes=[pob])
                    V(lambda e: e.tensor_tensor(out=t1, in0=Nr, in1=bc32(arn), op=ALU.mult))
                    V(lambda e: e.tensor_tensor(out=t2, in0=Ni, in1=bc32(ain), op=ALU.mult))
                    VO(lambda e: e.tensor_tensor(out=POK[sl][:, 0], in0=t1, in1=t2, op=ALU.add))
                    V(lambda e: e.tensor_tensor(out=t1, in0=Ni, in1=bc32(arn), op=ALU.mult))
                    V(lambda e: e.tensor_tensor(out=t2, in0=Nr, in1=bc32(ain), op=ALU.mult))
                    VO(lambda e: e.tensor_tensor(out=POK[sl][:, 1], in0=t1, in1=t2, op=ALU.subtract))
                    if phase1:
                        plb = buf("PAL")
                        k.op("dve", lambda e: e.tensor_copy(out=L32r[:, g0:g0 + NB4], in_=T4[:, :, 98]), reads=[tb], writes=[plb])
                        k.op("dve", lambda e: e.tensor_copy(out=L32i[:, g0:g0 + NB4], in_=T3[:, :, 98]), reads=[tb], writes=[plb])
                    else:
                        VO(lambda e: e.tensor_copy(out=POL[sl][:, 0], in_=T4[:, :, 64:97]))
                        VO(lambda e: e.tensor_copy(out=POL[sl][:, 1], in_=T3[:, :, 64:97]))
                        VO(lambda e: e.tensor_scalar(out=POL[sl][:, 2], in0=T3[:, :, 64:97], scalar1=-1.0, scalar2=None, op0=ALU.mult))

                def build_BB(g, pab, dst=None, dname="BB"):
                    sl, bi, gi = g % 2, g // NB4, g % NB4
                    if dst is None:
                        dst = BB
                    pob, bbb, p12 = buf("PO%d" % (bi % 2)), buf("%s%d" % (dname, sl)), buf("P12")
                    kr_, ki_ = POK[bi % 2][:, 0], POK[bi % 2][:, 1]
                    def kb_(t):
                        return t[:, gi, :].unsqueeze(2).broadcast_to([128, 32, 16])
                    def bb_(t):
                        return t[:, g, :].unsqueeze(1).broadcast_to([128, 32, 16])
                    p1v = P1[:, 0:512].rearrange("p (i h) -> p i h", h=16)
                    p2v = P2[:, 0:512].rearrange("p (i h) -> p i h", h=16)
                    V = lambda fn, w_: k.op("dve", fn, reads=[pob, pab, p12], writes=w_)
                    V(lambda e: e.tensor_tensor(out=p1v, in0=kb_(kr_), in1=bb_(Bre), op=ALU.mult), [p12])
                    V(lambda e: e.tensor_tensor(out=p2v, in0=kb_(ki_), in1=bb_(Bim), op=ALU.mult), [p12])
                    V(lambda e: e.tensor_tensor(out=dst[sl][0], in0=P1[:, 0:512], in1=P2[:, 0:512], op=ALU.subtract), [bbb])
                    V(lambda e: e.tensor_tensor(out=p1v, in0=kb_(kr_), in1=bb_(Bim), op=ALU.mult), [p12])
                    V(lambda e: e.tensor_tensor(out=p2v, in0=kb_(ki_), in1=bb_(Bre), op=ALU.mult), [p12])
                    V(lambda e: e.tensor_tensor(out=dst[sl][1], in0=P1[:, 0:512], in1=P2[:, 0:512], op=ALU.add), [bbb])
                    return bbb

                def build_U(g, slot):
                    kc, gl = g // 8, g % 8
                    ub_, ubank, ubb = buf("U%d" % slot), pb[slot], buf("pb%d" % slot)
                    for kti in range(4):
                        for i8 in range(8):
                            off = 8 * kti + i8
                            rhs = uT[:, kc, off * NCH:(off + 1) * NCH]
                            k.op("pe", lambda e: e.matmul(ubank[:, kti * 128:(kti + 1) * 128], SelM[:, gl, i8, :], rhs, start=(i8 == 0), stop=(i8 == 7)),
                                 reads=[buf("uT"), buf("SelM")], writes=[ubb])
                    k.op("act", act(Ug[slot], ubank[:].rearrange("p (k c) -> p k c", k=4), AF.Copy), reads=[ubb], writes=[ub_])
                    return ub_

                def ssm(l):
                    k.barrier()
                    pab = buf("PA")
                    for nm, dst_off in (("ssm_a_re", 0), ("ssm_a_im", 1)):
                        k.dma("sp", st_misc, tmpA[0:32, dst_off * 128:(dst_off + 1) * 128].rearrange("g (d p) -> g d p", d=2),
                              w[nm][l].rearrange("d g p -> g d p"), writes=[pab])
                    for d_ in range(2):
                        k.dma("sp", st_misc, ldt[64 * d_:64 * d_ + 64, :], w["ssm_log_dt"][l, d_].partition_broadcast(64), writes=[pab],
                              allow_slow_non_contiguous=True)
                        k.dma("sp", st_misc, Bre[64 * d_:64 * d_ + 64, :, :], w["ssm_b_re"][l, d_].rearrange("g p h -> p g h"), writes=[pab])
                        k.dma("sp", st_misc, Bim[64 * d_:64 * d_ + 64, :, :], w["ssm_b_im"][l, d_].rearrange("g p h -> p g h"), writes=[pab])
                    for i8 in range(8):
                        k.dma("sp", st_misc, Dv[16 * i8:16 * i8 + 16, :], w["ssm_d"][l].rearrange("(g h) -> h g", h=16), writes=[pab],
                              allow_slow_non_contiguous=True)
                    if SSM_STOP <= 0:
                        raise StopSSM()
                    tb = buf("PBt")
                    Cin = PBt[:, 0:512].rearrange("p (b m) -> p b m", b=4)
                    p6, p6b = pb[6], buf("pb6")
                    for nm, dstC in (("ssm_c_re", Cre), ("ssm_c_im", Cim)):
                        for d_ in range(2):
                            k.dma("sp", st_misc, Cin[:, :, 64 * d_:64 * d_ + 64], w[nm][l, d_].rearrange("(gb g8) h p -> (g8 h) gb p", g8=8), writes=[tb])
                        for gb in range(4):
                            k.op("pe", lambda e: e.transpose(p6[:, gb * 128:(gb + 1) * 128], Cin[:, gb, :], ident_f[:]), reads=[tb, buf("identf")], writes=[p6b])
                        k.op("dve", lambda e: e.tensor_copy(out=dstC.rearrange("p g h -> p (g h)"), in_=p6[:]), reads=[p6b], writes=[pab])
                    if SSM_STOP <= 1:
                        raise StopSSM()
                    for i_, dstA in ((0, are), (1, aim)):
                        k.op("pe", lambda e: e.transpose(p6[:, i_ * 32:(i_ + 1) * 32], tmpA[0:32, i_ * 128:(i_ + 1) * 128], ident_f[0:32, 0:32]),
                             reads=[pab, buf("identf")], writes=[p6b])
                    k.op("dve", lambda e: e.tensor_copy(out=are, in_=p6[:, 0:32]), reads=[p6b], writes=[pab])
                    k.op("dve", lambda e: e.tensor_copy(out=aim, in_=p6[:, 32:64]), reads=[p6b], writes=[pab])
                    if SSM_STOP <= 2:
                        raise StopSSM()
                    V = lambda fn: k.op("dve", fn, reads=[pab], writes=[pab])
                    k.op("act", act(dtv, ldt, AF.Exp), reads=[pab], writes=[pab])
                    V(lambda e: e.tensor_tensor(out=rho, in0=dtv, in1=are, op=ALU.mult))
                    V(lambda e: e.tensor_tensor(out=tht, in0=dtv, in1=aim, op=ALU.mult))
                    V(lambda e: e.tensor_tensor(out=den, in0=are, in1=are, op=ALU.mult))
                    V(lambda e: e.tensor_tensor(out=arn, in0=aim, in1=aim, op=ALU.mult))
                    V(lambda e: e.tensor_tensor(out=den, in0=den, in1=arn, op=ALU.add))
                    V(lambda e: e.reciprocal(out=den, in_=den))
                    V(lambda e: e.tensor_tensor(out=arn, in0=are, in1=den, op=ALU.mult))
                    V(lambda e: e.tensor_tensor(out=ain, in0=aim, in1=den, op=ALU.mult))
                    if SSM_STOP <= 3:
                        raise StopSSM()
                    selb = buf("SelM")
                    k.op("pool", lambda e: e.memset(D15, 0.0), reads=[tb], writes=[tb])
                    k.op("pool", lambda e: e.affine_select(out=D15, in_=D15, pattern=[[-16, 15], [1, 128]], base=112, channel_multiplier=-1,
                                                           compare_op=ALU.not_equal, fill=1.0), reads=[tb], writes=[tb])
                    for a_ in range(8):
                        for b_ in range(8):
                            k.op("dve", lambda e: e.tensor_scalar(out=SelM[:, a_, b_, :], in0=D15[:, b_ - a_ + 7, :], scalar1=RM[:, a_:a_ + 1], scalar2=None, op0=ALU.mult),
                                 reads=[tb, buf("rmt")], writes=[selb])
                    if SSM_STOP <= 4:
                        raise StopSSM()
                    xsb = buf("XS")
                    plb = buf("PAL")
                    p6b = None

                    def p1_A(g):
                        build_BB(g, pab, BX, "BX")

                    def p1_B(g):
                        sl = g % 2
                        bxb, wxb = buf("BX%d" % sl), buf("WX%d" % sl)
                        tb_, tbb_ = pb[6 + sl], buf("pb%d" % (6 + sl))
                        tv = tb_[:].bitcast(BF16)
                        for kt in range(4):
                            for part in range(2):
                                col = (kt * 2 + part) * 128
                                k.op("pe", lambda e: e.transpose(tv[:, col:col + 128], BX[sl][part][:, kt * 128:(kt + 1) * 128], ident_b[:]), reads=[bxb, buf("identb")], writes=[tbb_])
                        k.op("act", act(WX[sl].rearrange("p k t m -> p (k t m)"), tv, AF.Copy), reads=[tbb_], writes=[wxb])
                        ub_ = build_U(g, sl)
                        xbank, xbb = pb[2 + sl], buf("pb%d" % (2 + sl))
                        for part in range(2):
                            for kt in range(4):
                                k.op("pe", lambda e: e.matmul(xbank[:, part * 128:(part + 1) * 128], WX[sl][:, kt, part, :], Ug[sl][:, kt, :], start=(kt == 0), stop=(kt == 3)),
                                     reads=[wxb, ub_], writes=[xbb])
                        k.op("act", act(XS[:, :, :, g].rearrange("p c t -> p t c"), xbank[:, 0:256].rearrange("p (t c) -> p t c", t=2), AF.Copy),
                             reads=[xbb], writes=[xsb])

                    for step in range(G + 1):
                        if 0 <= step - 1 < G:
                            p1_B(step - 1)
                        if step < G:
                            if step % NB4 == 0:
                                power_batch(step // NB4, pab, True)
                            p1_A(step)
                    if SSM_STOP <= 6:
                        raise StopSSM()
                    LrB = sctmp[:, 2, :].rearrange("p (t g) -> p t g", t=2)
                    LiS = sctmp[:, 3, :].rearrange("p (t g) -> p t g", t=2)
                    scb = buf("sctmp")
                    k.op("dve", lambda e: e.tensor_copy(out=LrB, in_=L32r.unsqueeze(1).broadcast_to([128, 2, G])), reads=[plb], writes=[scb])
                    k.op("dve", lambda e: e.tensor_scalar(out=LiS[:, 0, :], in0=L32i, scalar1=-1.0, scalar2=None, op0=ALU.mult), reads=[plb], writes=[scb])
                    k.op("dve", lambda e: e.tensor_copy(out=LiS[:, 1, :], in_=L32i), reads=[plb], writes=[scb])
                    for s_ in range(1, NCH):
                        for d_ in range(2):
                            en_ = "dve" if d_ == 0 else "pool"
                            lo, hi = 64 * d_, 64 * d_ + 64
                            c = s_ if d_ == 0 else NCH - 1 - s_
                            cp = c - 1 if d_ == 0 else c + 1
                            prev = XS[lo:hi, cp, :, :]
                            cur = XS[lo:hi, c, :, :]
                            t1_ = sctmp[lo:hi, 0, :].rearrange("p (t g) -> p t g", t=2)
                            t2_ = sctmp[lo:hi, 1, :].rearrange("p (t g) -> p t g", t=2)
                            tb1, tb2 = buf("sct1_%d" % d_), buf("sct2_%d" % d_)
                            xh = buf("XS%d" % d_)
                            k.op(en_, lambda e: e.tensor_tensor(out=t1_, in0=prev, in1=LrB[lo:hi], op=ALU.mult), reads=[xh, xsb, scb], writes=[tb1])
                            k.op(en_, lambda e: e.tensor_tensor(out=t2_[:, 0, :], in0=prev[:, 1, :], in1=LiS[lo:hi, 0, :], op=ALU.mult), reads=[xh, xsb, scb], writes=[tb2])
                            k.op(en_, lambda e: e.tensor_tensor(out=t2_[:, 1, :], in0=prev[:, 0, :], in1=LiS[lo:hi, 1, :], op=ALU.mult), reads=[xh, xsb, scb], writes=[tb2])
                            k.op(en_, lambda e: e.tensor_tensor(out=t1_, in0=t1_, in1=t2_, op=ALU.add), reads=[tb1, tb2], writes=[tb1])
                            k.op(en_, lambda e: e.tensor_tensor(out=cur, in0=cur, in1=t1_, op=ALU.add), reads=[tb1, xh, xsb], writes=[xh])
                    xs_done = [buf("XS0"), buf("XS1"), xsb]
                    if SSM_STOP <= 7:
                        raise StopSSM()
                    zb = buf("Zt")
                    for s3_ in range(3):
                        k.op("pool", lambda e: e.memset(WYF[s3_][64:128, :, :], 0.0), writes=[buf("WYFB%d" % s3_)])
                        k.op("pool", lambda e: e.memset(WYB[s3_][0:64, :, :], 0.0), writes=[buf("WYFB%d" % s3_)])

                    def p2_A(g):
                        sl, bi, gi = g % 2, g // NB4, g % NB4
                        pob = buf("PO%d" % (bi % 2))
                        build_BB(g, pab)
                        ddb = buf("DD%d" % sl)
                        k.op("dve", lambda e: e.tensor_scalar(out=DDt[sl], in0=ident_f[:], scalar1=Dv[:, g:g + 1], scalar2=None, op0=ALU.mult), reads=[pab, buf("identf")], writes=[ddb])
                        s3 = g % 3
                        wyfb, p34 = buf("WYFB%d" % s3), buf("P34")
                        LPr, LPi, LPin = POL[bi % 2][:, 0], POL[bi % 2][:, 1], POL[bi % 2][:, 2]
                        def lp(t):
                            return t[:, gi, :].unsqueeze(2).broadcast_to([128, 33, 16])
                        def cb_(t):
                            return t[:, g, :].unsqueeze(1).broadcast_to([128, 33, 16])
                        p3v = P3.rearrange("p (t h) -> p t h", h=16)
                        p4v = P4.rearrange("p (t h) -> p t h", h=16)
                        Pl = lambda fn, w_: k.op("pool", fn, reads=[pob, pab, p34], writes=w_)
                        Pl(lambda e: e.tensor_tensor(out=p3v, in0=cb_(Cre), in1=lp(LPr), op=ALU.mult), [p34])
                        Pl(lambda e: e.tensor_tensor(out=p4v, in0=cb_(Cim), in1=lp(LPi), op=ALU.mult), [p34])
                        Pl(lambda e: e.tensor_tensor(out=WYF[s3][0:64, 0, :], in0=P3[0:64], in1=P4[0:64], op=ALU.subtract), [wyfb])
                        Pl(lambda e: e.tensor_tensor(out=WYB[s3][64:128, 0, :], in0=P3[64:128], in1=P4[64:128], op=ALU.subtract), [wyfb])
                        Pl(lambda e: e.tensor_tensor(out=p3v, in0=cb_(Cre), in1=lp(LPin), op=ALU.mult), [p34])
                        Pl(lambda e: e.tensor_tensor(out=p4v, in0=cb_(Cim), in1=lp(LPr), op=ALU.mult), [p34])
                        Pl(lambda e: e.tensor_tensor(out=WYF[s3][0:64, 1, :], in0=P3[0:64], in1=P4[0:64], op=ALU.subtract), [wyfb])
                        Pl(lambda e: e.tensor_tensor(out=WYB[s3][64:128, 1, :], in0=P3[64:128], in1=P4[64:128], op=ALU.subtract), [wyfb])

                    def p2_B(g):
                        sl = g % 2
                        s3 = g % 3
                        bbb, wyfb = buf("BB%d" % sl), buf("WYFB%d" % s3)
                        wtb = [buf("WT%d_%d" % (sl, d_)) for d_ in range(2)]
                        for d_ in range(2):
                            toff = 0 if d_ == 0 else 16
                            wyp = WYF[s3] if d_ == 0 else WYB[s3]
                            for kt in range(4):
                                jlo, jhi = (kt * 128, 512) if d_ == 0 else (0, (kt + 1) * 128)
                                bi_ = 4 + 2 * d_ + kt % 2
                                wbank, wbb = pb[bi_], buf("pb%d" % bi_)
                                k.op("pe", lambda e: e.matmul(wbank[:, jlo:jhi], BB[sl][0][:, kt * 128:(kt + 1) * 128], wyp[:, 0, toff + jlo:toff + jhi], start=True, stop=False),
                                     reads=[bbb, wyfb], writes=[wbb])
                                k.op("pe", lambda e: e.matmul(wbank[:, jlo:jhi], BB[sl][1][:, kt * 128:(kt + 1) * 128], wyp[:, 1, toff + jlo:toff + jhi], start=False, stop=True),
                                     reads=[bbb, wyfb], writes=[wbb])
                                dlo = kt * 128
                                olo, ohi = (dlo + 128, 512) if d_ == 0 else (0, dlo)
                                if ohi > olo:
                                    k.op("act", act(WT[sl][d_][:, kt, olo:ohi], wbank[:, olo:ohi], AF.Copy), reads=[wbb], writes=[wtb[d_]])
                                msk = MASKF if d_ == 0 else MASKB
                                k.op("dve", lambda e: e.tensor_tensor(out=WT[sl][d_][:, kt, dlo:dlo + 128], in0=wbank[:, dlo:dlo + 128], in1=msk, op=ALU.mult), reads=[wbb, buf("cst")], writes=[wtb[d_]])
                        sgb = buf("SG%d" % sl)
                        k.op("act", act(SG[sl], XS[:, :, :, g], AF.Copy), reads=xs_done, writes=[sgb])
                        build_U(g, sl)

                    def p2_C(g):
                        sl = g % 2
                        kc, gl = g // 8, g % 8
                        s3 = g % 3
                        wyfb, sgb, ddb, ub_ = buf("WYFB%d" % s3), buf("SG%d" % sl), buf("DD%d" % sl), buf("U%d" % sl)
                        wtb = [buf("WT%d_%d" % (sl, d_)) for d_ in range(2)]
                        ybank, ybb = pb[2 + sl], buf("pb%d" % (2 + sl))
                        for mt in range(4):
                            mlo = mt * 128
                            first = True
                            for kt in range(0, mt + 1):
                                k.op("pe", lambda e: e.matmul(ybank[:, mlo:mlo + 128], WT[sl][0][:, kt, mlo:mlo + 128], Ug[sl][:, kt, :], start=first, stop=False),
                                     reads=[wtb[0], ub_], writes=[ybb])
                                first = False
                            for kt in range(mt, 4):
                                k.op("pe", lambda e: e.matmul(ybank[:, mlo:mlo + 128], WT[sl][1][:, kt, mlo:mlo + 128], Ug[sl][:, kt, :], start=False, stop=False),
                                     reads=[wtb[1], ub_], writes=[ybb])
                            k.op("pe", lambda e: e.matmul(ybank[:, mlo:mlo + 128], DDt[sl], Ug[sl][:, mt, :], start=False, stop=False), reads=[ddb, ub_], writes=[ybb])
                            for part in range(2):
                                k.op("pe", lambda e: e.matmul(ybank[:, mlo + 1:mlo + 128], WYF[s3][:, part, 16 + mlo:16 + mlo + 128], SG[sl][:, 0:NCH - 1, part], start=False, stop=False),
                                     reads=[wyfb, sgb], writes=[ybb])
                            for part in range(2):
                                k.op("pe", lambda e: e.matmul(ybank[:, mlo:mlo + 127], WYB[s3][:, part, mlo:mlo + 128], SG[sl][:, 1:NCH, part], start=False, stop=(part == 1)),
                                     reads=[wyfb, sgb], writes=[ybb])
                        k.op("act", act(Zt[:, gl, :, :].rearrange("p m c -> p (m c)"), ybank[:], AF.Gelu_apprx_tanh), reads=[ybb], writes=[zb])
                        if gl == 7:
                            ubuf = buf("uT")
                            for j4 in range(8):
                                ibank, ibb = pb[j4 % 2], buf("pb%d" % (j4 % 2))
                                for jj in range(4):
                                    j = j4 * 4 + jj
                                    for gl2 in range(8):
                                        k.op("pe", lambda e: e.matmul(ibank[:, jj * 128:(jj + 1) * 128], SelM[:, j % 8, gl2, :], Zt[:, gl2, j // 8, :], start=(gl2 == 0), stop=(gl2 == 7)),
                                             reads=[zb, buf("SelM")], writes=[ibb])
                                dst = uT[:, kc, :].rearrange("p (c j) -> p j c", j=T1)[:, j4 * 4:(j4 + 1) * 4, :]
                                if j4 % 2 == 0:
                                    k.op("act", act(dst, ibank[:].rearrange("p (j c) -> p j c", j=4), AF.Copy), reads=[ibb], writes=[ubuf])
                                else:
                                    k.op("dve", lambda e: e.tensor_copy(out=dst, in_=ibank[:].rearrange("p (j c) -> p j c", j=4)), reads=[ibb], writes=[ubuf])

                    for step in range(G + 2):
                        if 0 <= step - 1 < G:
                            p2_B(step - 1)
                        if 0 <= step - 2 < G:
                            p2_C(step - 2)
                        if step < G:
                            if step % NB4 == 0:
                                power_batch(step // NB4, pab, False)
                            p2_A(step)
                    if SSM_STOP <= 8:
                        raise StopSSM()
                    k.barrier()
                    mb = buf("mixedT")
                    zTb = buf("uT")
                    for nt in range(4):
                        s_ = wctr[0] % 6
                        wctr[0] += 1
                        wb = buf("wgu%d" % s_)
                        wt = wgu[s_]
                        k.dma("pool", st_w[s_], wt[:, 0:4, :], w["ssm_w_glu"][l][:, nt * 128:(nt + 1) * 128].rearrange("(kt p) f -> p kt f", p=128), writes=[wb])
                        for blk in range(8):
                            bank, bb = pb[blk % 2], buf("pb%d" % (blk % 2))
                            for kt in range(4):
                                k.op("pe", lambda e: e.matmul(bank[:], wt[:, kt, :], uT[:, kt, blk * 512:(blk + 1) * 512], start=(kt == 0), stop=(kt == 3)),
                                     reads=[wb, zTb], writes=[bb])
                            sgs, sgbf = sg[blk % 2], buf("sg%d" % (blk % 2))
                            k.op("act", lambda e: e.activation(out=sgs, in_=bank[:], func=AF.Sigmoid, bias=bglu[:, l, nt:nt + 1], scale=1.0), reads=[bb, buf("bglu")], writes=[sgbf])
                            k.op("dve", lambda e: e.tensor_tensor(out=mixedT[:, nt, blk * 512:(blk + 1) * 512], in0=uT[:, nt, blk * 512:(blk + 1) * 512], in1=sgs, op=ALU.mult),
                                 reads=[sgbf, zTb], writes=[mb])
                    for blk in range(8):
                        pbn, pbb = pb[6], buf("pb6")
                        for kt in range(4):
                            q2 = kt % 2
                            sqb = buf("sq%d" % q2)
                            k.op("act", act(sq[q2][:], mixedT[:, kt, blk * 512:(blk + 1) * 512], AF.Square), reads=[mb], writes=[sqb])
                            k.op("pe", lambda e: e.matmul(pbn[:], ones_bf[:], sq[q2][:], start=(kt == 0), stop=(kt == 3)), reads=[sqb, buf("ones")], writes=[pbb])
                        s2 = blk % 2
                        rb = buf("rs%d" % s2)
                        k.op("act", act(rs[s2][:], pbn[:], AF.Sqrt, scale=1.0 / 512, bias=eps_t[:, 0:1]), reads=[pbb, buf("eps")], writes=[rb])
                        k.op("dve", lambda e: e.reciprocal(out=rs[s2][:], in_=rs[s2][:]), reads=[rb], writes=[rb])
                        for kt in range(4):
                            k.op("dve", lambda e: e.scalar_tensor_tensor(out=mixedT[:, kt, blk * 512:(blk + 1) * 512], in0=mixedT[:, kt, blk * 512:(blk + 1) * 512],
                                                                         scalar=gains2[:, l, 1, kt:kt + 1], in1=rs[s2][:], op0=ALU.mult, op1=ALU.mult),
                                 reads=[mb, rb, buf("gains2")], writes=[mb])
                    out_proj(l, 1, mb)
                    k.barrier()

                if do_ssm:
                    ssm_consts()
                    k.barrier()
                for l in range(depth):
                    if do_ffn:
                        ffn(l, "ffn1", gidx[("mix_norm", l)] if do_mixer else None)
                    if do_mixer:
                        mixer(l)
                        if do_ssm:
                            try:
                                ssm(l)
                            except StopSSM:
                                k.barrier()
                    if do_ffn:
                        ffn(l, "ffn2", gidx[("ffn1_norm", l + 1)] if l + 1 < depth else None)

                gfin = gidx["final"]
                for blk in range(8):
                    s = blk % 2
                    xb = buf("xn%d" % s)
                    k.dma("sp", st_x[s], xn[s][:], xres[:, :, blk * 512:(blk + 1) * 512].rearrange("kt p n -> p kt n"),
                          reads=[XB[kt][blk] for kt in range(8)], writes=[xb])
                    pbn, pbb = pb[6], buf("pb6")
                    for kt in range(8):
                        q2 = kt % 2
                        sqb = buf("sq%d" % q2)
                        k.op("act", act(sq[q2][:], xn[s][:, kt, :], AF.Square), reads=[xb], writes=[sqb])
                        k.op("pe", lambda e, kt=kt, q2=q2: e.matmul(pbn[:], ones_bf[:], sq[q2][:], start=(kt == 0), stop=(kt == 7)),
                             reads=[sqb, buf("ones")], writes=[pbb])
                    rb = buf("rs%d" % s)
                    k.op("act", act(rs[s][:], pbn[:], AF.Sqrt, scale=1.0 / D, bias=eps_t[:, 0:1]), reads=[pbb, buf("eps")], writes=[rb])
                    k.op("dve", lambda e: e.reciprocal(out=rs[s][:], in_=rs[s][:]), reads=[rb], writes=[rb])
                    for kt in range(8):
                        k.op("dve", lambda e, kt=kt: e.scalar_tensor_tensor(out=xn[s][:, kt, :], in0=xn[s][:, kt, :],
                                                                           scalar=gains[:, gfin, kt:kt + 1], in1=rs[s][:],
                                                                           op0=ALU.mult, op1=ALU.mult),
                             reads=[xb, rb, buf("gains")], writes=[xb])
                    for t4 in range(4):
                        tt = blk * 4 + t4
                        xt, xtb = xtok[tt % 4], buf("xtok%d" % (tt % 4))
                        for kt in range(8):
                            bank, bb = pb[kt], buf("pb%d" % kt)
                            k.op("pe", lambda e, bank=bank, kt=kt, t4=t4: e.transpose(bank[:, 0:128], xn[s][:, kt, t4 * 128:(t4 + 1) * 128], ident_f[:]),
                                 reads=[xb, buf("identf")], writes=[bb])
                            if kt % 2 == 0:
                                k.op("act", act(xt[:, kt * 128:(kt + 1) * 128], bank[:, 0:128], AF.Copy), reads=[bb], writes=[xtb])
                            else:
                                k.op("dve", lambda e, bank=bank, xt=xt, kt=kt: e.tensor_copy(out=xt[:, kt * 128:(kt + 1) * 128], in_=bank[:, 0:128]),
                                     reads=[bb], writes=[xtb])
                        k.dma("pool", st_out[tt % 4], y_out[tt * 128:(tt + 1) * 128, :], xt[:], reads=[xtb], writes=[buf("yout")])
                for s_ in st_out:
                    k.wait_stream("pool", s_)
                if k.active == "pe":
                    print("instructions:", k.ninstr, k.cnt)

        with nc.Block() as block:
            @block.tensor
            def _(en):
                gen(K(nc, sems, dsems, "pe", en))

            @block.scalar
            def _(en):
                gen(K(nc, sems, dsems, "act", en))

            @block.vector
            def _(en):
                gen(K(nc, sems, dsems, "dve", en))

            @block.gpsimd
            def _(en):
                gen(K(nc, sems, dsems, "pool", en))

            @block.sync
            def _(en):
                gen(K(nc, sems, dsems, "sp", en))
    return nc


_BUCKET = None


def _bias_host(rel_bias_table):
    rel = (np.arange(384)[None, :] - 128) - np.arange(128)[:, None]
    bucket = t5_bucket_np(rel)
    b = np.asarray(rel_bias_table)[bucket]
    b = np.transpose(b, (0, 2, 1)).copy()
    band = np.abs(rel) <= 128
    b = np.where(band[:, None, :], b, np.float32(NEG)).astype(np.float32)
    return np.ascontiguousarray(b)


def kernel(**inputs):
    x = np.asarray(inputs["x"], dtype=np.float32)
    nb = x.shape[0]
    nc = build()
    shared = {nm: np.ascontiguousarray(np.asarray(v, dtype=np.float32)) for nm, v in inputs.items()
              if nm not in ("x", "rel_bias_table")}
    shared["biasm"] = _bias_host(inputs["rel_bias_table"])
    in_maps = []
    for b in range(nb):
        m = dict(shared)
        m["x"] = np.ascontiguousarray(x[b])
        in_maps.append(m)
    res = run_bass_kernel_spmd(nc, in_maps, core_ids=list(range(nb)))
    return np.stack([r["y"] for r in res.results], axis=0).astype(np.float32)
```
